# Optimizing a Trainium2 kernel written in Bass

```python
import math
import jax
import jax.numpy as jnp
from jax import lax
import numpy as np

D_MODEL = 1024
BATCH = 1
SEQ = 16384
DEPTH = 4
DEC_BATCH = 32
DEC_SEQ = 64
PAST_LEN = 4096

CHUNK = 64
EPS = 1e-6
N_BRANCH = 3
D_FF = 4 * D_MODEL

A_HEADS = 4
A_DK = 128
A_DV = 128
A_KW = A_HEADS * A_DK
A_VW = A_HEADS * A_DV
A_BLOCK = 32

B_WIDTH = 512
B_BLOCKS = 8
B_BW = B_WIDTH // B_BLOCKS
B_CONV = 4
RG_C = 8.0

C_HEADS = 4
C_DK = 128
C_DV = 128
C_KW = C_HEADS * C_DK
C_VW = C_HEADS * C_DV
C_QKV = 2 * C_KW + C_VW
C_CONV = 4
C_BLOCK = CHUNK

IN_SIZES = (A_KW, A_KW, A_VW, A_VW, B_WIDTH, B_WIDTH, C_QKV, C_VW, C_HEADS, C_HEADS, N_BRANCH * D_MODEL)
D_IN = 2 * A_KW + 2 * A_VW + 2 * B_WIDTH + C_QKV + C_VW + 2 * C_HEADS + N_BRANCH * D_MODEL

kernel_name = 'hybrid_stream_hgrn2_rglru_gdn_step'


def split_points():
    pts, acc = [], 0
    for s in IN_SIZES[:-1]:
        acc += s
        pts.append(acc)
    return pts


def rms_norm(x, w):
    xf = x.astype(jnp.float32)
    y = xf * lax.rsqrt(jnp.mean(xf * xf, axis=-1, keepdims=True) + EPS)
    return (y * w.astype(jnp.float32)).astype(x.dtype)


def l2_norm(x):
    return x * lax.rsqrt(jnp.sum(x * x, axis=-1, keepdims=True) + EPS)


def causal_dwconv(x, prev, w):
    width = w.shape[0]
    t = x.shape[1]
    xp = jnp.concatenate([prev.astype(x.dtype), x], axis=1)
    y = xp[:, 0:t] * w[0]
    for j in range(1, width):
        y = y + xp[:, j:j + t] * w[j]
    return y, xp[:, t:]


def to_blocks(a, block):
    b, t = a.shape[:2]
    n = -(-t // block)
    a = jnp.pad(a, [(0, 0), (0, n * block - t)] + [(0, 0)] * (a.ndim - 2))
    a = a.reshape((b, n, block) + a.shape[2:])
    perm = (1, 0, 3, 2) + tuple(range(4, a.ndim))
    return a.transpose(perm)


def from_blocks(o, t):
    n, b, h, l, d = o.shape
    return o.transpose(1, 0, 3, 2, 4).reshape(b, n * l, h, d)[:, :t]


def hgrn2_core(q, k, v, logf, s0):
    t = q.shape[1]
    qb, kb, vb, gb = (to_blocks(a, A_BLOCK) for a in (q, k, v, logf))
    incl = jnp.tril(jnp.ones((A_BLOCK, A_BLOCK), dtype=bool))

    def step(s, inp):
        q_i, k_i, v_i, g_i = inp
        b = jnp.cumsum(g_i, axis=2)
        diff = b[:, :, :, None, :] - b[:, :, None, :, :]
        decay = jnp.exp(jnp.where(incl[:, :, None], diff, -jnp.inf))
        attn = jnp.einsum('bhtk,bhsk,bhtsk->bhts', q_i, k_i, decay)
        o = jnp.einsum('bhts,bhsv->bhtv', attn, v_i) + jnp.einsum('bhtk,bhkv->bhtv', q_i * jnp.exp(b), s)
        b_last = b[:, :, -1:, :]
        s = jnp.exp(b_last)[:, :, 0, :, None] * s + jnp.einsum('bhsk,bhsv->bhkv', k_i * jnp.exp(b_last - b), v_i)
        return s, o

    s, o = lax.scan(step, s0.astype(jnp.float32), (qb, kb, vb, gb))
    return from_blocks(o, t), s


def gdn_core(q, k, v, g, beta, s0):
    t = q.shape[1]
    qb, kb, vb = (to_blocks(a, C_BLOCK) for a in (q, k, v))
    gb, bb = to_blocks(g, C_BLOCK), to_blocks(beta, C_BLOCK)
    gc = jnp.cumsum(gb, axis=-1)
    incl = jnp.tril(jnp.ones((C_BLOCK, C_BLOCK), dtype=bool))
    strict = jnp.tril(jnp.ones((C_BLOCK, C_BLOCK), dtype=bool), -1)
    decay = jnp.exp(jnp.where(incl, gc[..., :, None] - gc[..., None, :], -jnp.inf))
    k_beta = kb * bb[..., None]
    a_mat = jnp.where(strict, jnp.einsum('nbhtk,nbhsk->nbhts', k_beta, kb) * decay, 0.0)
    eye = jnp.eye(C_BLOCK, dtype=jnp.float32)
    t_mat = lax.linalg.triangular_solve(eye + a_mat, jnp.broadcast_to(eye, a_mat.shape),
                                        left_side=True, lower=True, unit_diagonal=True)
    u = jnp.einsum('nbhts,nbhsv->nbhtv', t_mat, vb * bb[..., None])
    w = jnp.einsum('nbhts,nbhsk->nbhtk', t_mat, k_beta * jnp.exp(gc)[..., None])
    qk = jnp.where(incl, jnp.einsum('nbhtk,nbhsk->nbhts', qb, kb) * decay, 0.0)

    def step(s, inp):
        q_i, k_i, u_i, w_i, g_i, qk_i = inp
        v_new = u_i - jnp.einsum('bhtk,bhkv->bhtv', w_i, s)
        o = (jnp.einsum('bhtk,bhkv->bhtv', q_i * jnp.exp(g_i)[..., None], s)
             + jnp.einsum('bhts,bhsv->bhtv', qk_i, v_new))
        g_last = g_i[..., -1:]
        s = (s * jnp.exp(g_last)[..., None]
             + jnp.einsum('bhtk,bhtv->bhkv', k_i * jnp.exp(g_last - g_i)[..., None], v_new))
        return s, o

    s, o = lax.scan(step, s0.astype(jnp.float32), (qb, kb, u, w, gc, qk))
    return from_blocks(o, t), s


def rg_lru(x, h0, w_a, b_a, w_x, b_x, lam, first):
    f32 = jnp.float32
    b, t, _ = x.shape
    xf = x.astype(f32)
    xb = xf.reshape(b, t, B_BLOCKS, B_BW)
    r = jax.nn.sigmoid(jnp.einsum('btnc,ncd->btnd', xb, w_a.astype(f32)).reshape(b, t, B_WIDTH) + b_a.astype(f32))
    i = jax.nn.sigmoid(jnp.einsum('btnc,ncd->btnd', xb, w_x.astype(f32)).reshape(b, t, B_WIDTH) + b_x.astype(f32))
    log_a = -RG_C * r * jax.nn.softplus(-lam.astype(f32))
    a = jnp.exp(log_a)
    mult = jnp.sqrt(-jnp.expm1(2.0 * log_a))
    if first:
        mult = mult.at[:, 0].set(1.0)
    u = mult * i * xf
    u = u.at[:, 0].add(a[:, 0] * h0.astype(f32))

    def combine(e1, e2):
        return e1[0] * e2[0], e2[0] * e1[1] + e2[1]

    _, h = lax.associative_scan(combine, (a, u), axis=1)
    return h, h[:, -1]


def trunk_layer(x, states, p, lb, first):
    f32 = jnp.float32
    s_a, h_b, conv_b, s_c, conv_c = states
    bsz, t, _ = x.shape
    h = rms_norm(x, p['pre_mix'])
    z = h @ p['w_in']
    (za_q, za_f, za_i, za_g, zb_x, zb_g, zc_qkv, zc_g, zc_beta, zc_alpha, z_merge) = jnp.split(z, split_points(), axis=-1)

    qa = jax.nn.silu(za_q.astype(f32)).reshape(bsz, t, A_HEADS, A_DK)
    zf = za_f.astype(f32)
    lbf = lb.astype(f32)
    logf = jnp.logaddexp(jnp.log(lbf), jnp.log1p(-lbf) + jax.nn.log_sigmoid(zf))
    ka = (1.0 - lbf) * jax.nn.sigmoid(-zf)
    va = za_i.astype(f32).reshape(bsz, t, A_HEADS, A_DV)
    oa, s_a_new = hgrn2_core(qa, ka.reshape(bsz, t, A_HEADS, A_DK), va, logf.reshape(bsz, t, A_HEADS, A_DK), s_a)
    oa = rms_norm(oa, p['a_norm']) * jax.nn.silu(za_g.astype(f32)).reshape(bsz, t, A_HEADS, A_DV)
    oa = oa.reshape(bsz, t, A_VW).astype(x.dtype)

    xb_c, conv_b_new = causal_dwconv(zb_x, conv_b, p['b_conv_w'])
    xb_c = xb_c + p['b_conv_b']
    hb, h_b_new = rg_lru(xb_c, h_b, p['b_gate_a_w'], p['b_gate_a_b'], p['b_gate_x_w'], p['b_gate_x_b'], p['b_lambda'], first)
    ob = (jax.nn.gelu(zb_g.astype(f32)) * hb).astype(x.dtype)

    qkv, conv_c_new = causal_dwconv(zc_qkv, conv_c, p['c_conv_w'])
    qkv = jax.nn.silu(qkv.astype(f32))
    qc, kc, vc = jnp.split(qkv, [C_KW, 2 * C_KW], axis=-1)
    qc = l2_norm(qc.reshape(bsz, t, C_HEADS, C_DK)) * (C_DK ** -0.5)
    kc = l2_norm(kc.reshape(bsz, t, C_HEADS, C_DK))
    vc = vc.reshape(bsz, t, C_HEADS, C_DV)
    beta = jax.nn.sigmoid(zc_beta.astype(f32))
    gdec = -jnp.exp(p['c_a_log'].astype(f32)) * jax.nn.softplus(zc_alpha.astype(f32) + p['c_dt_bias'].astype(f32))
    oc, s_c_new = gdn_core(qc, kc, vc, gdec, beta, s_c)
    oc = rms_norm(oc, p['c_norm']) * jax.nn.silu(zc_g.astype(f32)).reshape(bsz, t, C_HEADS, C_DV)
    oc = oc.reshape(bsz, t, C_VW).astype(x.dtype)

    gates = jax.nn.sigmoid(z_merge.astype(f32)).reshape(bsz, t, N_BRANCH, D_MODEL).astype(x.dtype)
    merged = (gates[:, :, 0] * (oa @ p['w_br_a']) + gates[:, :, 1] * (ob @ p['w_br_b'])
              + gates[:, :, 2] * (oc @ p['w_br_c']))
    x = x + rms_norm(merged @ p['w_out'], p['post_mix'])

    h2 = rms_norm(x, p['pre_mlp'])
    m = jnp.square(jax.nn.relu(h2 @ p['w_up'])) @ p['w_down']
    x = x + rms_norm(m, p['post_mlp'])
    new = (s_a_new, h_b_new, conv_b_new, s_c_new, conv_c_new)
    return x, tuple(n.astype(o.dtype) for n, o in zip(new, states))


def setup_inputs(seed: int = 0) -> dict:
    key = jax.random.key(seed)
    ks = jax.random.split(key, 32)
    f32 = jnp.float32

    def nrm(k, shape, scale):
        return scale * jax.random.normal(k, shape, f32)

    def gain(k, shape):
        return 1.0 + 0.02 * jax.random.normal(k, shape, f32)

    u = jax.random.uniform(ks[20], (DEPTH, B_WIDTH), f32, 0.9, 0.999)
    a_base = u ** (1.0 / RG_C)
    b_lambda = jnp.log(a_base) - jnp.log1p(-a_base)
    c_a_log = jnp.log(jax.random.uniform(ks[22], (DEPTH, C_HEADS), f32, 1.0, 16.0))
    dt = jnp.exp(jax.random.uniform(ks[23], (DEPTH, C_HEADS), f32, math.log(1e-3), math.log(1e-1)))
    c_dt_bias = dt + jnp.log(-jnp.expm1(-dt))
    return {
        'x_prompt': nrm(ks[0], (BATCH, SEQ, D_MODEL), 1.0),
        'x_sample': nrm(ks[1], (DEC_BATCH, DEC_SEQ, D_MODEL), 1.0),
        'state_hgrn': nrm(ks[2], (DEPTH, DEC_BATCH, A_HEADS, A_DK, A_DV), 0.5),
        'state_rglru': nrm(ks[3], (DEPTH, DEC_BATCH, B_WIDTH), 0.5),
        'state_rglru_conv': nrm(ks[4], (DEPTH, DEC_BATCH, B_CONV - 1, B_WIDTH), 1.0),
        'state_gdn': nrm(ks[5], (DEPTH, DEC_BATCH, C_HEADS, C_DK, C_DV), 0.1),
        'state_gdn_conv': nrm(ks[6], (DEPTH, DEC_BATCH, C_CONV - 1, C_QKV), 1.0),
        'lb_raw': nrm(ks[7], (DEPTH, A_KW), 0.1),
        'norm_pre_mix': gain(ks[8], (DEPTH, D_MODEL)),
        'norm_post_mix': gain(ks[9], (DEPTH, D_MODEL)),
        'norm_pre_mlp': gain(ks[10], (DEPTH, D_MODEL)),
        'norm_post_mlp': gain(ks[11], (DEPTH, D_MODEL)),
        'w_in': nrm(ks[12], (DEPTH, D_MODEL, D_IN), D_MODEL ** -0.5),
        'a_norm': gain(ks[13], (DEPTH, A_DV)),
        'b_conv_w': nrm(ks[14], (DEPTH, B_CONV, B_WIDTH), 0.5),
        'b_conv_b': nrm(ks[15], (DEPTH, B_WIDTH), 0.01),
        'b_gate_a_w': nrm(ks[16], (DEPTH, B_BLOCKS, B_BW, B_BW), B_BW ** -0.5),
        'b_gate_a_b': nrm(ks[17], (DEPTH, B_WIDTH), 0.01),
        'b_gate_x_w': nrm(ks[18], (DEPTH, B_BLOCKS, B_BW, B_BW), B_BW ** -0.5),
        'b_gate_x_b': nrm(ks[19], (DEPTH, B_WIDTH), 0.01),
        'b_lambda': b_lambda,
        'c_conv_w': nrm(ks[21], (DEPTH, C_CONV, C_QKV), 0.5),
        'c_a_log': c_a_log,
        'c_dt_bias': c_dt_bias,
        'c_norm': gain(ks[24], (DEPTH, C_DV)),
        'w_br_a': nrm(ks[25], (DEPTH, A_VW, D_MODEL), A_VW ** -0.5),
        'w_br_b': nrm(ks[26], (DEPTH, B_WIDTH, D_MODEL), B_WIDTH ** -0.5),
        'w_br_c': nrm(ks[27], (DEPTH, C_VW, D_MODEL), C_VW ** -0.5),
        'w_out': nrm(ks[28], (DEPTH, D_MODEL, D_MODEL), D_MODEL ** -0.5),
        'w_up': nrm(ks[29], (DEPTH, D_MODEL, D_FF), D_MODEL ** -0.5),
        'w_down': nrm(ks[30], (DEPTH, D_FF, D_MODEL), (1.5 * D_FF) ** -0.5),
    }


def reference(x_prompt, x_sample, state_hgrn, state_rglru, state_rglru_conv, state_gdn, state_gdn_conv,
              lb_raw, norm_pre_mix, norm_post_mix, norm_pre_mlp, norm_post_mlp, w_in, a_norm,
              b_conv_w, b_conv_b, b_gate_a_w, b_gate_a_b, b_gate_x_w, b_gate_x_b, b_lambda,
              c_conv_w, c_a_log, c_dt_bias, c_norm, w_br_a, w_br_b, w_br_c, w_out, w_up, w_down):
    f32 = jnp.float32
    lb_cum = jnp.cumsum(jax.nn.softmax(lb_raw.astype(f32), axis=0), axis=0)
    lower_bounds = lb_cum - lb_cum[0:1]
    bp = x_prompt.shape[0]
    sdt = state_hgrn.dtype
    prompt_state = (jnp.zeros((bp, A_HEADS, A_DK, A_DV), sdt), jnp.zeros((bp, B_WIDTH), sdt),
                    jnp.zeros((bp, B_CONV - 1, B_WIDTH), sdt), jnp.zeros((bp, C_HEADS, C_DK, C_DV), sdt),
                    jnp.zeros((bp, C_CONV - 1, C_QKV), sdt))
    yp, ys = x_prompt, x_sample
    new_p, new_s = [], []
    for l in range(DEPTH):
        p = dict(pre_mix=norm_pre_mix[l], post_mix=norm_post_mix[l], pre_mlp=norm_pre_mlp[l],
                 post_mlp=norm_post_mlp[l], w_in=w_in[l], a_norm=a_norm[l], b_conv_w=b_conv_w[l],
                 b_conv_b=b_conv_b[l], b_gate_a_w=b_gate_a_w[l], b_gate_a_b=b_gate_a_b[l],
                 b_gate_x_w=b_gate_x_w[l], b_gate_x_b=b_gate_x_b[l], b_lambda=b_lambda[l],
                 c_conv_w=c_conv_w[l], c_a_log=c_a_log[l], c_dt_bias=c_dt_bias[l], c_norm=c_norm[l],
                 w_br_a=w_br_a[l], w_br_b=w_br_b[l], w_br_c=w_br_c[l], w_out=w_out[l],
                 w_up=w_up[l], w_down=w_down[l])
        yp, st_p = trunk_layer(yp, prompt_state, p, lower_bounds[l], True)
        new_p.append(st_p)
        st_in = (state_hgrn[l], state_rglru[l], state_rglru_conv[l], state_gdn[l], state_gdn_conv[l])
        ys, st_s = trunk_layer(ys, st_in, p, lower_bounds[l], False)
        new_s.append(st_s)
    p_hgrn, p_rglru, p_rglru_conv, p_gdn, p_gdn_conv = [jnp.stack([st[j] for st in new_p]) for j in range(5)]
    s_hgrn, s_rglru, s_rglru_conv, s_gdn, s_gdn_conv = [jnp.stack([st[j] for st in new_s]) for j in range(5)]
    return (yp, ys, p_hgrn, p_rglru, p_rglru_conv, p_gdn, p_gdn_conv,
            s_hgrn, s_rglru, s_rglru_conv, s_gdn, s_gdn_conv)
```

```python
import numpy as np
import concourse.bass as bass
import concourse.mybir as mybir
from concourse.bass_utils import run_bass_kernel_spmd

F32 = mybir.dt.float32
BF16 = mybir.dt.bfloat16
F32R = mybir.dt.float32r
USE_F32R = False
ALU = mybir.AluOpType
AF = mybir.ActivationFunctionType

D = 1024
DEPTH = 4
SEQ = 16384
NCORE = 8
NSEQ = 4
CH = 64
EPS = 1e-6
D_IN = 8200
NPT = 32

ENGS = ("tensor", "vector", "scalar", "gpsimd", "sync")

WT = []
WOFF = {}


def _build_wt():
    off = 0
    def add(name, kc, nw):
        nonlocal off
        WT.append((name, kc, nw, off))
        WOFF[name] = (kc, nw, off)
        off += kc * nw
    for n in ("aq", "af", "ai", "ag", "bx", "bg", "cq", "ck", "cv", "cg"):
        add(n, 8, 512)
    add("ba", 8, 8)
    for j in range(8):
        add("mg%d" % j, 8, 384)
        add("br%d" % j, 12, 128)
    add("out0", 8, 512)
    add("out1", 8, 512)
    for j in range(8):
        add("up%d" % j, 8, 512)
    for h in range(2):
        for g in range(8):
            add("dn%d_%d" % (h, g), 4, 512)
    return off


WCOLS = _build_wt()
WCOLS_PAD = ((WCOLS + 4095) // 4096) * 4096

CL = {}


def _build_cl():
    off = 0
    def add(name, n):
        nonlocal off
        CL[name] = off
        off += n
    add("g_pre", 8); add("g_post", 8); add("g_pmlp", 8); add("g_postmlp", 8)
    add("a_norm", 1); add("c_norm", 1)
    add("lbraw", 16)
    add("bcw", 16)
    add("bcb", 4); add("gab", 4); add("gxb", 4); add("lam", 4)
    add("ccw", 48)
    add("alog", 4); add("dtb", 4)
    return off


NCL = _build_cl()


class Buf:
    __slots__ = ("name", "lw", "rd", "dsem", "dcnt")

    def __init__(self, name=""):
        self.name = name
        self.lw = None
        self.rd = []
        self.dsem = None
        self.dcnt = 0


class Prog:
    def __init__(self, nc):
        self.nc = nc
        self.streams = {e: [] for e in ENGS}
        self.cnt = {e: 0 for e in ENGS}
        self.sem = {}
        self.waited = {}
        self.ctx = []
        for e in ENGS:
            cm = nc.semaphore("cs_" + e)
            self.sem[e] = cm.__enter__()
            self.ctx.append(cm)
        self.ndsem = 0
        self.final = []

    def _dsem(self, b):
        if b.dsem is None:
            cm = self.nc.semaphore("ds%d" % self.ndsem)
            self.ndsem += 1
            b.dsem = cm.__enter__()
            self.ctx.append(cm)
        return b.dsem

    def _deps(self, reads, writes):
        deps = {}
        def add(kv):
            if kv is None:
                return
            k, v = kv
            if deps.get(k, 0) < v:
                deps[k] = v
        for b in reads:
            add(b.lw)
        for b in writes:
            add(b.lw)
            for r in b.rd:
                add(r)
        return deps

    def _waits(self, eng, deps, skip_self):
        waits = []
        for k, v in deps.items():
            if skip_self and k == eng:
                continue
            if self.waited.get((eng, k), 0) >= v:
                continue
            self.waited[(eng, k)] = v
            semh = self.sem[k] if isinstance(k, str) else k
            waits.append((semh, v))
        return waits

    def _mark(self, done, reads, writes):
        for b in reads:
            b.rd.append(done)
            if len(b.rd) > 64:
                mx = {}
                for k, v in b.rd:
                    if mx.get(k, 0) < v:
                        mx[k] = v
                b.rd = list(mx.items())
        for b in writes:
            b.lw = done
            b.rd = []

    def op(self, eng, name, kw, reads=(), writes=()):
        deps = self._deps(reads, writes)
        waits = self._waits(eng, deps, skip_self=(eng == "tensor"))
        self.cnt[eng] += 1
        done = (eng, self.cnt[eng])
        self.streams[eng].append((waits, name, kw, self.sem[eng], 1))
        self._mark(done, reads, writes)

    def dma(self, eng, kw, slot, reads=(), writes=(), final=False):
        semh = self._dsem(slot)
        deps = self._deps(reads, writes)
        if slot.dcnt > 0 and deps.get(semh, 0) < slot.dcnt:
            deps[semh] = slot.dcnt
        waits = self._waits(eng, deps, skip_self=False)
        slot.dcnt += 16
        done = (semh, slot.dcnt)
        self.streams[eng].append((waits, "dma_start", kw, semh, 16))
        self._mark(done, reads, writes)
        if final:
            self.final.append(done)

    def emit(self):
        nc = self.nc
        fin = {}
        for k, v in self.final:
            fin[k] = max(fin.get(k, 0), v)
        streams = self.streams
        with nc.Block() as block:
            def mk(ename):
                def body(engine):
                    for waits, name, kw, semh, inc in streams[ename]:
                        for (s, v) in waits:
                            engine.wait_ge(s, v)
                        getattr(engine, name)(**kw).then_inc(semh, inc)
                    if ename == "sync":
                        for s, v in fin.items():
                            engine.wait_ge(s, v)
                return body
            block.tensor(mk("tensor"))
            block.vector(mk("vector"))
            block.scalar(mk("scalar"))
            block.gpsimd(mk("gpsimd"))
            block.sync(mk("sync"))

    def close(self):
        for cm in reversed(self.ctx):
            cm.__exit__(None, None, None)


class KB:
    def __init__(self, npt=NPT, do_sample=True):
        self.npt = npt
        self.do_sample = do_sample
        nc = bass.Bass("TRN2", target_bir_lowering=False)
        self.nc = nc
        self.P = Prog(nc)
        self.cms = []
        self.rr = 0
        ntok = npt * 512
        self.ntok = ntok
        di = lambda n, s: nc.dram_tensor(n, s, F32, kind="ExternalInput").ap()
        do = lambda n, s: nc.dram_tensor(n, s, F32, kind="ExternalOutput").ap()
        self.xp = di("xp", [128, 8, ntok])
        self.xs = di("xs", [128, 8, NSEQ * CH])
        self.wf = di("wf", [DEPTH, 128, WCOLS_PAD])
        self.cst_d = di("cst", [128, DEPTH * NCL])
        self.gw_d = di("gw", [128, DEPTH * 1024])
        self.k128_d = di("k128", [128, 128 + 128 + 512])
        self.k64_d = di("k64", [64, 896])
        self.s_hg = di("s_hg", [DEPTH, NSEQ, 4, 128, 128])
        self.s_rg = di("s_rg", [DEPTH, NSEQ, 128, 4])
        self.s_rgc = di("s_rgc", [DEPTH, NSEQ, 128, 4, 3])
        self.s_gd = di("s_gd", [DEPTH, NSEQ, 4, 128, 128])
        self.s_gdc = di("s_gdc", [DEPTH, NSEQ, 128, 12, 3])
        self.yp = do("yp", [128, 8, ntok])
        self.ys = do("ys", [128, 8, NSEQ * CH])
        self.o_hg = do("o_hg", [DEPTH, NSEQ + 1, 4, 128, 128])
        self.o_rg = do("o_rg", [DEPTH, NSEQ + 1, 128, 4])
        self.o_rgc = do("o_rgc", [DEPTH, NSEQ + 1, 128, 4, 3])
        self.o_gd = do("o_gd", [DEPTH, NSEQ + 1, 4, 128, 128])
        self.o_gdc = do("o_gdc", [DEPTH, NSEQ + 1, 128, 12, 3])
        self.wb = nc.dram_tensor("wb", [DEPTH, 128, WCOLS_PAD], BF16).ap()
        self.wb_tok = [[Buf("wb%d_%d" % (l, i)) for i in range(WCOLS_PAD // 4096)] for l in range(DEPTH)]
        self.outslot = Buf("outslot")

    def sb(self, name, shape, dt):
        cm = self.nc.sbuf_tensor("sb_" + name, shape, dt)
        t = cm.__enter__()
        self.cms.append(cm)
        return t

    def ps(self, name, shape, dt):
        cm = self.nc.psum_tensor("ps_" + name, shape, dt)
        t = cm.__enter__()
        self.cms.append(cm)
        return t

    def T(self, name, r=(), w=(), **kw):
        self.P.op("tensor", name, kw, r, w)

    def V(self, name, r=(), w=(), **kw):
        self.P.op("vector", name, kw, r, w)

    def A(self, name, r=(), w=(), **kw):
        self.P.op("scalar", name, kw, r, w)

    def G(self, name, r=(), w=(), **kw):
        self.P.op("gpsimd", name, kw, r, w)

    def E(self, name, r=(), w=(), **kw):
        self.rr += 1
        self.P.op("gpsimd" if (self.rr % 3 == 0) else "vector", name, kw, r, w)

    def DMA(self, q, slot, r=(), w=(), final=False, **kw):
        self.P.dma(q, kw, slot, r, w, final)

    def pquad(self):
        i = self.pq_i
        self.pq_i = 1 - i
        return self.quads[i], self.pb_tok[i * 4:(i + 1) * 4]

    def pbank(self):
        i = self.pb_i
        self.pb_i = (i + 1) % 8
        return self.pbanks[i], self.pb_tok[i]

    def alloc(self):
        sb = self.sb
        self.quads = [self.ps("quad%d" % i, [128, 2048], F32) for i in range(2)]
        self.pbanks = [self.quads[i // 4][:, (i % 4) * 512:(i % 4 + 1) * 512] for i in range(8)]
        self.pq_i = 0
        self.pb_tok = [Buf("pb%d" % i) for i in range(8)]
        self.pb_i = 0
        self.cst = sb("cst", [128, DEPTH * NCL], F32); self.t_cst = Buf("cst")
        self.der = sb("der", [128, DEPTH * 16], F32); self.t_der = Buf("der")
        self.nega = sb("nega", [64, DEPTH * 4], F32)
        self.gwb = sb("gwb", [128, 1024], BF16); self.t_gwb = Buf("gwb")
        self.k128 = sb("k128", [128, 768], F32); self.t_k128 = Buf("k128")
        self.k64 = sb("k64", [64, 896], F32); self.t_k64 = Buf("k64")
        self.identb = sb("identb", [128, 128], BF16)
        self.onesb = sb("onesb", [128, 128], BF16)
        self.epst = sb("epst", [128, 1], F32)
        self.t_kc = Buf("kconst")
        self.xT = sb("xT", [128, 8, 512], F32); self.t_x = Buf("xT")
        self.hT = sb("hT", [128, 8, 512], BF16); self.t_hc = [Buf("hT%d" % i) for i in range(8)]
        self.t_yc = [Buf("yT%d" % i) for i in range(8)]
        self.NW = 3
        self.wsl = [sb("wsl%d" % i, [128, 4096], BF16) for i in range(self.NW)]
        self.t_wsl = [Buf("wsl%d" % i) for i in range(self.NW)]
        self.ws_i = 0
        self.GA = sb("GA", [128, 8192], F32)
        self.t_G = [Buf("G%d" % i) for i in range(6)]
        self.GB = sb("GB", [128, 4096 + 64], F32)
        self.BB = sb("BB", [128, 6 * 2048], BF16)
        self.t_B = [Buf("B%d" % i) for i in range(6)]
        self.TK = [sb("TK0", [64, 8, 512], BF16)]
        self.t_TK = [Buf("TK0")]
        self.oT = [sb("oT%d" % i, [128, 4, 512], BF16) for i in range(3)]
        self.t_o = [Buf("oT%d" % i) for i in range(3)]
        self.rstd = sb("rstd", [128, 512], F32); self.t_rstd = Buf("rstd")
        self.fence = sb("fence", [128, 2], F32)
        self.t_aT = [Buf("aT%d" % i) for i in range(32)]
        self.rs4 = sb("rs4", [128, 4, 512], F32); self.t_rs4 = Buf("rs4")
        self.tmpa = [sb("tmpa%d" % i, [128, 512], F32) for i in range(3)]
        self.t_tmpa = [Buf("tmpa%d" % i) for i in range(3)]
        self.tmpb = [sb("tmpb%d" % i, [128, 512], BF16) for i in range(3)]
        self.t_tmpb = [Buf("tmpb%d" % i) for i in range(3)]
        self.SA = [sb("SA%d" % l, [128, 4, 128], F32) for l in range(DEPTH)]
        self.SC = [sb("SC%d" % l, [128, 4, 128], F32) for l in range(DEPTH)]
        self.t_SA = [Buf("SA%d" % l) for l in range(DEPTH)]
        self.t_SC = [Buf("SC%d" % l) for l in range(DEPTH)]
        self.SAb = sb("SAb", [128, 4, 128], BF16); self.t_SAb = Buf("SAb")
        self.SCb = sb("SCb", [128, 4, 128], BF16); self.t_SCb = Buf("SCb")
        self.hB = [sb("hB%d" % l, [128, 4], F32) for l in range(DEPTH)]
        self.t_hB = [Buf("hB%d" % l) for l in range(DEPTH)]
        self.halB = [sb("halB%d" % l, [128, 4, 3], F32) for l in range(DEPTH)]
        self.t_halB = [Buf("halB%d" % l) for l in range(DEPTH)]
        self.halC = [sb("halC%d" % l, [128, 12, 3], F32) for l in range(DEPTH)]
        self.t_halC = [Buf("halC%d" % l) for l in range(DEPTH)]
        def two(name, shape, dt):
            return [sb("%s%d" % (name, i), shape, dt) for i in range(2)], [Buf("%s%d" % (name, i)) for i in range(2)]
        def one(name, shape, dt):
            t = sb(name, shape, dt); b = Buf(name)
            return [t, t], [b, b]
        self.bgA = sb("bgA", [64, 8, 8], F32); self.t_bgA = Buf("bgA")
        self.gcs = sb("gcs", [64, 6, 32], F32); self.t_gcs = Buf("gcs")
        self.egl = sb("egl", [128, 32], F32); self.t_egl = Buf("egl")
        self.vn, self.t_vn = two("vn", [64, 4, 128], BF16)
        self.ident64rep = None

    def Gv(self, i, n=512, halo=0):
        if i < 4:
            base = self.GA[:, i * 2048:(i + 1) * 2048]
            return base.rearrange("p (a b) -> p a b", a=4)
        if i == 4:
            return self.GB[:, 0:2048].rearrange("p (a b) -> p a b", a=4)
        return self.GB[:, 2048:2048 + 4 * 515].rearrange("p (a b) -> p a b", a=4)

    def Bv(self, i):
        return self.BB[:, i * 2048:(i + 1) * 2048].rearrange("p (a b) -> p a b", a=4)

    def setup(self):
        k = self
        k.DMA("sync", k.t_cst, w=[k.t_cst], out=k.cst[:], in_=k.cst_d[:, :])
        k.DMA("sync", k.t_k128, w=[k.t_k128], out=k.k128[:], in_=k.k128_d[:, :])
        k.DMA("sync", k.t_k64, w=[k.t_k64], out=k.k64[:], in_=k.k64_d[:, :])
        k.V("tensor_copy", r=[k.t_k128], w=[k.t_kc], out=k.identb[:], in_=k.k128[:, 0:128])
        k.V("tensor_copy", r=[k.t_k128], w=[k.t_kc], out=k.onesb[:], in_=k.k128[:, 128:256])
        k.V("memset", w=[k.t_kc], ap=k.epst[:], constant=EPS)
        self.ident = k.k128[:, 0:128]
        self.scanmask = k.k128[:, 256:768]
        self.L64 = k.k64[:, 0:64]
        self.ones64 = k.k64[:, 64:192]
        self.strictL = k.k64[:, 192:256].unsqueeze(1).to_broadcast([64, 4, 64])
        self.strictU = k.k64[:, 256:320].unsqueeze(1).to_broadcast([64, 4, 64])
        self.inclU = k.k64[:, 320:384].unsqueeze(1).to_broadcast([64, 4, 64])
        self.identrep = k.k64[:, 384:896].rearrange("p (a b c) -> p a b c", a=4, b=2)
        self.m_strictL = k.k64[:, 192:256]
        self.m_strictU = k.k64[:, 256:320]
        self.m_inclU = k.k64[:, 320:384]
        self.m_ident = k.k64[:, 384:448]
        for l in range(DEPTH):
            c0 = l * NCL
            d0 = l * 16
            cst = k.cst
            if l == 0:
                lr = cst[:, c0 + CL["lbraw"]:c0 + CL["lbraw"] + 16].rearrange("p (h l) -> p h l", h=4)
                ex = k.tmpa[0][:, 0:16].rearrange("p (h l) -> p h l", h=4)
                sm = k.tmpa[0][:, 16:20]
                k.A("activation", r=[k.t_cst], w=[k.t_tmpa[0]], out=ex, in_=lr, func=AF.Exp)
                k.V("tensor_reduce", r=[k.t_tmpa[0]], w=[k.t_tmpa[0]], out=sm, in_=ex, axis=mybir.AxisListType.X, op=ALU.add)
                k.V("reciprocal", r=[k.t_tmpa[0]], w=[k.t_tmpa[0]], out=sm, in_=sm)
                k.V("tensor_tensor", r=[k.t_tmpa[0]], w=[k.t_tmpa[0]], out=ex, in0=ex,
                    in1=sm.unsqueeze(2).to_broadcast([128, 4, 4]), op=ALU.mult)
                k.V("memset", w=[k.t_der], ap=k.der[:, 0:4], constant=0.0)
                for ll in range(1, DEPTH):
                    k.V("tensor_tensor", r=[k.t_tmpa[0], k.t_der], w=[k.t_der], out=k.der[:, ll * 16:ll * 16 + 4],
                        in0=k.der[:, (ll - 1) * 16:(ll - 1) * 16 + 4], in1=ex[:, :, ll], op=ALU.add)
            k.V("tensor_scalar", r=[k.t_der], w=[k.t_der], out=k.der[:, d0 + 4:d0 + 8], in0=k.der[:, d0:d0 + 4],
                scalar1=-1.0, scalar2=1.0, op0=ALU.mult, op1=ALU.add)
            lam = cst[:, c0 + CL["lam"]:c0 + CL["lam"] + 4]
            t1 = k.tmpa[1][:, 0:4]
            k.A("activation", r=[k.t_cst], w=[k.t_tmpa[1]], out=t1, in_=lam, func=AF.Exp, scale=-1.0)
            k.V("tensor_scalar", r=[k.t_tmpa[1]], w=[k.t_tmpa[1]], out=t1, in0=t1, scalar1=1.0, scalar2=None, op0=ALU.add)
            k.A("activation", r=[k.t_tmpa[1]], w=[k.t_tmpa[1]], out=t1, in_=t1, func=AF.Ln)
            k.V("tensor_scalar", r=[k.t_tmpa[1]], w=[k.t_der], out=k.der[:, d0 + 8:d0 + 12], in0=t1, scalar1=-8.0, scalar2=None, op0=ALU.mult)
            k.V("tensor_scalar", r=[k.t_tmpa[1]], w=[k.t_der], out=k.der[:, d0 + 12:d0 + 16], in0=t1, scalar1=-16.0, scalar2=None, op0=ALU.mult)
            k.A("activation", r=[k.t_cst], w=[k.t_tmpa[2]], out=k.tmpa[2][0:64, 0:4], in_=cst[0:64, c0 + CL["alog"]:c0 + CL["alog"] + 4], func=AF.Exp)
            k.V("tensor_scalar", r=[k.t_tmpa[2]], w=[k.t_der], out=k.nega[:, l * 4:l * 4 + 4], in0=k.tmpa[2][0:64, 0:4], scalar1=-1.0, scalar2=None, op0=ALU.mult)
        for l in range(DEPTH):
            k.V("memset", w=[k.t_SA[l]], ap=k.SA[l][:], constant=0.0)
            k.V("memset", w=[k.t_SC[l]], ap=k.SC[l][:], constant=0.0)
            k.V("memset", w=[k.t_hB[l]], ap=k.hB[l][:], constant=0.0)
            k.V("memset", w=[k.t_halB[l]], ap=k.halB[l][:], constant=0.0)
            k.V("memset", w=[k.t_halC[l]], ap=k.halC[l][:], constant=0.0)

    def preconvert(self):
        k = self
        nblk = WCOLS_PAD // 4096
        i = 0
        stg = [(k.GA[:, 0:4096], k.t_G[0], k.t_G[1]), (k.GA[:, 4096:8192], k.t_G[2], k.t_G[3])]
        outb = [(k.BB[:, 0:4096], k.t_B[0]), (k.BB[:, 4096:8192], k.t_B[2])]
        for l in range(DEPTH):
            for b in range(nblk):
                s_ap, s_t, _ = stg[i % 2]
                o_ap, o_t = outb[i % 2]
                k.DMA("sync", s_t, w=[s_t], out=s_ap, in_=k.wf[l, :, b * 4096:(b + 1) * 4096])
                eng = ("vector", "gpsimd", "scalar")[i % 3]
                if eng == "scalar":
                    k.A("activation", r=[s_t], w=[o_t], out=o_ap, in_=s_ap, func=AF.Copy)
                else:
                    k.P.op(eng, "tensor_copy", dict(out=o_ap, in_=s_ap), [s_t], [o_t])
                k.DMA("gpsimd", o_t, r=[o_t], w=[k.wb_tok[l][b]], out=k.wb[l, :, b * 4096:(b + 1) * 4096], in_=o_ap)
                i += 1

    def wload(self, l, name):
        kc, nw, off = WOFF[name]
        i = self.ws_i
        self.ws_i = (i + 1) % self.NW
        n = kc * nw
        toks = self.wb_tok[l][off // 4096:(off + n - 1) // 4096 + 1]
        self.DMA("sync", self.t_wsl[i], r=list(toks), w=[self.t_wsl[i]], out=self.wsl[i][:, 0:n], in_=self.wb[l, :, off:off + n])
        return self.wsl[i][:, 0:n].rearrange("p (a b) -> p a b", a=kc), self.t_wsl[i]

    def sumsq_bcast(self, src_ap_list, toks, nsq, scale, TT):
        k = self
        pb, pt = k.pbank()
        n = len(src_ap_list)
        for i, a in enumerate(src_ap_list):
            k.T("matmul", r=list(toks) + [k.t_kc], w=[pt], out=pb[:, 0:TT], lhsT=k.onesb[:], rhs=a, start=(i == 0), stop=(i == n - 1))
        return pb, pt

    def rstd_from(self, pb, pt, TT, scale, out_ap, out_tok):
        k = self
        k.A("activation", r=[pt, k.t_kc], w=[out_tok], out=out_ap, in_=pb[:, 0:TT], func=AF.Ln, scale=scale, bias=k.epst[:])
        k.A("activation", r=[out_tok], w=[out_tok], out=out_ap, in_=out_ap, func=AF.Exp, scale=-0.5)

    def prenorm(self, gname, l, TT):
        k = self
        sq = k.BB[:, 0:4096].rearrange("p (a b) -> p a b", a=8)
        tsq = [k.t_B[0], k.t_B[1]]
        k.A("activation", r=[k.t_x], w=tsq, out=sq[:, :, 0:TT], in_=k.xT[:, :, 0:TT], func=AF.Square)
        pb, pt = k.sumsq_bcast([sq[:, c, 0:TT] for c in range(8)], tsq, 8, 1.0 / D, TT)
        k.rstd_from(pb, pt, TT, 1.0 / D, k.rstd[:, 0:TT], k.t_rstd)
        g0 = l * NCL + CL[gname]
        for c in range(8):
            if c % 3 != 2:
                k.V("scalar_tensor_tensor", r=[k.t_x, k.t_rstd, k.t_cst], w=[k.t_hc[c]], out=k.hT[:, c, 0:TT], in0=k.xT[:, c, 0:TT],
                    scalar=k.cst[:, g0 + c:g0 + c + 1], in1=k.rstd[:, 0:TT], op0=ALU.mult, op1=ALU.mult)
            else:
                tp, t_tp = k.tmpa[2], k.t_tmpa[2]
                k.G("tensor_scalar", r=[k.t_x, k.t_cst], w=[t_tp], out=tp[:, 0:TT], in0=k.xT[:, c, 0:TT], scalar1=k.cst[:, g0 + c:g0 + c + 1], scalar2=None, op0=ALU.mult)
                k.G("tensor_tensor", r=[t_tp, k.t_rstd], w=[k.t_hc[c]], out=k.hT[:, c, 0:TT], in0=tp[:, 0:TT], in1=k.rstd[:, 0:TT], op=ALU.mult)

    def postnorm_add(self, gname, l, TT):
        k = self
        yT = k.GA[:, 0:4096].rearrange("p (a b) -> p a b", a=8)
        ty = [k.t_G[0], k.t_G[1]]
        sq = k.BB[:, 0:4096].rearrange("p (a b) -> p a b", a=8)
        tsq = [k.t_B[0], k.t_B[1]]
        k.A("activation", r=ty, w=tsq, out=sq[:, :, 0:TT], in_=yT[:, :, 0:TT], func=AF.Square)
        pb, pt = k.sumsq_bcast([sq[:, c, 0:TT] for c in range(8)], tsq, 8, 1.0 / D, TT)
        k.rstd_from(pb, pt, TT, 1.0 / D, k.rstd[:, 0:TT], k.t_rstd)
        g0 = l * NCL + CL[gname]
        k.V("memset", w=ty + k.t_yc, ap=k.fence[:, 0:1], constant=0.0)
        for c in range(8):
            k.V("scalar_tensor_tensor", r=[k.t_rstd, k.t_cst], w=[k.t_yc[c]], out=yT[:, c, 0:TT], in0=yT[:, c, 0:TT],
                scalar=k.cst[:, g0 + c:g0 + c + 1], in1=k.rstd[:, 0:TT], op0=ALU.mult, op1=ALU.mult)
            k.G("tensor_tensor", r=[k.t_yc[c], k.t_x], w=[k.t_x], out=k.xT[:, c, 0:TT], in0=k.xT[:, c, 0:TT], in1=yT[:, c, 0:TT], op=ALU.add)
        k.G("memset", w=ty + k.t_yc, ap=k.fence[:, 1:2], constant=0.0)

    def headnorm_gate(self, src, t_src, gate, t_gate, ncol, out, t_out, TT, DH=128):
        k = self
        sq = k.BB[:, 0:2048].rearrange("p (a b) -> p a b", a=4)
        tsq = [k.t_B[0]]
        k.A("activation", r=[t_src], w=tsq, out=sq[:, :, 0:TT], in_=src[:, :, 0:TT], func=AF.Square)
        for h in range(4):
            pb, pt = k.sumsq_bcast([sq[:, h, 0:TT]], tsq, 1, 1.0 / DH, TT)
            k.rstd_from(pb, pt, TT, 1.0 / DH, k.rs4[:, h, 0:TT], k.t_rs4)
        k.V("scalar_tensor_tensor", r=[t_src, k.t_rs4, k.t_cst], w=[t_src], out=src[:, :, 0:TT], in0=src[:, :, 0:TT],
            scalar=k.cst[:, ncol:ncol + 1], in1=k.rs4[:, :, 0:TT], op0=ALU.mult, op1=ALU.mult)
        k.E("tensor_tensor", r=[t_src, t_gate], w=[t_out], out=out[:, :, 0:TT], in0=src[:, :, 0:TT], in1=gate[:, :, 0:TT], op=ALU.mult)

    def proj_fm(self, wt, t_w, ncol, TT, evac):
        k = self
        banks = [k.pbank() for _ in range(ncol)]
        for kc in range(8):
            for j in range(ncol):
                pb, pt = banks[j]
                k.T("matmul", r=[k.t_hc[kc], t_w], w=[pt], out=pb[:, 0:TT], lhsT=wt[:, kc, j * 128:(j + 1) * 128], rhs=k.hT[:, kc, 0:TT],
                    start=(kc == 0), stop=(kc == 7))
        for j in range(ncol):
            pb, pt = banks[j]
            evac(j, pb[:, 0:TT], pt)

    def proj_tm(self, wt, t_w, n, c, evac):
        k = self
        pb, pt = k.pbank()
        for kc in range(8):
            k.T("matmul", r=[k.t_hc[kc], t_w], w=[pt], out=pb[0:64, 0:n], lhsT=k.hT[:, kc, c * CH:(c + 1) * CH], rhs=wt[:, kc, 0:n],
                start=(kc == 0), stop=(kc == 7))
        evac(pb[0:64, 0:n], pt)

    def block(self, ti, l, TT, sample):
        k = self
        nch = TT // CH
        c0 = l * NCL
        d0 = l * 16
        cst = k.cst
        G = [k.Gv(i) for i in range(6)]
        tG = k.t_G
        B = [k.Bv(i) for i in range(6)]
        tB = k.t_B
        first = (ti == 0 and not sample)

        k.prenorm("g_pre", l, TT)

        qa, t_qa = G[0], tG[0]
        lf, t_lf = G[1], tG[1]
        ka, t_ka = G[2], tG[2]
        gA, t_gA = B[2], tB[2]
        vtok, t_vtok = k.TK[0], k.t_TK[0]
        wt, tw = k.wload(l, "aq")
        k.proj_fm(wt, tw, 4, TT, lambda j, p, pt: k.A("activation", r=[pt], w=[t_qa], out=qa[:, j, 0:TT], in_=p, func=AF.Silu))
        wt, tw = k.wload(l, "af")
        def ev_f(j, p, pt):
            k.A("activation", r=[pt], w=[t_ka], out=ka[:, j, 0:TT], in_=p, func=AF.Sigmoid)
            k.V("tensor_scalar", r=[t_ka, k.t_der], w=[t_lf], out=lf[:, j, 0:TT], in0=ka[:, j, 0:TT],
                scalar1=k.der[:, d0 + 4 + j:d0 + 5 + j], scalar2=k.der[:, d0 + j:d0 + j + 1], op0=ALU.mult, op1=ALU.add)
            k.V("tensor_scalar", r=[t_lf], w=[t_ka], out=ka[:, j, 0:TT], in0=lf[:, j, 0:TT], scalar1=-1.0, scalar2=1.0, op0=ALU.mult, op1=ALU.add)
            k.A("activation", r=[t_lf], w=[t_lf], out=lf[:, j, 0:TT], in_=lf[:, j, 0:TT], func=AF.Ln)
        k.proj_fm(wt, tw, 4, TT, ev_f)
        wt, tw = k.wload(l, "ai")
        for c in range(nch):
            k.proj_tm(wt, tw, 512, c, lambda p, pt, c=c: k.V("tensor_copy", r=[pt], w=[t_vtok], out=vtok[:, c, :], in_=p))
        wt, tw = k.wload(l, "ag")
        k.proj_fm(wt, tw, 4, TT, lambda j, p, pt: k.A("activation", r=[pt], w=[t_gA], out=gA[:, j, 0:TT], in_=p, func=AF.Silu))
        bcs, t_bcs = G[3], tG[3]
        for h in range(4):
            k.V("tensor_tensor_scan", r=[t_lf, k.t_k128], w=[t_bcs], out=bcs[:, h, 0:TT], data0=k.scanmask[:, 0:TT], data1=lf[:, h, 0:TT],
                initial=0.0, op0=ALU.mult, op1=ALU.add)
        eb, t_eb = G[4], tG[4]
        qt, t_qt = B[3], tB[3]
        kt, t_kt = B[4], tB[4]
        kh, t_kh = B[5], tB[5]
        k.A("activation", r=[t_bcs], w=[t_eb], out=eb[:, :, 0:TT], in_=bcs[:, :, 0:TT], func=AF.Exp)
        k.E("tensor_tensor", r=[t_qa, t_eb], w=[t_qt], out=qt[:, :, 0:TT], in0=qa[:, :, 0:TT], in1=eb[:, :, 0:TT], op=ALU.mult)
        k.A("activation", r=[t_bcs, t_qt], w=[t_eb], out=eb[:, :, 0:TT], in_=bcs[:, :, 0:TT], func=AF.Exp, scale=-1.0)
        k.E("tensor_tensor", r=[t_ka, t_eb], w=[t_kt], out=kt[:, :, 0:TT], in0=ka[:, :, 0:TT], in1=eb[:, :, 0:TT], op=ALU.mult)
        for c in range(nch):
            for h in range(4):
                k.A("activation", r=[t_bcs, t_kt], w=[t_eb], out=eb[:, h, c * CH:(c + 1) * CH], in_=bcs[:, h, c * CH:(c + 1) * CH], func=AF.Exp,
                    scale=-1.0, bias=bcs[:, h, (c + 1) * CH - 1:(c + 1) * CH])
        k.E("tensor_tensor", r=[t_ka, t_eb], w=[t_kh], out=kh[:, :, 0:TT], in0=ka[:, :, 0:TT], in1=eb[:, :, 0:TT], op=ALU.mult)
        ebl = k.rs4[:, :, 0:nch]
        k.A("activation", r=[t_bcs], w=[k.t_rs4], out=ebl, in_=bcs[:, :, 0:TT].rearrange("p h (c t) -> p h c t", t=CH)[:, :, :, CH - 1], func=AF.Exp)
        oA, t_oA = G[0], tG[0]
        NB = nch * 256
        khtok = k.GB[0:64, 0:2048].bitcast(BF16).rearrange("p (c x) -> p c x", c=8)
        t_khtok = tG[4]
        q, tq = k.pquad()
        qv = q[:].bitcast(BF16)
        for c in range(nch):
            for h in range(4):
                k.T("transpose", r=[t_kh, k.t_kc], w=tq, out=qv[0:64, c * 512 + h * 128:c * 512 + (h + 1) * 128], in_=kh[:, h, c * CH:(c + 1) * CH], identity=k.identb[:])
        k.V("tensor_copy", r=tq + [t_eb], w=[t_khtok], out=khtok[:, 0:nch, :], in_=qv[0:64, 0:nch * 512].rearrange("p (c x) -> p c x", c=nch))
        atT = k.BB[0:64, 2048:4096].rearrange("p (c h s) -> p c h s", c=8, h=4)
        t_atT = tB[1]
        q, tq = k.pquad()
        for c in range(nch):
            for h in range(4):
                k.T("matmul", r=[t_kt, t_qt], w=tq, out=q[0:64, (c * 4 + h) * 64:(c * 4 + h + 1) * 64], lhsT=kt[:, h, c * CH:(c + 1) * CH], rhs=qt[:, h, c * CH:(c + 1) * CH], start=True, stop=True)
        k.V("tensor_tensor", r=tq + [k.t_k64], w=[t_atT], out=atT[:, 0:nch].rearrange("p c h s -> p (c h) s"),
            in0=q[0:64, 0:NB].rearrange("p (w s) -> p w s", s=64), in1=k.m_inclU.unsqueeze(1).to_broadcast([64, nch * 4, 64]), op=ALU.mult)
        genB = k.mixB(l, TT, sample, first)
        for c in range(nch):
            cs = slice(c * CH, (c + 1) * CH)
            for _ in range(3):
                next(genB, None)
            if sample:
                k.load_state_A(l, c)
            k.V("tensor_copy", r=[k.t_SA[l]], w=[k.t_SAb], out=k.SAb[:], in_=k.SA[l][:])
            pb, pt = k.pbank()
            for h in range(4):
                k.T("matmul", r=[t_vtok, t_atT], w=[pt], out=pb[:, h * 64:(h + 1) * 64], lhsT=vtok[:, c, h * 128:(h + 1) * 128], rhs=atT[:, c, h, :],
                    start=True, stop=False)
                k.T("matmul", r=[k.t_SAb, t_qt], w=[pt], out=pb[:, h * 64:(h + 1) * 64], lhsT=k.SAb[:, h, :], rhs=qt[:, h, cs], start=False, stop=True)
            k.A("activation", r=[pt], w=[t_oA], out=oA[:, :, cs], in_=pb[:, 0:256].rearrange("p (a b) -> p a b", a=4), func=AF.Copy)
            pb, pt = k.pbank()
            for h in range(4):
                k.T("matmul", r=[t_khtok, t_vtok], w=[pt], out=pb[:, h * 128:(h + 1) * 128], lhsT=khtok[:, c, h * 128:(h + 1) * 128],
                    rhs=vtok[:, c, h * 128:(h + 1) * 128], start=True, stop=True)
            k.V("tensor_tensor", r=[k.t_rs4, k.t_SA[l]], w=[k.t_SA[l]], out=k.SA[l][:], in0=k.SA[l][:],
                in1=k.rs4[:, :, c:c + 1].to_broadcast([128, 4, 128]), op=ALU.mult)
            k.V("tensor_tensor", r=[pt, k.t_SA[l]], w=[k.t_SA[l]], out=k.SA[l][:], in0=k.SA[l][:],
                in1=pb[:, 0:512].rearrange("p (a b) -> p a b", a=4), op=ALU.add)
            if sample:
                k.store_state_A(l, c)
        for _ in genB:
            pass
        k.headnorm_gate(oA, t_oA, gA, t_gA, c0 + CL["a_norm"], k.oT[0], k.t_o[0], TT)

        t_zx = tG[5]
        zxv = k.GB[:, 2048:2048 + 4 * 515].rearrange("p (a b) -> p a b", a=4)
        qT, t_qT = B[3], tB[3]
        kT, t_kT = B[4], tB[4]
        vT, t_vT = B[5], tB[5]
        gC, t_gC = B[2], tB[2]
        for gi, (wn, dst, t_dst) in enumerate((("cq", qT, t_qT), ("ck", kT, t_kT), ("cv", vT, t_vT))):
            wt, tw = k.wload(l, wn)
            hal = k.halC[l][:, gi * 4:(gi + 1) * 4, :]
            if not sample:
                k.V("tensor_copy", r=[k.t_halC[l]], w=[t_zx], out=zxv[:, :, 0:3], in_=hal)
            k.proj_fm(wt, tw, 4, TT, lambda j, p, pt: k.A("activation", r=[pt], w=[t_zx], out=zxv[:, j, 3:3 + TT], in_=p, func=AF.Copy))
            cv, t_cv = G[0], tG[0]
            if sample:
                for c in range(nch):
                    if gi == 0:
                        k.DMA("gpsimd", k.t_halCs, w=[k.t_halCs], out=k.halCs[c][:], in_=k.s_gdc[l, c])
                    k.conv(zxv, t_zx, k.halCs[c][:, gi * 4:(gi + 1) * 4, :], k.t_halCs, cv, t_cv, 4, c0 + CL["ccw"] + gi * 4, None, c * CH, CH, seq_halo=True, wstride=12)
                    k.DMA("gpsimd", k.outslot, r=[t_zx], out=k.o_gdc[l, 1 + c, :, gi * 4:(gi + 1) * 4, :],
                          in_=zxv[:, :, 3 + (c + 1) * CH - 3:3 + (c + 1) * CH], final=True)
            else:
                k.conv(zxv, t_zx, None, None, cv, t_cv, 4, c0 + CL["ccw"] + gi * 4, None, 0, TT, seq_halo=False, wstride=12)
                k.V("tensor_copy", r=[t_zx], w=[k.t_halC[l]], out=hal, in_=zxv[:, :, TT:TT + 3])
            k.A("activation", r=[t_cv], w=[t_cv], out=cv[:, :, 0:TT], in_=cv[:, :, 0:TT], func=AF.Silu)
            if gi < 2:
                sq = k.BB[:, 0:2048].rearrange("p (a b) -> p a b", a=4)
                tsq = [k.t_B[0]]
                k.A("activation", r=[t_cv], w=tsq, out=sq[:, :, 0:TT], in_=cv[:, :, 0:TT], func=AF.Square)
                for h in range(4):
                    pb, pt = k.sumsq_bcast([sq[:, h, 0:TT]], tsq, 1, 1.0, TT)
                    k.rstd_from(pb, pt, TT, 1.0, k.rs4[:, h, 0:TT], k.t_rs4)
                if gi == 0:
                    k.V("scalar_tensor_tensor", r=[t_cv, k.t_rs4], w=[t_dst], out=dst[:, :, 0:TT], in0=cv[:, :, 0:TT], scalar=128.0 ** -0.5,
                        in1=k.rs4[:, :, 0:TT], op0=ALU.mult, op1=ALU.mult)
                else:
                    k.V("tensor_tensor", r=[t_cv, k.t_rs4], w=[t_dst], out=dst[:, :, 0:TT], in0=cv[:, :, 0:TT], in1=k.rs4[:, :, 0:TT], op=ALU.mult)
            else:
                k.E("tensor_copy", r=[t_cv], w=[t_dst], out=dst[:, :, 0:TT], in_=cv[:, :, 0:TT])
        wt, tw = k.wload(l, "cg")
        k.proj_fm(wt, tw, 4, TT, lambda j, p, pt: k.A("activation", r=[pt], w=[t_gC], out=gC[:, j, 0:TT], in_=p, func=AF.Silu))
        wba, twba = k.wload(l, "ba")
        oC, t_oC = G[1], tG[1]
        NB = nch * 256
        W = nch * 4
        def gf(i):
            if i < 4:
                return k.GA[0:64, i * 2048:i * 2048 + NB]
            if i == 4:
                return k.GB[0:64, 0:NB]
            return k.GB[0:64, 2048:2048 + NB]
        def w3(ap):
            return ap.rearrange("p (w s) -> p w s", s=64)
        def rr(ap):
            return ap.bitcast(F32R) if USE_F32R else ap
        identW = k.m_ident.unsqueeze(1).to_broadcast([64, W, 64])
        pb, pt = k.pbank()
        for c in range(nch):
            for kc in range(8):
                k.T("matmul", r=[k.t_hc[kc], twba], w=[pt], out=pb[0:64, c * 8:(c + 1) * 8], lhsT=k.hT[:, kc, c * CH:(c + 1) * CH], rhs=wba[:, kc, 0:8],
                    start=(kc == 0), stop=(kc == 7))
        bgA, t_bgA = k.bgA, k.t_bgA
        p3 = pb[0:64, 0:nch * 8].rearrange("p (c x) -> p c x", x=8)
        b3 = bgA[:, 0:nch, :]
        k.A("activation", r=[pt], w=[t_bgA], out=b3[:, :, 0:4], in_=p3[:, :, 0:4], func=AF.Sigmoid)
        k.V("tensor_tensor", r=[pt, k.t_cst], w=[t_bgA], out=b3[:, :, 4:8], in0=p3[:, :, 4:8],
            in1=cst[0:64, c0 + CL["dtb"]:c0 + CL["dtb"] + 4].unsqueeze(1).to_broadcast([64, nch, 4]), op=ALU.add)
        k.A("activation", r=[t_bgA], w=[t_bgA], out=b3[:, :, 4:8], in_=b3[:, :, 4:8], func=AF.Exp)
        k.V("tensor_scalar", r=[t_bgA], w=[t_bgA], out=b3[:, :, 4:8], in0=b3[:, :, 4:8], scalar1=1.0, scalar2=None, op0=ALU.add)
        k.A("activation", r=[t_bgA], w=[t_bgA], out=b3[:, :, 4:8], in_=b3[:, :, 4:8], func=AF.Ln)
        k.V("tensor_tensor", r=[t_bgA, k.t_der], w=[t_bgA], out=b3[:, :, 4:8], in0=b3[:, :, 4:8],
            in1=k.nega[:, l * 4:l * 4 + 4].unsqueeze(1).to_broadcast([64, nch, 4]), op=ALU.mult)
        pb, pt = k.pbank()
        k.T("matmul", r=[t_bgA, k.t_k64], w=[pt], out=pb[0:64, 0:W].rearrange("p (c h) -> p c h", h=4), lhsT=k.L64, rhs=b3[:, :, 4:8], start=True, stop=True)
        gcs, t_gcs = k.gcs, k.t_gcs
        k.V("tensor_copy", r=[pt], w=[t_gcs], out=gcs[:, 0, 0:W], in_=pb[0:64, 0:W])
        k.V("tensor_copy", r=[t_bgA], w=[t_gcs], out=gcs[:, 1, 0:W].rearrange("p (c h) -> p c h", h=4), in_=b3[:, :, 0:4])
        gcv = gcs[:, 0, 0:W]
        btv = gcs[:, 1, 0:W]
        ktok, t_ktok = k.TK[0], k.t_TK[0]
        vtk = k.GB[0:64, 2048:4096].bitcast(BF16).rearrange("p (c x) -> p c x", c=8)
        t_vtk = tG[5]
        for src, t_src, dst, t_dst in ((kT, t_kT, ktok, t_ktok), (vT, t_vT, vtk, t_vtk)):
            q, tq = k.pquad()
            qv = q[:].bitcast(BF16)
            for c in range(nch):
                for h in range(4):
                    k.T("transpose", r=[t_src, k.t_kc], w=tq, out=qv[0:64, c * 512 + h * 128:c * 512 + (h + 1) * 128], in_=src[:, h, c * CH:(c + 1) * CH], identity=k.identb[:])
            k.V("tensor_copy", r=tq, w=[t_dst], out=dst[:, 0:nch, :], in_=qv[0:64, 0:nch * 512].rearrange("p (c x) -> p c x", c=nch))
        dgG = w3(gf(3))
        k.V("tensor_tensor", r=[t_gcs, k.t_k64], w=[tG[3]], out=dgG, in0=identW, in1=gcv.unsqueeze(2).to_broadcast([64, W, 64]), op=ALU.mult)
        qg, tqg = k.pquad()
        for b in range(NB // 512):
            k.T("matmul", r=[tG[3], k.t_k64], w=tqg, out=qg[:, b * 512:(b + 1) * 512], lhsT=k.ones64, rhs=gf(3)[:, b * 512:(b + 1) * 512], start=True, stop=True)
        egr = k.GA[:, 2 * 2048:2 * 2048 + NB]
        k.A("activation", r=tqg, w=[tG[2]], out=egr, in_=qg[:, 0:NB], func=AF.Exp)
        qtl = k.BB[:, 5 * 2048:5 * 2048 + NB].rearrange("p (c h s) -> p c h s", h=4, s=64)
        t_qtl = tB[5]
        for h in range(4):
            k.E("tensor_tensor", r=[t_qT, tG[2], t_vtk], w=[t_qtl], out=qtl[:, :, h, :], in0=qT[:, h, 0:TT].rearrange("p (c s) -> p c s", s=64),
                in1=egr.rearrange("p (c h s) -> p c h s", h=4, s=64)[:, :, h, :], op=ALU.mult)
        egl, t_egl = k.egl, k.t_egl
        k.V("tensor_copy", r=[tG[2]], w=[t_egl], out=egl[:, 0:W], in_=egr.rearrange("p (w s) -> p w s", s=64)[:, :, CH - 1])
        Em = w3(gf(0))
        k.V("tensor_tensor", r=tqg + [t_gcs], w=[tG[0]], out=Em, in0=w3(qg[0:64, 0:NB]), in1=gcv.unsqueeze(2).to_broadcast([64, W, 64]), op=ALU.subtract)
        k.V("tensor_copy", r=tqg, w=[t_gcs], out=gcs[:, 5, 0:W], in_=w3(qg[0:64, 0:NB])[:, :, CH - 1])
        k.V("tensor_tensor", r=[t_gcs], w=[t_gcs], out=gcs[:, 4, 0:W], in0=gcs[:, 5, 0:W], in1=gcv, op=ALU.subtract)
        k.A("activation", r=[t_gcs], w=[t_gcs], out=gcs[:, 4, 0:W], in_=gcs[:, 4, 0:W], func=AF.Exp)
        k.A("activation", r=[t_gcs], w=[t_gcs], out=gcs[:, 2, 0:W], in_=gcv, func=AF.Exp)
        k.V("tensor_tensor", r=[t_gcs], w=[t_gcs], out=gcs[:, 3, 0:W], in0=gcs[:, 2, 0:W], in1=btv, op=ALU.mult)
        GmU = w3(gf(2)); GmL = Em; GmI = w3(gf(3))
        k.V("tensor_scalar", r=[tG[0], t_qtl, t_egl], w=[tG[2]], out=GmU, in0=Em, scalar1=0.0, scalar2=None, op0=ALU.min)
        k.A("activation", r=[tG[2]], w=[tG[2]], out=GmU, in_=GmU, func=AF.Exp)
        k.V("tensor_scalar", r=[tG[0]], w=[tG[0]], out=GmL, in0=Em, scalar1=0.0, scalar2=None, op0=ALU.max)
        k.A("activation", r=[tG[0]], w=[tG[0]], out=GmL, in_=GmL, func=AF.Exp, scale=-1.0)
        k.E("tensor_tensor", r=[tG[2], k.t_k64], w=[tG[3]], out=GmI, in0=GmU, in1=k.m_inclU.unsqueeze(1).to_broadcast([64, W, 64]), op=ALU.mult)
        k.E("tensor_tensor", r=[tG[2], k.t_k64], w=[tG[2]], out=GmU, in0=GmU, in1=k.m_strictU.unsqueeze(1).to_broadcast([64, W, 64]), op=ALU.mult)
        k.E("tensor_tensor", r=[tG[0], k.t_k64], w=[tG[0]], out=GmL, in0=GmL, in1=k.m_strictL.unsqueeze(1).to_broadcast([64, W, 64]), op=ALU.mult)
        qk_, tqk = k.pquad()
        for c in range(nch):
            for h in range(4):
                w_ = c * 4 + h
                k.T("matmul", r=[t_kT], w=tqk, out=qk_[0:64, w_ * 64:(w_ + 1) * 64], lhsT=kT[:, h, c * CH:(c + 1) * CH], rhs=kT[:, h, c * CH:(c + 1) * CH], start=True, stop=True)
        Pn = w3(gf(0)); Pt = w3(gf(2))
        k.V("tensor_tensor", r=tqk + [tG[0]], w=[tG[0]], out=Pn, in0=w3(qk_[0:64, 0:NB]), in1=GmL, op=ALU.mult)
        k.V("scalar_tensor_tensor", r=[tG[0], t_gcs], w=[tG[0]], out=Pn, in0=Pn, scalar=-1.0, in1=btv.unsqueeze(2).to_broadcast([64, W, 64]), op0=ALU.mult, op1=ALU.mult)
        k.V("tensor_tensor", r=tqk + [tG[2]], w=[tG[2]], out=Pt, in0=w3(qk_[0:64, 0:NB]), in1=GmU, op=ALU.mult)
        qq, tqq = k.pquad()
        for c in range(nch):
            for h in range(4):
                w_ = c * 4 + h
                k.T("matmul", r=[t_kT, t_qT], w=tqq, out=qq[0:64, w_ * 64:(w_ + 1) * 64], lhsT=kT[:, h, c * CH:(c + 1) * CH], rhs=qT[:, h, c * CH:(c + 1) * CH], start=True, stop=True)
        QKm = k.BB[0:64, 0:NB].rearrange("p (c h s) -> p c h s", h=4, s=64)
        t_QKm = tB[0]
        k.V("tensor_tensor", r=tqq + [tG[3]], w=[t_QKm], out=QKm.rearrange("p c h s -> p (c h) s"), in0=w3(qq[0:64, 0:NB]), in1=GmI, op=ALU.mult)
        dgB = w3(gf(4))
        k.V("tensor_tensor", r=[t_gcs, k.t_k64], w=[tG[4]], out=dgB, in0=identW, in1=btv.unsqueeze(2).to_broadcast([64, W, 64]), op=ALU.mult)
        qb, tqb = k.pquad()
        for b in range(NB // 512):
            k.T("matmul", r=[tG[4], k.t_k64], w=tqb, out=qb[0:64, b * 512:(b + 1) * 512], lhsT=k.ones64[:, 0:64], rhs=gf(4)[:, b * 512:(b + 1) * 512], start=True, stop=True)
        k.V("scalar_tensor_tensor", r=tqb + [tG[2]], w=[tG[2]], out=Pt, in0=Pt, scalar=-1.0, in1=w3(qb[0:64, 0:NB]), op0=ALU.mult, op1=ALU.mult)
        Xt = w3(gf(3)); IP = w3(gf(4))
        k.V("tensor_tensor", r=[tG[2], k.t_k64, t_QKm], w=[tG[3]], out=Xt, in0=Pt, in1=identW, op=ALU.add)
        for lev in range(1, 6):
            qn, tqn = k.pquad()
            for w_ in range(W):
                k.T("matmul", r=[tG[0], tG[2]], w=tqn, out=qn[0:64, w_ * 64:(w_ + 1) * 64], lhsT=rr(gf(2)[:, w_ * 64:(w_ + 1) * 64]), rhs=rr(gf(0)[:, w_ * 64:(w_ + 1) * 64]), start=True, stop=True)
            if lev < 5:
                qt_, tqt = k.pquad()
                for w_ in range(W):
                    k.T("matmul", r=[tG[0], tG[2]], w=tqt, out=qt_[0:64, w_ * 64:(w_ + 1) * 64], lhsT=rr(gf(0)[:, w_ * 64:(w_ + 1) * 64]), rhs=rr(gf(2)[:, w_ * 64:(w_ + 1) * 64]), start=True, stop=True)
            k.V("tensor_tensor", r=tqn + [k.t_k64], w=[tG[4]], out=IP, in0=w3(qn[0:64, 0:NB]), in1=identW, op=ALU.add)
            if lev < 5:
                k.A("activation", r=tqn, w=[tG[0]], out=gf(0), in_=qn[0:64, 0:NB], func=AF.Copy)
                k.A("activation", r=tqt, w=[tG[2]], out=gf(2), in_=qt_[0:64, 0:NB], func=AF.Copy)
            qx, tqx = k.pquad()
            for w_ in range(W):
                k.T("matmul", r=[tG[4], tG[3]], w=tqx, out=qx[0:64, w_ * 64:(w_ + 1) * 64], lhsT=rr(gf(4)[:, w_ * 64:(w_ + 1) * 64]), rhs=rr(gf(3)[:, w_ * 64:(w_ + 1) * 64]), start=True, stop=True)
            k.V("tensor_copy", r=tqx, w=[tG[3]], out=gf(3), in_=qx[0:64, 0:NB])
        Xtb = k.BB[0:64, 3 * 2048:3 * 2048 + NB]
        t_Xtb = tB[3]
        k.V("tensor_copy", r=[tG[3], t_qtl], w=[t_Xtb], out=Xtb, in_=gf(3))
        k.E("tensor_tensor", r=[t_vtk, t_gcs], w=[t_vtk], out=vtk[:, 0:nch, :].rearrange("p c (h v) -> p (c h) v", h=4),
            in0=vtk[:, 0:nch, :].rearrange("p c (h v) -> p (c h) v", h=4), in1=btv.unsqueeze(2).to_broadcast([64, W, 128]), op=ALU.mult)
        kbg = k.GA[0:64, 0:2048].bitcast(BF16).rearrange("p (c x) -> p c x", c=8)
        khc = k.GA[0:64, 4096:6144].bitcast(BF16).rearrange("p (c x) -> p c x", c=8)
        k.E("tensor_tensor", r=[t_ktok, t_gcs, tG[0]], w=[tG[0]], out=kbg[:, 0:nch, :].rearrange("p c (h v) -> p (c h) v", h=4),
            in0=ktok[:, 0:nch, :].rearrange("p c (h v) -> p (c h) v", h=4), in1=gcs[:, 3, 0:W].unsqueeze(2).to_broadcast([64, W, 128]), op=ALU.mult)
        k.E("tensor_tensor", r=[t_ktok, t_gcs, tG[2]], w=[tG[2]], out=khc[:, 0:nch, :].rearrange("p c (h v) -> p (c h) v", h=4),
            in0=ktok[:, 0:nch, :].rearrange("p c (h v) -> p (c h) v", h=4), in1=gcs[:, 4, 0:W].unsqueeze(2).to_broadcast([64, W, 128]), op=ALU.mult)
        uu = k.GB[0:64, 0:2048].bitcast(BF16).rearrange("p (c x) -> p c x", c=8)
        t_uu = tG[4]
        for half in range((nch + 3) // 4):
            qu, tqu = k.pquad()
            ncl = min(4, nch - half * 4)
            for cl in range(ncl):
                c = half * 4 + cl
                for h in range(4):
                    k.T("matmul", r=[t_Xtb, t_vtk], w=tqu, out=qu[0:64, cl * 512 + h * 128:cl * 512 + (h + 1) * 128], lhsT=Xtb[:, (c * 4 + h) * 64:(c * 4 + h + 1) * 64],
                        rhs=vtk[:, c, h * 128:(h + 1) * 128], start=True, stop=True)
            k.A("activation", r=tqu, w=[t_uu], out=uu[:, half * 4:half * 4 + ncl, :], in_=qu[0:64, 0:ncl * 512].rearrange("p (c x) -> p c x", x=512), func=AF.Copy)
        wTc = k.BB[:, 2048:2048 + NB].rearrange("p (c h s) -> p c h s", h=4, s=64)
        t_wTc = tB[1]
        qw, tqw = k.pquad()
        for c in range(nch):
            for h in range(4):
                w_ = c * 4 + h
                k.T("matmul", r=[t_Xtb, tG[0]], w=tqw, out=qw[:, w_ * 64:(w_ + 1) * 64], lhsT=kbg[:, c, h * 128:(h + 1) * 128], rhs=Xtb[:, w_ * 64:(w_ + 1) * 64], start=True, stop=True)
        k.A("activation", r=tqw, w=[t_wTc], out=wTc.rearrange("p c h s -> p (c h s)"), in_=qw[:, 0:NB], func=AF.Copy)
        for c in range(nch):
            cs = slice(c * CH, (c + 1) * CH)
            pi = c % 2
            if sample:
                k.load_state_C(l, c)
            k.A("activation", r=[k.t_SC[l]], w=[k.t_SCb], out=k.SCb[:], in_=k.SC[l][:], func=AF.Copy)
            k.G("tensor_tensor", r=[t_egl, k.t_SC[l]], w=[k.t_SC[l]], out=k.SC[l][:], in0=k.SC[l][:],
                in1=egl[:, c * 4:(c + 1) * 4].unsqueeze(2).to_broadcast([128, 4, 128]), op=ALU.mult)
            pv, ptv = k.pbank()
            for h in range(4):
                k.T("matmul", r=[t_wTc, k.t_SCb], w=[ptv], out=pv[0:64, h * 128:(h + 1) * 128], lhsT=wTc[:, c, h, :], rhs=k.SCb[:, h, :], start=True, stop=True)
            vn, t_vn = k.vn[pi], k.t_vn[pi]
            k.V("tensor_tensor", r=[ptv, t_uu], w=[t_vn], out=vn[:], in0=uu[:, c, :].rearrange("p (a b) -> p a b", a=4),
                in1=pv[0:64, 0:512].rearrange("p (a b) -> p a b", a=4), op=ALU.subtract)
            po, pto = k.pbank()
            for h in range(4):
                k.T("matmul", r=[k.t_SCb, t_qtl], w=[pto], out=po[:, h * 64:(h + 1) * 64], lhsT=k.SCb[:, h, :], rhs=qtl[:, c, h, :], start=True, stop=False)
                k.T("matmul", r=[t_vn, t_QKm], w=[pto], out=po[:, h * 64:(h + 1) * 64], lhsT=vn[:, h, :], rhs=QKm[:, c, h, :], start=False, stop=True)
            k.A("activation", r=[pto], w=[t_oC], out=oC[:, :, cs], in_=po[:, 0:256].rearrange("p (a b) -> p a b", a=4), func=AF.Copy)
            psn, ptsn = k.pbank()
            for h in range(4):
                k.T("matmul", r=[tG[2], t_vn], w=[ptsn], out=psn[:, h * 128:(h + 1) * 128], lhsT=khc[:, c, h * 128:(h + 1) * 128], rhs=vn[:, h, :], start=True, stop=True)
            k.V("tensor_tensor", r=[ptsn, k.t_SC[l]], w=[k.t_SC[l]], out=k.SC[l][:], in0=k.SC[l][:],
                in1=psn[:, 0:512].rearrange("p (a b) -> p a b", a=4), op=ALU.add)
            if sample:
                k.store_state_C(l, c)
        k.headnorm_gate(oC, t_oC, gC, t_gC, c0 + CL["c_norm"], k.oT[2], k.t_o[2], TT)

        mbT = k.BB[:, 2 * 2048:4 * 2048].rearrange("p (a b) -> p a b", a=8)
        t_mb = [tB[2], tB[3]]
        for j in range(8):
            wg, twg = k.wload(l, "mg%d" % j)
            wr, twr = k.wload(l, "br%d" % j)
            acc, t_acc = k.tmpa[j % 2], k.t_tmpa[j % 2]
            gb = [k.pbank() for _ in range(3)]
            wbk = [k.pbank() for _ in range(3)]
            for kc in range(8):
                for b in range(3):
                    pgt, ptgt = gb[b]
                    k.T("matmul", r=[k.t_hc[kc], twg], w=[ptgt], out=pgt[:, 0:TT], lhsT=wg[:, kc, b * 128:(b + 1) * 128], rhs=k.hT[:, kc, 0:TT], start=(kc == 0), stop=(kc == 7))
            for kc in range(4):
                for b in range(3):
                    pwd, ptwd = wbk[b]
                    k.T("matmul", r=[k.t_o[b], twr], w=[ptwd], out=pwd[:, 0:TT], lhsT=wr[:, b * 4 + kc, :], rhs=k.oT[b][:, kc, 0:TT], start=(kc == 0), stop=(kc == 3))
            for b in range(3):
                pgt, ptgt = gb[b]
                pwd, ptwd = wbk[b]
                sg, t_sg = k.tmpb[b], k.t_tmpb[b]
                k.A("activation", r=[ptgt], w=[t_sg], out=sg[:, 0:TT], in_=pgt[:, 0:TT], func=AF.Sigmoid)
                if b == 0:
                    k.V("tensor_tensor", r=[ptwd, t_sg], w=[t_acc], out=acc[:, 0:TT], in0=pwd[:, 0:TT], in1=sg[:, 0:TT], op=ALU.mult)
                else:
                    t2, t_t2 = k.tmpa[2], k.t_tmpa[2]
                    k.V("tensor_tensor", r=[ptwd, t_sg], w=[t_t2], out=t2[:, 0:TT], in0=pwd[:, 0:TT], in1=sg[:, 0:TT], op=ALU.mult)
                    if b == 1:
                        k.V("tensor_tensor", r=[t_t2, t_acc], w=[t_acc], out=acc[:, 0:TT], in0=acc[:, 0:TT], in1=t2[:, 0:TT], op=ALU.add)
                    else:
                        k.V("tensor_tensor", r=[t_t2, t_acc], w=t_mb, out=mbT[:, j, 0:TT], in0=acc[:, 0:TT], in1=t2[:, 0:TT], op=ALU.add)
        yT = k.GA[:, 0:4096].rearrange("p (a b) -> p a b", a=8)
        ty = [tG[0], tG[1]]
        for half in range(2):
            wt, tw = k.wload(l, "out%d" % half)
            banks = [k.pbank() for _ in range(4)]
            for kc in range(8):
                for j in range(4):
                    pb, pt = banks[j]
                    k.T("matmul", r=t_mb + [tw], w=[pt], out=pb[:, 0:TT], lhsT=wt[:, kc, j * 128:(j + 1) * 128], rhs=mbT[:, kc, 0:TT], start=(kc == 0), stop=(kc == 7))
            for j in range(4):
                pb, pt = banks[j]
                k.A("activation", r=[pt], w=ty, out=yT[:, half * 4 + j, 0:TT], in_=pb[:, 0:TT], func=AF.Copy)
        k.postnorm_add("g_post", l, TT)

        k.prenorm("g_pmlp", l, TT)
        aT = k.GA[:].bitcast(BF16).rearrange("p (a b) -> p a b", a=32)
        t_a = [tG[0], tG[1], tG[2], tG[3]]
        t_aT = k.t_aT
        k.V("memset", w=t_a + t_aT, ap=k.fence[:], constant=0.0)
        for u in range(8):
            wt, tw = k.wload(l, "up%d" % u)
            banks = [k.pbank() for _ in range(4)]
            for kc in range(8):
                for j in range(4):
                    pb, pt = banks[j]
                    k.T("matmul", r=[k.t_hc[kc], tw], w=[pt], out=pb[:, 0:TT], lhsT=wt[:, kc, j * 128:(j + 1) * 128], rhs=k.hT[:, kc, 0:TT], start=(kc == 0), stop=(kc == 7))
            for j in range(4):
                pb, pt = banks[j]
                rl, t_rl = k.tmpa[j % 2], k.t_tmpa[j % 2]
                k.A("activation", r=[pt], w=[t_rl], out=rl[:, 0:TT], in_=pb[:, 0:TT], func=AF.Relu)
                k.E("tensor_tensor", r=[t_rl], w=[t_aT[u * 4 + j]], out=aT[:, u * 4 + j, 0:TT], in0=rl[:, 0:TT], in1=rl[:, 0:TT], op=ALU.mult)
        yT2 = k.GB[:, 0:4096].rearrange("p (a b) -> p a b", a=8)
        ty2 = [tG[4], tG[5]]
        for half in range(2):
            banks = [k.pbank() for _ in range(4)]
            for g in range(8):
                wt, tw = k.wload(l, "dn%d_%d" % (half, g))
                for kc in range(4):
                    for j in range(4):
                        pb, pt = banks[j]
                        k.T("matmul", r=[t_aT[g * 4 + kc], tw], w=[pt], out=pb[:, 0:TT], lhsT=wt[:, kc, j * 128:(j + 1) * 128], rhs=aT[:, g * 4 + kc, 0:TT],
                            start=(g == 0 and kc == 0), stop=(g == 7 and kc == 3))
            for j in range(4):
                pb, pt = banks[j]
                k.A("activation", r=[pt], w=ty2, out=yT2[:, half * 4 + j, 0:TT], in_=pb[:, 0:TT], func=AF.Copy)
        k.E("tensor_copy", r=ty2, w=t_a + t_aT, out=k.GA[:, 0:4096], in_=k.GB[:, 0:4096])
        k.postnorm_add("g_postmlp", l, TT)

    def mixB(self, l, TT, sample, first):
        k = self
        nch = TT // CH
        c0 = l * NCL
        d0 = l * 16
        cst = k.cst
        G = [k.Gv(i) for i in range(6)]
        tG = k.t_G
        B = [k.Bv(i) for i in range(6)]
        tB = k.t_B
        zx, t_zx = k.Gv(5), tG[5]
        zxv = k.GB[:, 2048:2048 + 4 * 515].rearrange("p (a b) -> p a b", a=4)
        gB, t_gB = B[0], tB[0]
        wt, tw = k.wload(l, "bx")
        if sample:
            pass
        else:
            k.V("tensor_copy", r=[k.t_halB[l]], w=[t_zx], out=zxv[:, :, 0:3], in_=k.halB[l][:])
        k.proj_fm(wt, tw, 4, TT, lambda j, p, pt: k.A("activation", r=[pt], w=[t_zx], out=zxv[:, j, 3:3 + TT], in_=p, func=AF.Copy))
        yield
        wt, tw = k.wload(l, "bg")
        k.proj_fm(wt, tw, 4, TT, lambda j, p, pt: k.A("activation", r=[pt], w=[t_gB], out=gB[:, j, 0:TT], in_=p, func=AF.Gelu))
        yield
        k.DMA("gpsimd", k.t_gwb, w=[k.t_gwb], out=k.gwb[:], in_=k.gw_d[:, l * 1024:(l + 1) * 1024])
        xc, t_xc = G[2], tG[2]
        if sample:
            for c in range(nch):
                k.DMA("gpsimd", k.t_halB[l], w=[k.t_halB[l]], out=k.halB[l][:], in_=k.s_rgc[l, c])
                k.conv(zxv, t_zx, k.halB[l], k.t_halB[l], xc, t_xc, 4, c0 + CL["bcw"], c0 + CL["bcb"], c * CH, CH, seq_halo=True)
                k.DMA("gpsimd", k.outslot, r=[t_zx], out=k.o_rgc[l, 1 + c], in_=zxv[:, :, 3 + (c + 1) * CH - 3:3 + (c + 1) * CH], final=True)
        else:
            for cc_ in range(4):
                k.conv(zxv, t_zx, None, None, xc, t_xc, 4, c0 + CL["bcw"], c0 + CL["bcb"], 0, TT, seq_halo=False, ccs=[cc_])
                yield
            k.V("tensor_copy", r=[t_zx], w=[k.t_halB[l]], out=k.halB[l][:], in_=zxv[:, :, TT:TT + 3])
        xcb, t_xcb = k.GA[:, 2048:4096].bitcast(BF16)[:, 0:2048].rearrange("p (a b) -> p a b", a=4), tG[1]
        k.E("tensor_copy", r=[t_xc], w=[t_xcb], out=xcb[:, :, 0:TT], in_=xc[:, :, 0:TT])
        hs, t_hs = G[3], tG[3]
        for cc in range(4):
            r_, i_, a_ = k.tmpa[0], k.tmpa[1], k.tmpa[2]
            tr, ti_, ta = k.t_tmpa
            pb, pt = k.pbank()
            k.T("matmul", r=[t_xcb, k.t_gwb], w=[pt], out=pb[:, 0:TT], lhsT=k.gwb[:, cc * 128:(cc + 1) * 128], rhs=xcb[:, cc, 0:TT], start=True, stop=True)
            k.A("activation", r=[pt, k.t_cst], w=[tr], out=r_[:, 0:TT], in_=pb[:, 0:TT], func=AF.Sigmoid, bias=cst[:, c0 + CL["gab"] + cc:c0 + CL["gab"] + cc + 1])
            pb, pt = k.pbank()
            k.T("matmul", r=[t_xcb, k.t_gwb], w=[pt], out=pb[:, 0:TT], lhsT=k.gwb[:, 512 + cc * 128:512 + (cc + 1) * 128], rhs=xcb[:, cc, 0:TT], start=True, stop=True)
            k.A("activation", r=[pt, k.t_cst], w=[ti_], out=i_[:, 0:TT], in_=pb[:, 0:TT], func=AF.Sigmoid, bias=cst[:, c0 + CL["gxb"] + cc:c0 + CL["gxb"] + cc + 1])
            yield
            k.A("activation", r=[tr, k.t_der], w=[ta], out=a_[:, 0:TT], in_=r_[:, 0:TT], func=AF.Exp, scale=k.der[:, d0 + 8 + cc:d0 + 9 + cc])
            k.A("activation", r=[tr, k.t_der], w=[tr], out=r_[:, 0:TT], in_=r_[:, 0:TT], func=AF.Exp, scale=k.der[:, d0 + 12 + cc:d0 + 13 + cc])
            k.V("tensor_scalar", r=[tr], w=[tr], out=r_[:, 0:TT], in0=r_[:, 0:TT], scalar1=-1.0, scalar2=1.0, op0=ALU.mult, op1=ALU.add)
            k.A("activation", r=[tr], w=[tr], out=r_[:, 0:TT], in_=r_[:, 0:TT], func=AF.Sqrt)
            if first:
                k.V("memset", w=[tr], ap=r_[:, 0:1], constant=1.0)
            yield
            k.V("tensor_tensor", r=[tr, ti_], w=[ti_], out=i_[:, 0:TT], in0=i_[:, 0:TT], in1=r_[:, 0:TT], op=ALU.mult)
            k.V("tensor_tensor", r=[ti_, t_xc], w=[ti_], out=i_[:, 0:TT], in0=i_[:, 0:TT], in1=xc[:, cc, 0:TT], op=ALU.mult)
            if sample:
                for c in range(nch):
                    if cc == 0:
                        k.DMA("gpsimd", k.t_hBs, w=[k.t_hBs], out=k.hBs[c][:], in_=k.s_rg[l, c])
                    k.V("tensor_tensor_scan", r=[ta, ti_, k.t_hBs], w=[t_hs], out=hs[:, cc, c * CH:(c + 1) * CH], data0=a_[:, c * CH:(c + 1) * CH],
                        data1=i_[:, c * CH:(c + 1) * CH], initial=k.hBs[c][:, cc:cc + 1], op0=ALU.mult, op1=ALU.add)
            else:
                k.V("tensor_tensor_scan", r=[ta, ti_, k.t_hB[l]], w=[t_hs], out=hs[:, cc, 0:TT], data0=a_[:, 0:TT], data1=i_[:, 0:TT],
                    initial=k.hB[l][:, cc:cc + 1], op0=ALU.mult, op1=ALU.add)
                k.V("tensor_copy", r=[t_hs], w=[k.t_hB[l]], out=k.hB[l][:, cc:cc + 1], in_=hs[:, cc, TT - 1:TT])
            yield
        if sample:
            for c in range(nch):
                k.V("tensor_copy", r=[t_hs], w=[k.t_hfin], out=k.hfin[:], in_=hs[:, :, (c + 1) * CH - 1])
                k.DMA("gpsimd", k.outslot, r=[k.t_hfin], out=k.o_rg[l, 1 + c], in_=k.hfin[:], final=True)
        k.E("tensor_tensor", r=[t_hs, t_gB], w=[k.t_o[1]], out=k.oT[1][:, :, 0:TT], in0=hs[:, :, 0:TT], in1=gB[:, :, 0:TT], op=ALU.mult)


    def conv(self, zxv, t_zx, hal, t_hal, out, t_out, ncc, wcol, bcol, t0, n, seq_halo, wstride=4, ccs=None):
        k = self
        cst = k.cst
        for cc in (range(ncc) if ccs is None else ccs):
            o = out[:, cc, t0:t0 + n]
            def wv(j):
                return cst[:, wcol + j * wstride + cc:wcol + j * wstride + cc + 1]
            rds = [t_zx, k.t_cst]
            if bcol is not None:
                k.V("tensor_scalar", r=rds, w=[t_out], out=o, in0=zxv[:, cc, 3 + t0:3 + t0 + n], scalar1=wv(3), scalar2=cst[:, bcol + cc:bcol + cc + 1],
                    op0=ALU.mult, op1=ALU.add)
            else:
                k.V("tensor_scalar", r=rds, w=[t_out], out=o, in0=zxv[:, cc, 3 + t0:3 + t0 + n], scalar1=wv(3), scalar2=None, op0=ALU.mult)
            for j in range(3):
                sh = 3 - j
                if not seq_halo:
                    k.V("scalar_tensor_tensor", r=rds + [t_out], w=[t_out], out=o, in0=zxv[:, cc, 3 + t0 - sh:3 + t0 - sh + n], scalar=wv(j), in1=o,
                        op0=ALU.mult, op1=ALU.add)
                else:
                    k.V("scalar_tensor_tensor", r=rds + [t_out], w=[t_out], out=out[:, cc, t0 + sh:t0 + n], in0=zxv[:, cc, 3 + t0:3 + t0 + n - sh], scalar=wv(j),
                        in1=out[:, cc, t0 + sh:t0 + n], op0=ALU.mult, op1=ALU.add)
                    k.V("scalar_tensor_tensor", r=rds + [t_out, t_hal], w=[t_out], out=out[:, cc, t0:t0 + sh], in0=hal[:, cc, 3 - sh:3], scalar=wv(j),
                        in1=out[:, cc, t0:t0 + sh], op0=ALU.mult, op1=ALU.add)

    def load_state_A(self, l, c):
        k = self
        k.DMA("gpsimd", k.t_SA[l], w=[k.t_SA[l]], out=k.SA[l][:], in_=k.s_hg[l, c].rearrange("h k v -> k h v"))

    def store_state_A(self, l, c):
        k = self
        k.DMA("gpsimd", k.outslot, r=[k.t_SA[l]], out=k.o_hg[l, 1 + c].rearrange("h k v -> k h v"), in_=k.SA[l][:], final=True)

    def load_state_C(self, l, c):
        k = self
        k.DMA("gpsimd", k.t_SC[l], w=[k.t_SC[l]], out=k.SC[l][:], in_=k.s_gd[l, c].rearrange("h k v -> k h v"))

    def store_state_C(self, l, c):
        k = self
        k.DMA("gpsimd", k.outslot, r=[k.t_SC[l]], out=k.o_gd[l, 1 + c].rearrange("h k v -> k h v"), in_=k.SC[l][:], final=True)

    def store_prompt_states(self):
        k = self
        for l in range(DEPTH):
            k.DMA("gpsimd", k.outslot, r=[k.t_SA[l]], out=k.o_hg[l, 0].rearrange("h k v -> k h v"), in_=k.SA[l][:], final=True)
            k.DMA("gpsimd", k.outslot, r=[k.t_SC[l]], out=k.o_gd[l, 0].rearrange("h k v -> k h v"), in_=k.SC[l][:], final=True)
            k.DMA("gpsimd", k.outslot, r=[k.t_hB[l]], out=k.o_rg[l, 0], in_=k.hB[l][:], final=True)
            k.DMA("gpsimd", k.outslot, r=[k.t_halB[l]], out=k.o_rgc[l, 0], in_=k.halB[l][:], final=True)
            k.DMA("gpsimd", k.outslot, r=[k.t_halC[l]], out=k.o_gdc[l, 0], in_=k.halC[l][:], final=True)

    def build(self):
        k = self
        k.alloc()
        k.hBs = [k.sb("hBs%d" % c, [128, 4], F32) for c in range(NSEQ)]
        k.halCs = [k.sb("halCs%d" % c, [128, 12, 3], F32) for c in range(NSEQ)]
        k.t_hBs = Buf("hBs"); k.t_halCs = Buf("halCs")
        k.hfin = k.sb("hfin", [128, 4], F32); k.t_hfin = Buf("hfin")
        k.setup()
        k.preconvert()
        xslot = Buf("xslot")
        for ti in range(k.npt):
            k.DMA("gpsimd", xslot, w=[k.t_x], out=k.xT[:], in_=k.xp[:, :, ti * 512:(ti + 1) * 512])
            for l in range(DEPTH):
                k.block(ti, l, 512, False)
            k.DMA("gpsimd", k.outslot, r=[k.t_x], out=k.yp[:, :, ti * 512:(ti + 1) * 512], in_=k.xT[:], final=True)
        k.store_prompt_states()
        if k.do_sample:
            TT = NSEQ * CH
            k.DMA("gpsimd", xslot, w=[k.t_x], out=k.xT[:, :, 0:TT], in_=k.xs[:, :, :])
            for l in range(DEPTH):
                k.block(0, l, TT, True)
            k.DMA("gpsimd", k.outslot, r=[k.t_x], out=k.ys[:, :, :], in_=k.xT[:, :, 0:TT], final=True)
        k.P.emit()
        k.P.close()
        return k.nc


def _wtile(src, kc, n0, nw, k0=0):
    blk = src[k0:k0 + kc * 128, n0:n0 + nw].reshape(kc, 128, nw)
    return np.ascontiguousarray(blk.transpose(1, 0, 2)).reshape(128, kc * nw)


def _layout_weights(w_in, w_br_a, w_br_b, w_br_c, w_out, w_up, w_down):
    wf = np.zeros((DEPTH, 128, WCOLS_PAD), np.float32)
    col_in = dict(aq=0, af=512, ai=1024, ag=1536, bx=2048, bg=2560, cq=3072, ck=3584, cv=4096, cg=4608)
    for l in range(DEPTH):
        for name, kc, nw, off in WT:
            if name in col_in:
                t = _wtile(w_in[l], 8, col_in[name], 512)
            elif name == "ba":
                t = _wtile(w_in[l], 8, 5120, 8)
            elif name.startswith("mg"):
                j = int(name[2:])
                parts = [w_in[l][:, 5128 + b * 1024 + j * 128:5128 + b * 1024 + (j + 1) * 128] for b in range(3)]
                t = _wtile(np.concatenate(parts, axis=1), 8, 0, 384)
            elif name.startswith("br"):
                j = int(name[2:])
                parts = [w[l][:, j * 128:(j + 1) * 128] for w in (w_br_a, w_br_b, w_br_c)]
                t = _wtile(np.concatenate(parts, axis=0), 12, 0, 128)
            elif name.startswith("out"):
                t = _wtile(w_out[l], 8, int(name[3:]) * 512, 512)
            elif name.startswith("up"):
                t = _wtile(w_up[l], 8, int(name[2:]) * 512, 512)
            else:
                h, g = name[2:].split("_")
                t = _wtile(w_down[l], 4, int(h) * 512, 512, k0=int(g) * 512)
            wf[l, :, off:off + kc * nw] = t
    return wf


def _fm(v, c):
    return np.ascontiguousarray(v.reshape(c, 128).T)


def _consts(inp):
    cst = np.zeros((128, DEPTH * NCL), np.float32)
    for l in range(DEPTH):
        b = l * NCL
        cst[:, b + CL["g_pre"]:b + CL["g_pre"] + 8] = _fm(inp["norm_pre_mix"][l], 8)
        cst[:, b + CL["g_post"]:b + CL["g_post"] + 8] = _fm(inp["norm_post_mix"][l], 8)
        cst[:, b + CL["g_pmlp"]:b + CL["g_pmlp"] + 8] = _fm(inp["norm_pre_mlp"][l], 8)
        cst[:, b + CL["g_postmlp"]:b + CL["g_postmlp"] + 8] = _fm(inp["norm_post_mlp"][l], 8)
        cst[:, b + CL["a_norm"]] = inp["a_norm"][l]
        cst[:, b + CL["c_norm"]] = inp["c_norm"][l]
        lr = np.stack([_fm(inp["lb_raw"][ll], 4) for ll in range(DEPTH)], axis=2)
        cst[:, b + CL["lbraw"]:b + CL["lbraw"] + 16] = lr.reshape(128, 16)
        for j in range(4):
            cst[:, b + CL["bcw"] + j * 4:b + CL["bcw"] + j * 4 + 4] = _fm(inp["b_conv_w"][l, j], 4)
            cst[:, b + CL["ccw"] + j * 12:b + CL["ccw"] + j * 12 + 12] = _fm(inp["c_conv_w"][l, j], 12)
        cst[:, b + CL["bcb"]:b + CL["bcb"] + 4] = _fm(inp["b_conv_b"][l], 4)
        cst[:, b + CL["gab"]:b + CL["gab"] + 4] = _fm(inp["b_gate_a_b"][l], 4)
        cst[:, b + CL["gxb"]:b + CL["gxb"] + 4] = _fm(inp["b_gate_x_b"][l], 4)
        cst[:, b + CL["lam"]:b + CL["lam"] + 4] = _fm(inp["b_lambda"][l], 4)
        cst[:, b + CL["alog"]:b + CL["alog"] + 4] = inp["c_a_log"][l][None, :]
        cst[:, b + CL["dtb"]:b + CL["dtb"] + 4] = inp["c_dt_bias"][l][None, :]
    gw = np.zeros((128, DEPTH * 1024), np.float32)
    for l in range(DEPTH):
        for wi, w in enumerate((inp["b_gate_a_w"][l], inp["b_gate_x_w"][l])):
            for cc in range(4):
                m = np.zeros((128, 128), np.float32)
                m[0:64, 0:64] = w[2 * cc]
                m[64:128, 64:128] = w[2 * cc + 1]
                gw[:, l * 1024 + wi * 512 + cc * 128:l * 1024 + wi * 512 + (cc + 1) * 128] = m
    k128 = np.zeros((128, 768), np.float32)
    k128[:, 0:128] = np.eye(128, dtype=np.float32)
    k128[:, 128:256] = 1.0
    sm = np.ones(512, np.float32)
    sm[::CH] = 0.0
    k128[:, 256:768] = sm[None, :]
    k64 = np.zeros((64, 896), np.float32)
    r = np.arange(64)
    k64[:, 0:64] = (r[:, None] <= r[None, :]).astype(np.float32)
    k64[:, 64:192] = 1.0
    strictL = (r[:, None] > r[None, :]).astype(np.float32)
    strictU = (r[:, None] < r[None, :]).astype(np.float32)
    inclU = (r[:, None] <= r[None, :]).astype(np.float32)
    k64[:, 192:256] = strictL
    k64[:, 256:320] = strictU
    k64[:, 320:384] = inclU
    k64[:, 384:896] = np.tile(np.eye(64, dtype=np.float32), (1, 8))
    return cst, gw, k128, k64


_NC_CACHE = {}
_LAST = {}


def _get_nc(npt, do_sample=True):
    key = (npt, do_sample)
    if key not in _NC_CACHE:
        kb = KB(npt, do_sample)
        _NC_CACHE[key] = kb.build()
        _LAST["streams"] = kb.P.streams
    return _NC_CACHE[key]


def kernel(x_prompt, x_sample, state_hgrn, state_rglru, state_rglru_conv, state_gdn, state_gdn_conv,
           lb_raw, norm_pre_mix, norm_post_mix, norm_pre_mlp, norm_post_mlp, w_in, a_norm,
           b_conv_w, b_conv_b, b_gate_a_w, b_gate_a_b, b_gate_x_w, b_gate_x_b, b_lambda,
           c_conv_w, c_a_log, c_dt_bias, c_norm, w_br_a, w_br_b, w_br_c, w_out, w_up, w_down, _npt=None):
    f = lambda a: np.asarray(a, dtype=np.float32)
    inp = dict(lb_raw=f(lb_raw), norm_pre_mix=f(norm_pre_mix), norm_post_mix=f(norm_post_mix), norm_pre_mlp=f(norm_pre_mlp),
               norm_post_mlp=f(norm_post_mlp), a_norm=f(a_norm), b_conv_w=f(b_conv_w), b_conv_b=f(b_conv_b),
               b_gate_a_w=f(b_gate_a_w), b_gate_a_b=f(b_gate_a_b), b_gate_x_w=f(b_gate_x_w), b_gate_x_b=f(b_gate_x_b),
               b_lambda=f(b_lambda), c_conv_w=f(c_conv_w), c_a_log=f(c_a_log), c_dt_bias=f(c_dt_bias), c_norm=f(c_norm))
    x_prompt = f(x_prompt); x_sample = f(x_sample)
    seq = x_prompt.shape[1]
    npt = seq // 512 if _npt is None else _npt
    ntok = npt * 512
    import time as _t, sys as _s
    _t0 = _t.time()
    nc = _get_nc(npt)
    print("[kernel] build %.1fs, ninstr=%s" % (_t.time() - _t0, {e: len(v) for e, v in _LAST.get("streams", {}).items()}), file=_s.stderr)
    wf = _layout_weights(f(w_in), f(w_br_a), f(w_br_b), f(w_br_c), f(w_out), f(w_up), f(w_down))
    cst, gw, k128, k64 = _consts(inp)
    xp = np.ascontiguousarray(x_prompt[0, :ntok].reshape(ntok, 8, 128).transpose(2, 1, 0))
    state_hgrn = f(state_hgrn); state_rglru = f(state_rglru); state_rglru_conv = f(state_rglru_conv)
    state_gdn = f(state_gdn); state_gdn_conv = f(state_gdn_conv)
    in_maps = []
    for c in range(NCORE):
        sl = slice(c * NSEQ, (c + 1) * NSEQ)
        xs = x_sample[sl].reshape(NSEQ * CH, 8, 128).transpose(2, 1, 0)
        in_maps.append(dict(
            xp=xp, xs=np.ascontiguousarray(xs), wf=wf, cst=cst, gw=gw, k128=k128, k64=k64,
            s_hg=np.ascontiguousarray(state_hgrn[:, sl]),
            s_rg=np.ascontiguousarray(state_rglru[:, sl].reshape(DEPTH, NSEQ, 4, 128).transpose(0, 1, 3, 2)),
            s_rgc=np.ascontiguousarray(state_rglru_conv[:, sl].reshape(DEPTH, NSEQ, 3, 4, 128).transpose(0, 1, 4, 3, 2)),
            s_gd=np.ascontiguousarray(state_gdn[:, sl]),
            s_gdc=np.ascontiguousarray(state_gdn_conv[:, sl].reshape(DEPTH, NSEQ, 3, 12, 128).transpose(0, 1, 4, 3, 2)),
        ))
    import time as _t, sys as _s
    _t0 = _t.time()
    res = run_bass_kernel_spmd(nc, in_maps, core_ids=list(range(NCORE)))
    print("[kernel] run %.1fs" % (_t.time() - _t0), file=_s.stderr)
    R = res.results
    yp = np.ascontiguousarray(R[0]["yp"].transpose(2, 1, 0)).reshape(1, ntok, D)
    ys = np.concatenate([np.ascontiguousarray(R[c]["ys"].transpose(2, 1, 0)).reshape(NSEQ, CH, D) for c in range(NCORE)], axis=0)
    def st(name, fn):
        p = fn(R[0][name][:, 0:1])
        s = np.concatenate([fn(R[c][name][:, 1:]) for c in range(NCORE)], axis=1)
        return np.ascontiguousarray(p), np.ascontiguousarray(s)
    ident = lambda a: a
    p_hg, s_hg = st("o_hg", ident)
    p_gd, s_gd = st("o_gd", ident)
    p_rg, s_rg = st("o_rg", lambda a: a.transpose(0, 1, 3, 2).reshape(DEPTH, a.shape[1], 512))
    p_rgc, s_rgc = st("o_rgc", lambda a: a.transpose(0, 1, 4, 3, 2).reshape(DEPTH, a.shape[1], 3, 512))
    p_gdc, s_gdc = st("o_gdc", lambda a: a.transpose(0, 1, 4, 3, 2).reshape(DEPTH, a.shape[1], 3, 1536))
    return (yp, ys, p_hg, p_rg, p_rgc, p_gd, p_gdc, s_hg, s_rg, s_rgc, s_gd, s_gdc)
```

```python
import numpy as np
import concourse.bass as bass
import concourse.mybir as mybir
from concourse.bass_utils import run_bass_kernel_spmd

F32 = mybir.dt.float32
BF16 = mybir.dt.bfloat16
ALU = mybir.AluOpType
AF = mybir.ActivationFunctionType

D = 1024
DEPTH = 4
SEQ = 16384
NCORE = 8
NSEQ = 4
CH = 64
EPS = 1e-6
D_IN = 8200
NPT = 32

ENGS = ("tensor", "vector", "scalar", "gpsimd", "sync")

WT = []
WOFF = {}


def _build_wt():
    off = 0
    def add(name, kc, nw):
        nonlocal off
        WT.append((name, kc, nw, off))
        WOFF[name] = (kc, nw, off)
        off += kc * nw
    for n in ("aq", "af", "ai", "ag", "bx", "bg", "cq", "ck", "cv", "cg"):
        add(n, 8, 512)
    add("ba", 8, 8)
    for j in range(8):
        add("mg%d" % j, 8, 384)
        add("br%d" % j, 12, 128)
    add("out0", 8, 512)
    add("out1", 8, 512)
    for j in range(8):
        add("up%d" % j, 8, 512)
    for h in range(2):
        for g in range(8):
            add("dn%d_%d" % (h, g), 4, 512)
    return off


WCOLS = _build_wt()
WCOLS_PAD = ((WCOLS + 4095) // 4096) * 4096

CL = {}


def _build_cl():
    off = 0
    def add(name, n):
        nonlocal off
        CL[name] = off
        off += n
    add("g_pre", 8); add("g_post", 8); add("g_pmlp", 8); add("g_postmlp", 8)
    add("a_norm", 1); add("c_norm", 1)
    add("lbraw", 16)
    add("bcw", 16)
    add("bcb", 4); add("gab", 4); add("gxb", 4); add("lam", 4)
    add("ccw", 48)
    add("alog", 4); add("dtb", 4)
    return off


NCL = _build_cl()


class Buf:
    __slots__ = ("name", "lw", "rd", "dsem", "dcnt")

    def __init__(self, name=""):
        self.name = name
        self.lw = None
        self.rd = []
        self.dsem = None
        self.dcnt = 0


class Prog:
    def __init__(self, nc):
        self.nc = nc
        self.streams = {e: [] for e in ENGS}
        self.cnt = {e: 0 for e in ENGS}
        self.sem = {}
        self.waited = {}
        self.ctx = []
        for e in ENGS:
            cm = nc.semaphore("cs_" + e)
            self.sem[e] = cm.__enter__()
            self.ctx.append(cm)
        self.ndsem = 0
        self.final = []

    def _dsem(self, b):
        if b.dsem is None:
            cm = self.nc.semaphore("ds%d" % self.ndsem)
            self.ndsem += 1
            b.dsem = cm.__enter__()
            self.ctx.append(cm)
        return b.dsem

    def _deps(self, reads, writes):
        deps = {}
        def add(kv):
            if kv is None:
                return
            k, v = kv
            if deps.get(k, 0) < v:
                deps[k] = v
        for b in reads:
            add(b.lw)
        for b in writes:
            add(b.lw)
            for r in b.rd:
                add(r)
        return deps

    def _waits(self, eng, deps, skip_self):
        waits = []
        for k, v in deps.items():
            if skip_self and k == eng:
                continue
            if self.waited.get((eng, k), 0) >= v:
                continue
            self.waited[(eng, k)] = v
            semh = self.sem[k] if isinstance(k, str) else k
            waits.append((semh, v))
        return waits

    def _mark(self, done, reads, writes):
        for b in reads:
            b.rd.append(done)
            if len(b.rd) > 64:
                mx = {}
                for k, v in b.rd:
                    if mx.get(k, 0) < v:
                        mx[k] = v
                b.rd = list(mx.items())
        for b in writes:
            b.lw = done
            b.rd = []

    def op(self, eng, name, kw, reads=(), writes=()):
        deps = self._deps(reads, writes)
        waits = self._waits(eng, deps, skip_self=(eng == "tensor"))
        self.cnt[eng] += 1
        done = (eng, self.cnt[eng])
        self.streams[eng].append((waits, name, kw, self.sem[eng], 1))
        self._mark(done, reads, writes)

    def dma(self, eng, kw, slot, reads=(), writes=(), final=False):
        semh = self._dsem(slot)
        deps = self._deps(reads, writes)
        if slot.dcnt > 0 and deps.get(semh, 0) < slot.dcnt:
            deps[semh] = slot.dcnt
        waits = self._waits(eng, deps, skip_self=False)
        slot.dcnt += 16
        done = (semh, slot.dcnt)
        self.streams[eng].append((waits, "dma_start", kw, semh, 16))
        self._mark(done, reads, writes)
        if final:
            self.final.append(done)

    def emit(self):
        nc = self.nc
        fin = {}
        for k, v in self.final:
            fin[k] = max(fin.get(k, 0), v)
        streams = self.streams
        with nc.Block() as block:
            def mk(ename):
                def body(engine):
                    for waits, name, kw, semh, inc in streams[ename]:
                        for (s, v) in waits:
                            engine.wait_ge(s, v)
                        getattr(engine, name)(**kw).then_inc(semh, inc)
                    if ename == "sync":
                        for s, v in fin.items():
                            engine.wait_ge(s, v)
                return body
            block.tensor(mk("tensor"))
            block.vector(mk("vector"))
            block.scalar(mk("scalar"))
            block.gpsimd(mk("gpsimd"))
            block.sync(mk("sync"))

    def close(self):
        for cm in reversed(self.ctx):
            cm.__exit__(None, None, None)


class KB:
    def __init__(self, npt=NPT, do_sample=True):
        self.npt = npt
        self.do_sample = do_sample
        nc = bass.Bass("TRN2", target_bir_lowering=False)
        self.nc = nc
        self.P = Prog(nc)
        self.cms = []
        self.rr = 0
        ntok = npt * 512
        self.ntok = ntok
        di = lambda n, s: nc.dram_tensor(n, s, F32, kind="ExternalInput").ap()
        do = lambda n, s: nc.dram_tensor(n, s, F32, kind="ExternalOutput").ap()
        self.xp = di("xp", [128, 8, ntok])
        self.xs = di("xs", [128, 8, NSEQ * CH])
        self.wf = di("wf", [DEPTH, 128, WCOLS_PAD])
        self.cst_d = di("cst", [128, DEPTH * NCL])
        self.gw_d = di("gw", [128, DEPTH * 1024])
        self.k128_d = di("k128", [128, 128 + 128 + 512])
        self.k64_d = di("k64", [64, 896])
        self.s_hg = di("s_hg", [DEPTH, NSEQ, 4, 128, 128])
        self.s_rg = di("s_rg", [DEPTH, NSEQ, 128, 4])
        self.s_rgc = di("s_rgc", [DEPTH, NSEQ, 128, 4, 3])
        self.s_gd = di("s_gd", [DEPTH, NSEQ, 4, 128, 128])
        self.s_gdc = di("s_gdc", [DEPTH, NSEQ, 128, 12, 3])
        self.yp = do("yp", [128, 8, ntok])
        self.ys = do("ys", [128, 8, NSEQ * CH])
        self.o_hg = do("o_hg", [DEPTH, NSEQ + 1, 4, 128, 128])
        self.o_rg = do("o_rg", [DEPTH, NSEQ + 1, 128, 4])
        self.o_rgc = do("o_rgc", [DEPTH, NSEQ + 1, 128, 4, 3])
        self.o_gd = do("o_gd", [DEPTH, NSEQ + 1, 4, 128, 128])
        self.o_gdc = do("o_gdc", [DEPTH, NSEQ + 1, 128, 12, 3])
        self.wb = nc.dram_tensor("wb", [DEPTH, 128, WCOLS_PAD], BF16).ap()
        self.wb_tok = [[Buf("wb%d_%d" % (l, i)) for i in range(WCOLS_PAD // 4096)] for l in range(DEPTH)]
        self.outslot = Buf("outslot")

    def sb(self, name, shape, dt):
        cm = self.nc.sbuf_tensor("sb_" + name, shape, dt)
        t = cm.__enter__()
        self.cms.append(cm)
        return t

    def ps(self, name, shape, dt):
        cm = self.nc.psum_tensor("ps_" + name, shape, dt)
        t = cm.__enter__()
        self.cms.append(cm)
        return t

    def T(self, name, r=(), w=(), **kw):
        self.P.op("tensor", name, kw, r, w)

    def V(self, name, r=(), w=(), **kw):
        self.P.op("vector", name, kw, r, w)

    def A(self, name, r=(), w=(), **kw):
        self.P.op("scalar", name, kw, r, w)

    def G(self, name, r=(), w=(), **kw):
        self.P.op("gpsimd", name, kw, r, w)

    def E(self, name, r=(), w=(), **kw):
        self.rr += 1
        self.P.op("gpsimd" if (self.rr % 3 == 0) else "vector", name, kw, r, w)

    def DMA(self, q, slot, r=(), w=(), final=False, **kw):
        self.P.dma(q, kw, slot, r, w, final)

    def pquad(self):
        i = self.pq_i
        self.pq_i = 1 - i
        return self.quads[i], self.pb_tok[i * 4:(i + 1) * 4]

    def pbank(self):
        i = self.pb_i
        self.pb_i = (i + 1) % 8
        return self.pbanks[i], self.pb_tok[i]

    def alloc(self):
        sb = self.sb
        self.quads = [self.ps("quad%d" % i, [128, 2048], F32) for i in range(2)]
        self.pbanks = [self.quads[i // 4][:, (i % 4) * 512:(i % 4 + 1) * 512] for i in range(8)]
        self.pq_i = 0
        self.pb_tok = [Buf("pb%d" % i) for i in range(8)]
        self.pb_i = 0
        self.cst = sb("cst", [128, DEPTH * NCL], F32); self.t_cst = Buf("cst")
        self.der = sb("der", [128, DEPTH * 16], F32); self.t_der = Buf("der")
        self.nega = sb("nega", [64, DEPTH * 4], F32)
        self.gwb = sb("gwb", [128, 1024], BF16); self.t_gwb = Buf("gwb")
        self.k128 = sb("k128", [128, 768], F32); self.t_k128 = Buf("k128")
        self.k64 = sb("k64", [64, 896], F32); self.t_k64 = Buf("k64")
        self.identb = sb("identb", [128, 128], BF16)
        self.onesb = sb("onesb", [128, 128], BF16)
        self.epst = sb("epst", [128, 1], F32)
        self.t_kc = Buf("kconst")
        self.xT = sb("xT", [128, 8, 512], F32); self.t_x = Buf("xT")
        self.hT = sb("hT", [128, 8, 512], BF16); self.t_h = Buf("hT")
        self.NW = 3
        self.wsl = [sb("wsl%d" % i, [128, 4096], BF16) for i in range(self.NW)]
        self.t_wsl = [Buf("wsl%d" % i) for i in range(self.NW)]
        self.ws_i = 0
        self.GA = sb("GA", [128, 8192], F32)
        self.t_G = [Buf("G%d" % i) for i in range(6)]
        self.GB = sb("GB", [128, 4096 + 64], F32)
        self.BB = sb("BB", [128, 6 * 2048], BF16)
        self.t_B = [Buf("B%d" % i) for i in range(6)]
        self.TK = [sb("TK0", [64, 8, 512], BF16)]
        self.t_TK = [Buf("TK0")]
        self.oT = [sb("oT%d" % i, [128, 4, 512], BF16) for i in range(3)]
        self.t_o = [Buf("oT%d" % i) for i in range(3)]
        self.rstd = sb("rstd", [128, 512], F32); self.t_rstd = Buf("rstd")
        self.fence = sb("fence", [128, 2], F32)
        self.t_aT = [Buf("aT%d" % i) for i in range(32)]
        self.rs4 = sb("rs4", [128, 4, 512], F32); self.t_rs4 = Buf("rs4")
        self.tmpa = [sb("tmpa%d" % i, [128, 512], F32) for i in range(3)]
        self.t_tmpa = [Buf("tmpa%d" % i) for i in range(3)]
        self.tmpb = [sb("tmpb%d" % i, [128, 512], BF16) for i in range(3)]
        self.t_tmpb = [Buf("tmpb%d" % i) for i in range(3)]
        self.SA = [sb("SA%d" % l, [128, 4, 128], F32) for l in range(DEPTH)]
        self.SC = [sb("SC%d" % l, [128, 4, 128], F32) for l in range(DEPTH)]
        self.t_SA = [Buf("SA%d" % l) for l in range(DEPTH)]
        self.t_SC = [Buf("SC%d" % l) for l in range(DEPTH)]
        self.SAb = sb("SAb", [128, 4, 128], BF16); self.t_SAb = Buf("SAb")
        self.SCb = sb("SCb", [128, 4, 128], BF16); self.t_SCb = Buf("SCb")
        self.hB = [sb("hB%d" % l, [128, 4], F32) for l in range(DEPTH)]
        self.t_hB = [Buf("hB%d" % l) for l in range(DEPTH)]
        self.halB = [sb("halB%d" % l, [128, 4, 3], F32) for l in range(DEPTH)]
        self.t_halB = [Buf("halB%d" % l) for l in range(DEPTH)]
        self.halC = [sb("halC%d" % l, [128, 12, 3], F32) for l in range(DEPTH)]
        self.t_halC = [Buf("halC%d" % l) for l in range(DEPTH)]
        def two(name, shape, dt):
            return [sb("%s%d" % (name, i), shape, dt) for i in range(2)], [Buf("%s%d" % (name, i)) for i in range(2)]
        def one(name, shape, dt):
            t = sb(name, shape, dt); b = Buf(name)
            return [t, t], [b, b]
        self.bgA = sb("bgA", [64, 8, 8], F32); self.t_bgA = Buf("bgA")
        self.gcs = sb("gcs", [64, 6, 32], F32); self.t_gcs = Buf("gcs")
        self.egl = sb("egl", [128, 32], F32); self.t_egl = Buf("egl")
        self.vn, self.t_vn = two("vn", [64, 4, 128], BF16)
        self.ident64rep = None

    def Gv(self, i, n=512, halo=0):
        if i < 4:
            base = self.GA[:, i * 2048:(i + 1) * 2048]
            return base.rearrange("p (a b) -> p a b", a=4)
        if i == 4:
            return self.GB[:, 0:2048].rearrange("p (a b) -> p a b", a=4)
        return self.GB[:, 2048:2048 + 4 * 515].rearrange("p (a b) -> p a b", a=4)

    def Bv(self, i):
        return self.BB[:, i * 2048:(i + 1) * 2048].rearrange("p (a b) -> p a b", a=4)

    def setup(self):
        k = self
        k.DMA("sync", k.t_cst, w=[k.t_cst], out=k.cst[:], in_=k.cst_d[:, :])
        k.DMA("sync", k.t_k128, w=[k.t_k128], out=k.k128[:], in_=k.k128_d[:, :])
        k.DMA("sync", k.t_k64, w=[k.t_k64], out=k.k64[:], in_=k.k64_d[:, :])
        k.V("tensor_copy", r=[k.t_k128], w=[k.t_kc], out=k.identb[:], in_=k.k128[:, 0:128])
        k.V("tensor_copy", r=[k.t_k128], w=[k.t_kc], out=k.onesb[:], in_=k.k128[:, 128:256])
        k.V("memset", w=[k.t_kc], ap=k.epst[:], constant=EPS)
        self.ident = k.k128[:, 0:128]
        self.scanmask = k.k128[:, 256:768]
        self.L64 = k.k64[:, 0:64]
        self.ones64 = k.k64[:, 64:192]
        self.strictL = k.k64[:, 192:256].unsqueeze(1).to_broadcast([64, 4, 64])
        self.strictU = k.k64[:, 256:320].unsqueeze(1).to_broadcast([64, 4, 64])
        self.inclU = k.k64[:, 320:384].unsqueeze(1).to_broadcast([64, 4, 64])
        self.identrep = k.k64[:, 384:896].rearrange("p (a b c) -> p a b c", a=4, b=2)
        self.m_strictL = k.k64[:, 192:256]
        self.m_strictU = k.k64[:, 256:320]
        self.m_inclU = k.k64[:, 320:384]
        self.m_ident = k.k64[:, 384:448]
        for l in range(DEPTH):
            c0 = l * NCL
            d0 = l * 16
            cst = k.cst
            if l == 0:
                lr = cst[:, c0 + CL["lbraw"]:c0 + CL["lbraw"] + 16].rearrange("p (h l) -> p h l", h=4)
                ex = k.tmpa[0][:, 0:16].rearrange("p (h l) -> p h l", h=4)
                sm = k.tmpa[0][:, 16:20]
                k.A("activation", r=[k.t_cst], w=[k.t_tmpa[0]], out=ex, in_=lr, func=AF.Exp)
                k.V("tensor_reduce", r=[k.t_tmpa[0]], w=[k.t_tmpa[0]], out=sm, in_=ex, axis=mybir.AxisListType.X, op=ALU.add)
                k.V("reciprocal", r=[k.t_tmpa[0]], w=[k.t_tmpa[0]], out=sm, in_=sm)
                k.V("tensor_tensor", r=[k.t_tmpa[0]], w=[k.t_tmpa[0]], out=ex, in0=ex,
                    in1=sm.unsqueeze(2).to_broadcast([128, 4, 4]), op=ALU.mult)
                k.V("memset", w=[k.t_der], ap=k.der[:, 0:4], constant=0.0)
                for ll in range(1, DEPTH):
                    k.V("tensor_tensor", r=[k.t_tmpa[0], k.t_der], w=[k.t_der], out=k.der[:, ll * 16:ll * 16 + 4],
                        in0=k.der[:, (ll - 1) * 16:(ll - 1) * 16 + 4], in1=ex[:, :, ll], op=ALU.add)
            k.V("tensor_scalar", r=[k.t_der], w=[k.t_der], out=k.der[:, d0 + 4:d0 + 8], in0=k.der[:, d0:d0 + 4],
                scalar1=-1.0, scalar2=1.0, op0=ALU.mult, op1=ALU.add)
            lam = cst[:, c0 + CL["lam"]:c0 + CL["lam"] + 4]
            t1 = k.tmpa[1][:, 0:4]
            k.A("activation", r=[k.t_cst], w=[k.t_tmpa[1]], out=t1, in_=lam, func=AF.Exp, scale=-1.0)
            k.V("tensor_scalar", r=[k.t_tmpa[1]], w=[k.t_tmpa[1]], out=t1, in0=t1, scalar1=1.0, scalar2=None, op0=ALU.add)
            k.A("activation", r=[k.t_tmpa[1]], w=[k.t_tmpa[1]], out=t1, in_=t1, func=AF.Ln)
            k.V("tensor_scalar", r=[k.t_tmpa[1]], w=[k.t_der], out=k.der[:, d0 + 8:d0 + 12], in0=t1, scalar1=-8.0, scalar2=None, op0=ALU.mult)
            k.V("tensor_scalar", r=[k.t_tmpa[1]], w=[k.t_der], out=k.der[:, d0 + 12:d0 + 16], in0=t1, scalar1=-16.0, scalar2=None, op0=ALU.mult)
            k.A("activation", r=[k.t_cst], w=[k.t_tmpa[2]], out=k.tmpa[2][0:64, 0:4], in_=cst[0:64, c0 + CL["alog"]:c0 + CL["alog"] + 4], func=AF.Exp)
            k.V("tensor_scalar", r=[k.t_tmpa[2]], w=[k.t_der], out=k.nega[:, l * 4:l * 4 + 4], in0=k.tmpa[2][0:64, 0:4], scalar1=-1.0, scalar2=None, op0=ALU.mult)
        for l in range(DEPTH):
            k.V("memset", w=[k.t_SA[l]], ap=k.SA[l][:], constant=0.0)
            k.V("memset", w=[k.t_SC[l]], ap=k.SC[l][:], constant=0.0)
            k.V("memset", w=[k.t_hB[l]], ap=k.hB[l][:], constant=0.0)
            k.V("memset", w=[k.t_halB[l]], ap=k.halB[l][:], constant=0.0)
            k.V("memset", w=[k.t_halC[l]], ap=k.halC[l][:], constant=0.0)

    def preconvert(self):
        k = self
        nblk = WCOLS_PAD // 4096
        i = 0
        stg = [(k.GA[:, 0:4096], k.t_G[0], k.t_G[1]), (k.GA[:, 4096:8192], k.t_G[2], k.t_G[3])]
        outb = [(k.BB[:, 0:4096], k.t_B[0]), (k.BB[:, 4096:8192], k.t_B[2])]
        for l in range(DEPTH):
            for b in range(nblk):
                s_ap, s_t, _ = stg[i % 2]
                o_ap, o_t = outb[i % 2]
                k.DMA("sync", s_t, w=[s_t], out=s_ap, in_=k.wf[l, :, b * 4096:(b + 1) * 4096])
                eng = ("vector", "gpsimd", "scalar")[i % 3]
                if eng == "scalar":
                    k.A("activation", r=[s_t], w=[o_t], out=o_ap, in_=s_ap, func=AF.Copy)
                else:
                    k.P.op(eng, "tensor_copy", dict(out=o_ap, in_=s_ap), [s_t], [o_t])
                k.DMA("gpsimd", o_t, r=[o_t], w=[k.wb_tok[l][b]], out=k.wb[l, :, b * 4096:(b + 1) * 4096], in_=o_ap)
                i += 1

    def wload(self, l, name):
        kc, nw, off = WOFF[name]
        i = self.ws_i
        self.ws_i = (i + 1) % self.NW
        n = kc * nw
        toks = self.wb_tok[l][off // 4096:(off + n - 1) // 4096 + 1]
        self.DMA("sync", self.t_wsl[i], r=list(toks), w=[self.t_wsl[i]], out=self.wsl[i][:, 0:n], in_=self.wb[l, :, off:off + n])
        return self.wsl[i][:, 0:n].rearrange("p (a b) -> p a b", a=kc), self.t_wsl[i]

    def sumsq_bcast(self, src_ap_list, toks, nsq, scale, TT):
        k = self
        pb, pt = k.pbank()
        n = len(src_ap_list)
        for i, a in enumerate(src_ap_list):
            k.T("matmul", r=list(toks) + [k.t_kc], w=[pt], out=pb[:, 0:TT], lhsT=k.onesb[:], rhs=a, start=(i == 0), stop=(i == n - 1))
        return pb, pt

    def rstd_from(self, pb, pt, TT, scale, out_ap, out_tok):
        k = self
        k.A("activation", r=[pt, k.t_kc], w=[out_tok], out=out_ap, in_=pb[:, 0:TT], func=AF.Ln, scale=scale, bias=k.epst[:])
        k.A("activation", r=[out_tok], w=[out_tok], out=out_ap, in_=out_ap, func=AF.Exp, scale=-0.5)

    def headsum_rstd(self, sq, tsq, TT, scale):
        k = self
        q, tq = k.pquad()
        for h in range(4):
            k.T("matmul", r=list(tsq) + [k.t_kc], w=tq, out=q[:, h * 512:h * 512 + TT], lhsT=k.onesb[:], rhs=sq[:, h, 0:TT], start=True, stop=True)
        qv = q[:].rearrange("p (h t) -> p h t", h=4)[:, :, 0:TT]
        k.A("activation", r=tq + [k.t_kc], w=[k.t_rs4], out=k.rs4[:, :, 0:TT], in_=qv, func=AF.Ln, scale=scale, bias=k.epst[:])
        k.A("activation", r=[k.t_rs4], w=[k.t_rs4], out=k.rs4[:, :, 0:TT], in_=k.rs4[:, :, 0:TT], func=AF.Exp, scale=-0.5)

    def prenorm(self, gname, l, TT):
        k = self
        sq = k.BB[:, 0:4096].rearrange("p (a b) -> p a b", a=8)
        tsq = [k.t_B[0], k.t_B[1]]
        k.A("activation", r=[k.t_x], w=tsq, out=sq[:, :, 0:TT], in_=k.xT[:, :, 0:TT], func=AF.Square)
        pb, pt = k.sumsq_bcast([sq[:, c, 0:TT] for c in range(8)], tsq, 8, 1.0 / D, TT)
        k.rstd_from(pb, pt, TT, 1.0 / D, k.rstd[:, 0:TT], k.t_rstd)
        g0 = l * NCL + CL[gname]
        for c in range(8):
            k.V("scalar_tensor_tensor", r=[k.t_x, k.t_rstd, k.t_cst], w=[k.t_h], out=k.hT[:, c, 0:TT], in0=k.xT[:, c, 0:TT],
                scalar=k.cst[:, g0 + c:g0 + c + 1], in1=k.rstd[:, 0:TT], op0=ALU.mult, op1=ALU.mult)

    def postnorm_add(self, gname, l, TT):
        k = self
        yT = k.GA[:, 0:4096].rearrange("p (a b) -> p a b", a=8)
        ty = [k.t_G[0], k.t_G[1]]
        sq = k.BB[:, 0:4096].rearrange("p (a b) -> p a b", a=8)
        tsq = [k.t_B[0], k.t_B[1]]
        k.A("activation", r=ty, w=tsq, out=sq[:, :, 0:TT], in_=yT[:, :, 0:TT], func=AF.Square)
        pb, pt = k.sumsq_bcast([sq[:, c, 0:TT] for c in range(8)], tsq, 8, 1.0 / D, TT)
        k.rstd_from(pb, pt, TT, 1.0 / D, k.rstd[:, 0:TT], k.t_rstd)
        g0 = l * NCL + CL[gname]
        for c in range(8):
            k.V("scalar_tensor_tensor", r=ty + [k.t_rstd, k.t_cst], w=ty, out=yT[:, c, 0:TT], in0=yT[:, c, 0:TT],
                scalar=k.cst[:, g0 + c:g0 + c + 1], in1=k.rstd[:, 0:TT], op0=ALU.mult, op1=ALU.mult)
        k.E("tensor_tensor", r=ty + [k.t_x], w=[k.t_x], out=k.xT[:, :, 0:TT], in0=k.xT[:, :, 0:TT], in1=yT[:, :, 0:TT], op=ALU.add)

    def headnorm_gate(self, src, t_src, gate, t_gate, ncol, out, t_out, TT, DH=128):
        k = self
        sq = k.BB[:, 0:2048].rearrange("p (a b) -> p a b", a=4)
        tsq = [k.t_B[0]]
        k.A("activation", r=[t_src], w=tsq, out=sq[:, :, 0:TT], in_=src[:, :, 0:TT], func=AF.Square)
        k.headsum_rstd(sq, tsq, TT, 1.0 / DH)
        k.V("scalar_tensor_tensor", r=[t_src, k.t_rs4, k.t_cst], w=[t_src], out=src[:, :, 0:TT], in0=src[:, :, 0:TT],
            scalar=k.cst[:, ncol:ncol + 1], in1=k.rs4[:, :, 0:TT], op0=ALU.mult, op1=ALU.mult)
        k.E("tensor_tensor", r=[t_src, t_gate], w=[t_out], out=out[:, :, 0:TT], in0=src[:, :, 0:TT], in1=gate[:, :, 0:TT], op=ALU.mult)

    def proj_fm(self, wt, t_w, ncol, TT, evac):
        k = self
        banks = [k.pbank() for _ in range(ncol)]
        for kc in range(8):
            for j in range(ncol):
                pb, pt = banks[j]
                k.T("matmul", r=[k.t_h, t_w], w=[pt], out=pb[:, 0:TT], lhsT=wt[:, kc, j * 128:(j + 1) * 128], rhs=k.hT[:, kc, 0:TT],
                    start=(kc == 0), stop=(kc == 7))
        for j in range(ncol):
            pb, pt = banks[j]
            evac(j, pb[:, 0:TT], pt)

    def proj_tm(self, wt, t_w, n, c, evac):
        k = self
        pb, pt = k.pbank()
        for kc in range(8):
            k.T("matmul", r=[k.t_h, t_w], w=[pt], out=pb[0:64, 0:n], lhsT=k.hT[:, kc, c * CH:(c + 1) * CH], rhs=wt[:, kc, 0:n],
                start=(kc == 0), stop=(kc == 7))
        evac(pb[0:64, 0:n], pt)

    def block(self, ti, l, TT, sample):
        k = self
        nch = TT // CH
        c0 = l * NCL
        d0 = l * 16
        cst = k.cst
        G = [k.Gv(i) for i in range(6)]
        tG = k.t_G
        B = [k.Bv(i) for i in range(6)]
        tB = k.t_B
        first = (ti == 0 and not sample)

        k.prenorm("g_pre", l, TT)

        qa, t_qa = G[0], tG[0]
        lf, t_lf = G[1], tG[1]
        ka, t_ka = G[2], tG[2]
        gA, t_gA = B[2], tB[2]
        vtok, t_vtok = k.TK[0], k.t_TK[0]
        wt, tw = k.wload(l, "aq")
        k.proj_fm(wt, tw, 4, TT, lambda j, p, pt: k.A("activation", r=[pt], w=[t_qa], out=qa[:, j, 0:TT], in_=p, func=AF.Silu))
        wt, tw = k.wload(l, "af")
        def ev_f(j, p, pt):
            k.A("activation", r=[pt], w=[t_ka], out=ka[:, j, 0:TT], in_=p, func=AF.Sigmoid)
            k.V("tensor_scalar", r=[t_ka, k.t_der], w=[t_lf], out=lf[:, j, 0:TT], in0=ka[:, j, 0:TT],
                scalar1=k.der[:, d0 + 4 + j:d0 + 5 + j], scalar2=k.der[:, d0 + j:d0 + j + 1], op0=ALU.mult, op1=ALU.add)
            k.V("tensor_scalar", r=[t_lf], w=[t_ka], out=ka[:, j, 0:TT], in0=lf[:, j, 0:TT], scalar1=-1.0, scalar2=1.0, op0=ALU.mult, op1=ALU.add)
            k.A("activation", r=[t_lf], w=[t_lf], out=lf[:, j, 0:TT], in_=lf[:, j, 0:TT], func=AF.Ln)
        k.proj_fm(wt, tw, 4, TT, ev_f)
        wt, tw = k.wload(l, "ai")
        for c in range(nch):
            k.proj_tm(wt, tw, 512, c, lambda p, pt, c=c: k.V("tensor_copy", r=[pt], w=[t_vtok], out=vtok[:, c, :], in_=p))
        wt, tw = k.wload(l, "ag")
        k.proj_fm(wt, tw, 4, TT, lambda j, p, pt: k.A("activation", r=[pt], w=[t_gA], out=gA[:, j, 0:TT], in_=p, func=AF.Silu))
        bcs, t_bcs = G[3], tG[3]
        for h in range(4):
            k.V("tensor_tensor_scan", r=[t_lf, k.t_k128], w=[t_bcs], out=bcs[:, h, 0:TT], data0=k.scanmask[:, 0:TT], data1=lf[:, h, 0:TT],
                initial=0.0, op0=ALU.mult, op1=ALU.add)
        eb, t_eb = G[4], tG[4]
        qt, t_qt = B[3], tB[3]
        kt, t_kt = B[4], tB[4]
        kh, t_kh = B[5], tB[5]
        k.A("activation", r=[t_bcs], w=[t_eb], out=eb[:, :, 0:TT], in_=bcs[:, :, 0:TT], func=AF.Exp)
        k.E("tensor_tensor", r=[t_qa, t_eb], w=[t_qt], out=qt[:, :, 0:TT], in0=qa[:, :, 0:TT], in1=eb[:, :, 0:TT], op=ALU.mult)
        k.A("activation", r=[t_bcs, t_qt], w=[t_eb], out=eb[:, :, 0:TT], in_=bcs[:, :, 0:TT], func=AF.Exp, scale=-1.0)
        k.E("tensor_tensor", r=[t_ka, t_eb], w=[t_kt], out=kt[:, :, 0:TT], in0=ka[:, :, 0:TT], in1=eb[:, :, 0:TT], op=ALU.mult)
        ebl = k.rs4[:, :, 0:nch]
        k.A("activation", r=[t_bcs], w=[k.t_rs4], out=ebl, in_=bcs[:, :, 0:TT].rearrange("p h (c t) -> p h c t", t=CH)[:, :, :, CH - 1], func=AF.Exp)
        k.V("tensor_tensor", r=[k.t_rs4, t_kt], w=[t_eb], out=eb[:, :, 0:TT].rearrange("p h (c t) -> p h c t", t=CH),
            in0=eb[:, :, 0:TT].rearrange("p h (c t) -> p h c t", t=CH), in1=ebl.unsqueeze(3).to_broadcast([128, 4, nch, CH]), op=ALU.mult)
        k.E("tensor_tensor", r=[t_ka, t_eb], w=[t_kh], out=kh[:, :, 0:TT], in0=ka[:, :, 0:TT], in1=eb[:, :, 0:TT], op=ALU.mult)
        oA, t_oA = G[0], tG[0]
        NB = nch * 256
        khtok = k.GB[0:64, 0:2048].bitcast(BF16).rearrange("p (c x) -> p c x", c=8)
        t_khtok = tG[4]
        q, tq = k.pquad()
        qv = q[:].bitcast(BF16)
        for c in range(nch):
            for h in range(4):
                k.T("transpose", r=[t_kh, k.t_kc], w=tq, out=qv[0:64, c * 512 + h * 128:c * 512 + (h + 1) * 128], in_=kh[:, h, c * CH:(c + 1) * CH], identity=k.identb[:])
        k.V("tensor_copy", r=tq + [t_eb], w=[t_khtok], out=khtok[:, 0:nch, :], in_=qv[0:64, 0:nch * 512].rearrange("p (c x) -> p c x", c=nch))
        atT = k.BB[0:64, 2048:4096].rearrange("p (c h s) -> p c h s", c=8, h=4)
        t_atT = tB[1]
        q, tq = k.pquad()
        for c in range(nch):
            for h in range(4):
                k.T("matmul", r=[t_kt, t_qt], w=tq, out=q[0:64, (c * 4 + h) * 64:(c * 4 + h + 1) * 64], lhsT=kt[:, h, c * CH:(c + 1) * CH], rhs=qt[:, h, c * CH:(c + 1) * CH], start=True, stop=True)
        k.V("tensor_tensor", r=tq + [k.t_k64], w=[t_atT], out=atT[:, 0:nch].rearrange("p c h s -> p (c h) s"),
            in0=q[0:64, 0:NB].rearrange("p (w s) -> p w s", s=64), in1=k.m_inclU.unsqueeze(1).to_broadcast([64, nch * 4, 64]), op=ALU.mult)
        genB = k.mixB(l, TT, sample, first)
        for c in range(nch):
            cs = slice(c * CH, (c + 1) * CH)
            for _ in range(3):
                next(genB, None)
            if sample:
                k.load_state_A(l, c)
            k.V("tensor_copy", r=[k.t_SA[l]], w=[k.t_SAb], out=k.SAb[:], in_=k.SA[l][:])
            pb, pt = k.pbank()
            for h in range(4):
                k.T("matmul", r=[t_vtok, t_atT], w=[pt], out=pb[:, h * 64:(h + 1) * 64], lhsT=vtok[:, c, h * 128:(h + 1) * 128], rhs=atT[:, c, h, :],
                    start=True, stop=False)
                k.T("matmul", r=[k.t_SAb, t_qt], w=[pt], out=pb[:, h * 64:(h + 1) * 64], lhsT=k.SAb[:, h, :], rhs=qt[:, h, cs], start=False, stop=True)
            k.A("activation", r=[pt], w=[t_oA], out=oA[:, :, cs], in_=pb[:, 0:256].rearrange("p (a b) -> p a b", a=4), func=AF.Copy)
            pb, pt = k.pbank()
            for h in range(4):
                k.T("matmul", r=[t_khtok, t_vtok], w=[pt], out=pb[:, h * 128:(h + 1) * 128], lhsT=khtok[:, c, h * 128:(h + 1) * 128],
                    rhs=vtok[:, c, h * 128:(h + 1) * 128], start=True, stop=True)
            k.V("tensor_tensor", r=[k.t_rs4, k.t_SA[l]], w=[k.t_SA[l]], out=k.SA[l][:], in0=k.SA[l][:],
                in1=k.rs4[:, :, c:c + 1].to_broadcast([128, 4, 128]), op=ALU.mult)
            k.V("tensor_tensor", r=[pt, k.t_SA[l]], w=[k.t_SA[l]], out=k.SA[l][:], in0=k.SA[l][:],
                in1=pb[:, 0:512].rearrange("p (a b) -> p a b", a=4), op=ALU.add)
            if sample:
                k.store_state_A(l, c)
        for _ in genB:
            pass
        k.headnorm_gate(oA, t_oA, gA, t_gA, c0 + CL["a_norm"], k.oT[0], k.t_o[0], TT)

        t_zx = tG[5]
        zxv = k.GB[:, 2048:2048 + 4 * 515].rearrange("p (a b) -> p a b", a=4)
        qT, t_qT = B[3], tB[3]
        kT, t_kT = B[4], tB[4]
        vT, t_vT = B[5], tB[5]
        gC, t_gC = B[2], tB[2]
        for gi, (wn, dst, t_dst) in enumerate((("cq", qT, t_qT), ("ck", kT, t_kT), ("cv", vT, t_vT))):
            wt, tw = k.wload(l, wn)
            hal = k.halC[l][:, gi * 4:(gi + 1) * 4, :]
            if not sample:
                k.V("tensor_copy", r=[k.t_halC[l]], w=[t_zx], out=zxv[:, :, 0:3], in_=hal)
            k.proj_fm(wt, tw, 4, TT, lambda j, p, pt: k.A("activation", r=[pt], w=[t_zx], out=zxv[:, j, 3:3 + TT], in_=p, func=AF.Copy))
            cv, t_cv = G[0], tG[0]
            if sample:
                for c in range(nch):
                    if gi == 0:
                        k.DMA("gpsimd", k.t_halCs, w=[k.t_halCs], out=k.halCs[c][:], in_=k.s_gdc[l, c])
                    k.conv(zxv, t_zx, k.halCs[c][:, gi * 4:(gi + 1) * 4, :], k.t_halCs, cv, t_cv, 4, c0 + CL["ccw"] + gi * 4, None, c * CH, CH, seq_halo=True, wstride=12)
                    k.DMA("gpsimd", k.outslot, r=[t_zx], out=k.o_gdc[l, 1 + c, :, gi * 4:(gi + 1) * 4, :],
                          in_=zxv[:, :, 3 + (c + 1) * CH - 3:3 + (c + 1) * CH], final=True)
            else:
                k.conv(zxv, t_zx, None, None, cv, t_cv, 4, c0 + CL["ccw"] + gi * 4, None, 0, TT, seq_halo=False, wstride=12)
                k.V("tensor_copy", r=[t_zx], w=[k.t_halC[l]], out=hal, in_=zxv[:, :, TT:TT + 3])
            k.A("activation", r=[t_cv], w=[t_cv], out=cv[:, :, 0:TT], in_=cv[:, :, 0:TT], func=AF.Silu)
            if gi < 2:
                sq = k.BB[:, 0:2048].rearrange("p (a b) -> p a b", a=4)
                tsq = [k.t_B[0]]
                k.A("activation", r=[t_cv], w=tsq, out=sq[:, :, 0:TT], in_=cv[:, :, 0:TT], func=AF.Square)
                k.headsum_rstd(sq, tsq, TT, 1.0)
                if gi == 0:
                    k.V("scalar_tensor_tensor", r=[t_cv, k.t_rs4], w=[t_dst], out=dst[:, :, 0:TT], in0=cv[:, :, 0:TT], scalar=128.0 ** -0.5,
                        in1=k.rs4[:, :, 0:TT], op0=ALU.mult, op1=ALU.mult)
                else:
                    k.V("tensor_tensor", r=[t_cv, k.t_rs4], w=[t_dst], out=dst[:, :, 0:TT], in0=cv[:, :, 0:TT], in1=k.rs4[:, :, 0:TT], op=ALU.mult)
            else:
                k.E("tensor_copy", r=[t_cv], w=[t_dst], out=dst[:, :, 0:TT], in_=cv[:, :, 0:TT])
        wt, tw = k.wload(l, "cg")
        k.proj_fm(wt, tw, 4, TT, lambda j, p, pt: k.A("activation", r=[pt], w=[t_gC], out=gC[:, j, 0:TT], in_=p, func=AF.Silu))
        wba, twba = k.wload(l, "ba")
        oC, t_oC = G[1], tG[1]
        NB = nch * 256
        W = nch * 4
        def gf(i):
            if i < 4:
                return k.GA[0:64, i * 2048:i * 2048 + NB]
            if i == 4:
                return k.GB[0:64, 0:NB]
            return k.GB[0:64, 2048:2048 + NB]
        def w3(ap):
            return ap.rearrange("p (w s) -> p w s", s=64)
        identW = k.m_ident.unsqueeze(1).to_broadcast([64, W, 64])
        pb, pt = k.pbank()
        for c in range(nch):
            for kc in range(8):
                k.T("matmul", r=[k.t_h, twba], w=[pt], out=pb[0:64, c * 8:(c + 1) * 8], lhsT=k.hT[:, kc, c * CH:(c + 1) * CH], rhs=wba[:, kc, 0:8],
                    start=(kc == 0), stop=(kc == 7))
        bgA, t_bgA = k.bgA, k.t_bgA
        p3 = pb[0:64, 0:nch * 8].rearrange("p (c x) -> p c x", x=8)
        b3 = bgA[:, 0:nch, :]
        k.A("activation", r=[pt], w=[t_bgA], out=b3[:, :, 0:4], in_=p3[:, :, 0:4], func=AF.Sigmoid)
        k.V("tensor_tensor", r=[pt, k.t_cst], w=[t_bgA], out=b3[:, :, 4:8], in0=p3[:, :, 4:8],
            in1=cst[0:64, c0 + CL["dtb"]:c0 + CL["dtb"] + 4].unsqueeze(1).to_broadcast([64, nch, 4]), op=ALU.add)
        k.A("activation", r=[t_bgA], w=[t_bgA], out=b3[:, :, 4:8], in_=b3[:, :, 4:8], func=AF.Exp)
        k.V("tensor_scalar", r=[t_bgA], w=[t_bgA], out=b3[:, :, 4:8], in0=b3[:, :, 4:8], scalar1=1.0, scalar2=None, op0=ALU.add)
        k.A("activation", r=[t_bgA], w=[t_bgA], out=b3[:, :, 4:8], in_=b3[:, :, 4:8], func=AF.Ln)
        k.V("tensor_tensor", r=[t_bgA, k.t_der], w=[t_bgA], out=b3[:, :, 4:8], in0=b3[:, :, 4:8],
            in1=k.nega[:, l * 4:l * 4 + 4].unsqueeze(1).to_broadcast([64, nch, 4]), op=ALU.mult)
        pb, pt = k.pbank()
        k.T("matmul", r=[t_bgA, k.t_k64], w=[pt], out=pb[0:64, 0:W].rearrange("p (c h) -> p c h", h=4), lhsT=k.L64, rhs=b3[:, :, 4:8], start=True, stop=True)
        gcs, t_gcs = k.gcs, k.t_gcs
        k.V("tensor_copy", r=[pt], w=[t_gcs], out=gcs[:, 0, 0:W], in_=pb[0:64, 0:W])
        k.V("tensor_copy", r=[t_bgA], w=[t_gcs], out=gcs[:, 1, 0:W].rearrange("p (c h) -> p c h", h=4), in_=b3[:, :, 0:4])
        gcv = gcs[:, 0, 0:W]
        btv = gcs[:, 1, 0:W]
        ktok, t_ktok = k.TK[0], k.t_TK[0]
        vtk = k.GB[0:64, 2048:4096].bitcast(BF16).rearrange("p (c x) -> p c x", c=8)
        t_vtk = tG[5]
        for src, t_src, dst, t_dst in ((kT, t_kT, ktok, t_ktok), (vT, t_vT, vtk, t_vtk)):
            q, tq = k.pquad()
            qv = q[:].bitcast(BF16)
            for c in range(nch):
                for h in range(4):
                    k.T("transpose", r=[t_src, k.t_kc], w=tq, out=qv[0:64, c * 512 + h * 128:c * 512 + (h + 1) * 128], in_=src[:, h, c * CH:(c + 1) * CH], identity=k.identb[:])
            k.V("tensor_copy", r=tq, w=[t_dst], out=dst[:, 0:nch, :], in_=qv[0:64, 0:nch * 512].rearrange("p (c x) -> p c x", c=nch))
        dgG = w3(gf(3))
        k.V("tensor_tensor", r=[t_gcs, k.t_k64], w=[tG[3]], out=dgG, in0=identW, in1=gcv.unsqueeze(2).to_broadcast([64, W, 64]), op=ALU.mult)
        qg, tqg = k.pquad()
        for b in range(NB // 512):
            k.T("matmul", r=[tG[3], k.t_k64], w=tqg, out=qg[:, b * 512:(b + 1) * 512], lhsT=k.ones64, rhs=gf(3)[:, b * 512:(b + 1) * 512], start=True, stop=True)
        egr = k.GA[:, 2 * 2048:2 * 2048 + NB]
        k.A("activation", r=tqg, w=[tG[2]], out=egr, in_=qg[:, 0:NB], func=AF.Exp)
        qtl = k.BB[:, 5 * 2048:5 * 2048 + NB].rearrange("p (c h s) -> p c h s", h=4, s=64)
        t_qtl = tB[5]
        for h in range(4):
            k.E("tensor_tensor", r=[t_qT, tG[2], t_vtk], w=[t_qtl], out=qtl[:, :, h, :], in0=qT[:, h, 0:TT].rearrange("p (c s) -> p c s", s=64),
                in1=egr.rearrange("p (c h s) -> p c h s", h=4, s=64)[:, :, h, :], op=ALU.mult)
        egl, t_egl = k.egl, k.t_egl
        k.V("tensor_copy", r=[tG[2]], w=[t_egl], out=egl[:, 0:W], in_=egr.rearrange("p (w s) -> p w s", s=64)[:, :, CH - 1])
        Em = w3(gf(0))
        k.V("tensor_tensor", r=tqg + [t_gcs], w=[tG[0]], out=Em, in0=w3(qg[0:64, 0:NB]), in1=gcv.unsqueeze(2).to_broadcast([64, W, 64]), op=ALU.subtract)
        k.V("tensor_copy", r=tqg, w=[t_gcs], out=gcs[:, 5, 0:W], in_=w3(qg[0:64, 0:NB])[:, :, CH - 1])
        k.V("tensor_tensor", r=[t_gcs], w=[t_gcs], out=gcs[:, 4, 0:W], in0=gcs[:, 5, 0:W], in1=gcv, op=ALU.subtract)
        k.A("activation", r=[t_gcs], w=[t_gcs], out=gcs[:, 4, 0:W], in_=gcs[:, 4, 0:W], func=AF.Exp)
        k.A("activation", r=[t_gcs], w=[t_gcs], out=gcs[:, 2, 0:W], in_=gcv, func=AF.Exp)
        k.V("tensor_tensor", r=[t_gcs], w=[t_gcs], out=gcs[:, 3, 0:W], in0=gcs[:, 2, 0:W], in1=btv, op=ALU.mult)
        GmU = w3(gf(2)); GmL = Em; GmI = w3(gf(3))
        k.V("tensor_scalar", r=[tG[0], t_qtl, t_egl], w=[tG[2]], out=GmU, in0=Em, scalar1=0.0, scalar2=None, op0=ALU.min)
        k.A("activation", r=[tG[2]], w=[tG[2]], out=GmU, in_=GmU, func=AF.Exp)
        k.V("tensor_scalar", r=[tG[0]], w=[tG[0]], out=GmL, in0=Em, scalar1=0.0, scalar2=None, op0=ALU.max)
        k.A("activation", r=[tG[0]], w=[tG[0]], out=GmL, in_=GmL, func=AF.Exp, scale=-1.0)
        k.E("tensor_tensor", r=[tG[2], k.t_k64], w=[tG[3]], out=GmI, in0=GmU, in1=k.m_inclU.unsqueeze(1).to_broadcast([64, W, 64]), op=ALU.mult)
        k.E("tensor_tensor", r=[tG[2], k.t_k64], w=[tG[2]], out=GmU, in0=GmU, in1=k.m_strictU.unsqueeze(1).to_broadcast([64, W, 64]), op=ALU.mult)
        k.E("tensor_tensor", r=[tG[0], k.t_k64], w=[tG[0]], out=GmL, in0=GmL, in1=k.m_strictL.unsqueeze(1).to_broadcast([64, W, 64]), op=ALU.mult)
        qk_, tqk = k.pquad()
        for c in range(nch):
            for h in range(4):
                w_ = c * 4 + h
                k.T("matmul", r=[t_kT], w=tqk, out=qk_[0:64, w_ * 64:(w_ + 1) * 64], lhsT=kT[:, h, c * CH:(c + 1) * CH], rhs=kT[:, h, c * CH:(c + 1) * CH], start=True, stop=True)
        Pn = w3(gf(0)); Pt = w3(gf(2))
        k.V("tensor_tensor", r=tqk + [tG[0]], w=[tG[0]], out=Pn, in0=w3(qk_[0:64, 0:NB]), in1=GmL, op=ALU.mult)
        k.V("scalar_tensor_tensor", r=[tG[0], t_gcs], w=[tG[0]], out=Pn, in0=Pn, scalar=-1.0, in1=btv.unsqueeze(2).to_broadcast([64, W, 64]), op0=ALU.mult, op1=ALU.mult)
        k.V("tensor_tensor", r=tqk + [tG[2]], w=[tG[2]], out=Pt, in0=w3(qk_[0:64, 0:NB]), in1=GmU, op=ALU.mult)
        qq, tqq = k.pquad()
        for c in range(nch):
            for h in range(4):
                w_ = c * 4 + h
                k.T("matmul", r=[t_kT, t_qT], w=tqq, out=qq[0:64, w_ * 64:(w_ + 1) * 64], lhsT=kT[:, h, c * CH:(c + 1) * CH], rhs=qT[:, h, c * CH:(c + 1) * CH], start=True, stop=True)
        QKm = k.BB[0:64, 0:NB].rearrange("p (c h s) -> p c h s", h=4, s=64)
        t_QKm = tB[0]
        k.V("tensor_tensor", r=tqq + [tG[3]], w=[t_QKm], out=QKm.rearrange("p c h s -> p (c h) s"), in0=w3(qq[0:64, 0:NB]), in1=GmI, op=ALU.mult)
        dgB = w3(gf(4))
        k.V("tensor_tensor", r=[t_gcs, k.t_k64], w=[tG[4]], out=dgB, in0=identW, in1=btv.unsqueeze(2).to_broadcast([64, W, 64]), op=ALU.mult)
        qb, tqb = k.pquad()
        for b in range(NB // 512):
            k.T("matmul", r=[tG[4], k.t_k64], w=tqb, out=qb[0:64, b * 512:(b + 1) * 512], lhsT=k.ones64[:, 0:64], rhs=gf(4)[:, b * 512:(b + 1) * 512], start=True, stop=True)
        k.V("scalar_tensor_tensor", r=tqb + [tG[2]], w=[tG[2]], out=Pt, in0=Pt, scalar=-1.0, in1=w3(qb[0:64, 0:NB]), op0=ALU.mult, op1=ALU.mult)
        Xt = w3(gf(3)); IP = w3(gf(4))
        k.V("tensor_tensor", r=[tG[2], k.t_k64, t_QKm], w=[tG[3]], out=Xt, in0=Pt, in1=identW, op=ALU.add)
        for lev in range(1, 6):
            qn, tqn = k.pquad()
            for w_ in range(W):
                k.T("matmul", r=[tG[0], tG[2]], w=tqn, out=qn[0:64, w_ * 64:(w_ + 1) * 64], lhsT=gf(2)[:, w_ * 64:(w_ + 1) * 64], rhs=gf(0)[:, w_ * 64:(w_ + 1) * 64], start=True, stop=True)
            if lev < 5:
                qt_, tqt = k.pquad()
                for w_ in range(W):
                    k.T("matmul", r=[tG[0], tG[2]], w=tqt, out=qt_[0:64, w_ * 64:(w_ + 1) * 64], lhsT=gf(0)[:, w_ * 64:(w_ + 1) * 64], rhs=gf(2)[:, w_ * 64:(w_ + 1) * 64], start=True, stop=True)
            k.V("tensor_tensor", r=tqn + [k.t_k64], w=[tG[4]], out=IP, in0=w3(qn[0:64, 0:NB]), in1=identW, op=ALU.add)
            if lev < 5:
                k.A("activation", r=tqn, w=[tG[0]], out=gf(0), in_=qn[0:64, 0:NB], func=AF.Copy)
                k.A("activation", r=tqt, w=[tG[2]], out=gf(2), in_=qt_[0:64, 0:NB], func=AF.Copy)
            qx, tqx = k.pquad()
            for w_ in range(W):
                k.T("matmul", r=[tG[4], tG[3]], w=tqx, out=qx[0:64, w_ * 64:(w_ + 1) * 64], lhsT=gf(4)[:, w_ * 64:(w_ + 1) * 64], rhs=gf(3)[:, w_ * 64:(w_ + 1) * 64], start=True, stop=True)
            k.V("tensor_copy", r=tqx, w=[tG[3]], out=gf(3), in_=qx[0:64, 0:NB])
        Xtb = k.BB[0:64, 3 * 2048:3 * 2048 + NB]
        t_Xtb = tB[3]
        k.V("tensor_copy", r=[tG[3], t_qtl], w=[t_Xtb], out=Xtb, in_=gf(3))
        k.E("tensor_tensor", r=[t_vtk, t_gcs], w=[t_vtk], out=vtk[:, 0:nch, :].rearrange("p c (h v) -> p (c h) v", h=4),
            in0=vtk[:, 0:nch, :].rearrange("p c (h v) -> p (c h) v", h=4), in1=btv.unsqueeze(2).to_broadcast([64, W, 128]), op=ALU.mult)
        kbg = k.GA[0:64, 0:2048].bitcast(BF16).rearrange("p (c x) -> p c x", c=8)
        khc = k.GA[0:64, 4096:6144].bitcast(BF16).rearrange("p (c x) -> p c x", c=8)
        k.E("tensor_tensor", r=[t_ktok, t_gcs, tG[0]], w=[tG[0]], out=kbg[:, 0:nch, :].rearrange("p c (h v) -> p (c h) v", h=4),
            in0=ktok[:, 0:nch, :].rearrange("p c (h v) -> p (c h) v", h=4), in1=gcs[:, 3, 0:W].unsqueeze(2).to_broadcast([64, W, 128]), op=ALU.mult)
        k.E("tensor_tensor", r=[t_ktok, t_gcs, tG[2]], w=[tG[2]], out=khc[:, 0:nch, :].rearrange("p c (h v) -> p (c h) v", h=4),
            in0=ktok[:, 0:nch, :].rearrange("p c (h v) -> p (c h) v", h=4), in1=gcs[:, 4, 0:W].unsqueeze(2).to_broadcast([64, W, 128]), op=ALU.mult)
        uu = k.GB[0:64, 0:2048].bitcast(BF16).rearrange("p (c x) -> p c x", c=8)
        t_uu = tG[4]
        for half in range((nch + 3) // 4):
            qu, tqu = k.pquad()
            ncl = min(4, nch - half * 4)
            for cl in range(ncl):
                c = half * 4 + cl
                for h in range(4):
                    k.T("matmul", r=[t_Xtb, t_vtk], w=tqu, out=qu[0:64, cl * 512 + h * 128:cl * 512 + (h + 1) * 128], lhsT=Xtb[:, (c * 4 + h) * 64:(c * 4 + h + 1) * 64],
                        rhs=vtk[:, c, h * 128:(h + 1) * 128], start=True, stop=True)
            k.A("activation", r=tqu, w=[t_uu], out=uu[:, half * 4:half * 4 + ncl, :], in_=qu[0:64, 0:ncl * 512].rearrange("p (c x) -> p c x", x=512), func=AF.Copy)
        wTc = k.BB[:, 2048:2048 + NB].rearrange("p (c h s) -> p c h s", h=4, s=64)
        t_wTc = tB[1]
        qw, tqw = k.pquad()
        for c in range(nch):
            for h in range(4):
                w_ = c * 4 + h
                k.T("matmul", r=[t_Xtb, tG[0]], w=tqw, out=qw[:, w_ * 64:(w_ + 1) * 64], lhsT=kbg[:, c, h * 128:(h + 1) * 128], rhs=Xtb[:, w_ * 64:(w_ + 1) * 64], start=True, stop=True)
        k.A("activation", r=tqw, w=[t_wTc], out=wTc.rearrange("p c h s -> p (c h s)"), in_=qw[:, 0:NB], func=AF.Copy)
        for c in range(nch):
            cs = slice(c * CH, (c + 1) * CH)
            pi = c % 2
            if sample:
                k.load_state_C(l, c)
            k.A("activation", r=[k.t_SC[l]], w=[k.t_SCb], out=k.SCb[:], in_=k.SC[l][:], func=AF.Copy)
            k.G("tensor_tensor", r=[t_egl, k.t_SC[l]], w=[k.t_SC[l]], out=k.SC[l][:], in0=k.SC[l][:],
                in1=egl[:, c * 4:(c + 1) * 4].unsqueeze(2).to_broadcast([128, 4, 128]), op=ALU.mult)
            pv, ptv = k.pbank()
            for h in range(4):
                k.T("matmul", r=[t_wTc, k.t_SCb], w=[ptv], out=pv[0:64, h * 128:(h + 1) * 128], lhsT=wTc[:, c, h, :], rhs=k.SCb[:, h, :], start=True, stop=True)
            vn, t_vn = k.vn[pi], k.t_vn[pi]
            k.V("tensor_tensor", r=[ptv, t_uu], w=[t_vn], out=vn[:], in0=uu[:, c, :].rearrange("p (a b) -> p a b", a=4),
                in1=pv[0:64, 0:512].rearrange("p (a b) -> p a b", a=4), op=ALU.subtract)
            po, pto = k.pbank()
            for h in range(4):
                k.T("matmul", r=[k.t_SCb, t_qtl], w=[pto], out=po[:, h * 64:(h + 1) * 64], lhsT=k.SCb[:, h, :], rhs=qtl[:, c, h, :], start=True, stop=False)
                k.T("matmul", r=[t_vn, t_QKm], w=[pto], out=po[:, h * 64:(h + 1) * 64], lhsT=vn[:, h, :], rhs=QKm[:, c, h, :], start=False, stop=True)
            k.A("activation", r=[pto], w=[t_oC], out=oC[:, :, cs], in_=po[:, 0:256].rearrange("p (a b) -> p a b", a=4), func=AF.Copy)
            psn, ptsn = k.pbank()
            for h in range(4):
                k.T("matmul", r=[tG[2], t_vn], w=[ptsn], out=psn[:, h * 128:(h + 1) * 128], lhsT=khc[:, c, h * 128:(h + 1) * 128], rhs=vn[:, h, :], start=True, stop=True)
            k.V("tensor_tensor", r=[ptsn, k.t_SC[l]], w=[k.t_SC[l]], out=k.SC[l][:], in0=k.SC[l][:],
                in1=psn[:, 0:512].rearrange("p (a b) -> p a b", a=4), op=ALU.add)
            if sample:
                k.store_state_C(l, c)
        k.headnorm_gate(oC, t_oC, gC, t_gC, c0 + CL["c_norm"], k.oT[2], k.t_o[2], TT)

        mbT = k.BB[:, 2 * 2048:4 * 2048].rearrange("p (a b) -> p a b", a=8)
        t_mb = [tB[2], tB[3]]
        for j in range(8):
            wg, twg = k.wload(l, "mg%d" % j)
            wr, twr = k.wload(l, "br%d" % j)
            acc, t_acc = k.tmpa[j % 2], k.t_tmpa[j % 2]
            gb = [k.pbank() for _ in range(3)]
            wbk = [k.pbank() for _ in range(3)]
            for kc in range(8):
                for b in range(3):
                    pgt, ptgt = gb[b]
                    k.T("matmul", r=[k.t_h, twg], w=[ptgt], out=pgt[:, 0:TT], lhsT=wg[:, kc, b * 128:(b + 1) * 128], rhs=k.hT[:, kc, 0:TT], start=(kc == 0), stop=(kc == 7))
            for kc in range(4):
                for b in range(3):
                    pwd, ptwd = wbk[b]
                    k.T("matmul", r=[k.t_o[b], twr], w=[ptwd], out=pwd[:, 0:TT], lhsT=wr[:, b * 4 + kc, :], rhs=k.oT[b][:, kc, 0:TT], start=(kc == 0), stop=(kc == 3))
            for b in range(3):
                pgt, ptgt = gb[b]
                pwd, ptwd = wbk[b]
                sg, t_sg = k.tmpb[b], k.t_tmpb[b]
                k.A("activation", r=[ptgt], w=[t_sg], out=sg[:, 0:TT], in_=pgt[:, 0:TT], func=AF.Sigmoid)
                if b == 0:
                    k.V("tensor_tensor", r=[ptwd, t_sg], w=[t_acc], out=acc[:, 0:TT], in0=pwd[:, 0:TT], in1=sg[:, 0:TT], op=ALU.mult)
                else:
                    t2, t_t2 = k.tmpa[2], k.t_tmpa[2]
                    k.V("tensor_tensor", r=[ptwd, t_sg], w=[t_t2], out=t2[:, 0:TT], in0=pwd[:, 0:TT], in1=sg[:, 0:TT], op=ALU.mult)
                    if b == 1:
                        k.V("tensor_tensor", r=[t_t2, t_acc], w=[t_acc], out=acc[:, 0:TT], in0=acc[:, 0:TT], in1=t2[:, 0:TT], op=ALU.add)
                    else:
                        k.V("tensor_tensor", r=[t_t2, t_acc], w=t_mb, out=mbT[:, j, 0:TT], in0=acc[:, 0:TT], in1=t2[:, 0:TT], op=ALU.add)
        yT = k.GA[:, 0:4096].rearrange("p (a b) -> p a b", a=8)
        ty = [tG[0], tG[1]]
        for half in range(2):
            wt, tw = k.wload(l, "out%d" % half)
            banks = [k.pbank() for _ in range(4)]
            for kc in range(8):
                for j in range(4):
                    pb, pt = banks[j]
                    k.T("matmul", r=t_mb + [tw], w=[pt], out=pb[:, 0:TT], lhsT=wt[:, kc, j * 128:(j + 1) * 128], rhs=mbT[:, kc, 0:TT], start=(kc == 0), stop=(kc == 7))
            for j in range(4):
                pb, pt = banks[j]
                k.A("activation", r=[pt], w=ty, out=yT[:, half * 4 + j, 0:TT], in_=pb[:, 0:TT], func=AF.Copy)
        k.postnorm_add("g_post", l, TT)

        k.prenorm("g_pmlp", l, TT)
        aT = k.GA[:].bitcast(BF16).rearrange("p (a b) -> p a b", a=32)
        t_a = [tG[0], tG[1], tG[2], tG[3]]
        t_aT = k.t_aT
        k.V("memset", w=t_a + t_aT, ap=k.fence[:], constant=0.0)
        for u in range(8):
            wt, tw = k.wload(l, "up%d" % u)
            banks = [k.pbank() for _ in range(4)]
            for kc in range(8):
                for j in range(4):
                    pb, pt = banks[j]
                    k.T("matmul", r=[k.t_h, tw], w=[pt], out=pb[:, 0:TT], lhsT=wt[:, kc, j * 128:(j + 1) * 128], rhs=k.hT[:, kc, 0:TT], start=(kc == 0), stop=(kc == 7))
            for j in range(4):
                pb, pt = banks[j]
                rl, t_rl = k.tmpa[j % 2], k.t_tmpa[j % 2]
                k.A("activation", r=[pt], w=[t_rl], out=rl[:, 0:TT], in_=pb[:, 0:TT], func=AF.Relu)
                k.E("tensor_tensor", r=[t_rl], w=[t_aT[u * 4 + j]], out=aT[:, u * 4 + j, 0:TT], in0=rl[:, 0:TT], in1=rl[:, 0:TT], op=ALU.mult)
        yT2 = k.GB[:, 0:4096].rearrange("p (a b) -> p a b", a=8)
        ty2 = [tG[4], tG[5]]
        for half in range(2):
            banks = [k.pbank() for _ in range(4)]
            for g in range(8):
                wt, tw = k.wload(l, "dn%d_%d" % (half, g))
                for kc in range(4):
                    for j in range(4):
                        pb, pt = banks[j]
                        k.T("matmul", r=[t_aT[g * 4 + kc], tw], w=[pt], out=pb[:, 0:TT], lhsT=wt[:, kc, j * 128:(j + 1) * 128], rhs=aT[:, g * 4 + kc, 0:TT],
                            start=(g == 0 and kc == 0), stop=(g == 7 and kc == 3))
            for j in range(4):
                pb, pt = banks[j]
                k.A("activation", r=[pt], w=ty2, out=yT2[:, half * 4 + j, 0:TT], in_=pb[:, 0:TT], func=AF.Copy)
        k.E("tensor_copy", r=ty2, w=t_a + t_aT, out=k.GA[:, 0:4096], in_=k.GB[:, 0:4096])
        k.postnorm_add("g_postmlp", l, TT)

    def mixB(self, l, TT, sample, first):
        k = self
        nch = TT // CH
        c0 = l * NCL
        d0 = l * 16
        cst = k.cst
        G = [k.Gv(i) for i in range(6)]
        tG = k.t_G
        B = [k.Bv(i) for i in range(6)]
        tB = k.t_B
        zx, t_zx = k.Gv(5), tG[5]
        zxv = k.GB[:, 2048:2048 + 4 * 515].rearrange("p (a b) -> p a b", a=4)
        gB, t_gB = B[0], tB[0]
        wt, tw = k.wload(l, "bx")
        if sample:
            pass
        else:
            k.V("tensor_copy", r=[k.t_halB[l]], w=[t_zx], out=zxv[:, :, 0:3], in_=k.halB[l][:])
        k.proj_fm(wt, tw, 4, TT, lambda j, p, pt: k.A("activation", r=[pt], w=[t_zx], out=zxv[:, j, 3:3 + TT], in_=p, func=AF.Copy))
        yield
        wt, tw = k.wload(l, "bg")
        k.proj_fm(wt, tw, 4, TT, lambda j, p, pt: k.A("activation", r=[pt], w=[t_gB], out=gB[:, j, 0:TT], in_=p, func=AF.Gelu))
        yield
        k.DMA("gpsimd", k.t_gwb, w=[k.t_gwb], out=k.gwb[:], in_=k.gw_d[:, l * 1024:(l + 1) * 1024])
        xc, t_xc = G[2], tG[2]
        if sample:
            for c in range(nch):
                k.DMA("gpsimd", k.t_halB[l], w=[k.t_halB[l]], out=k.halB[l][:], in_=k.s_rgc[l, c])
                k.conv(zxv, t_zx, k.halB[l], k.t_halB[l], xc, t_xc, 4, c0 + CL["bcw"], c0 + CL["bcb"], c * CH, CH, seq_halo=True)
                k.DMA("gpsimd", k.outslot, r=[t_zx], out=k.o_rgc[l, 1 + c], in_=zxv[:, :, 3 + (c + 1) * CH - 3:3 + (c + 1) * CH], final=True)
        else:
            for cc_ in range(4):
                k.conv(zxv, t_zx, None, None, xc, t_xc, 4, c0 + CL["bcw"], c0 + CL["bcb"], 0, TT, seq_halo=False, ccs=[cc_])
                yield
            k.V("tensor_copy", r=[t_zx], w=[k.t_halB[l]], out=k.halB[l][:], in_=zxv[:, :, TT:TT + 3])
        xcb, t_xcb = k.GA[:, 2048:4096].bitcast(BF16)[:, 0:2048].rearrange("p (a b) -> p a b", a=4), tG[1]
        k.E("tensor_copy", r=[t_xc], w=[t_xcb], out=xcb[:, :, 0:TT], in_=xc[:, :, 0:TT])
        hs, t_hs = G[3], tG[3]
        for cc in range(4):
            r_, i_, a_ = k.tmpa[0], k.tmpa[1], k.tmpa[2]
            tr, ti_, ta = k.t_tmpa
            pb, pt = k.pbank()
            k.T("matmul", r=[t_xcb, k.t_gwb], w=[pt], out=pb[:, 0:TT], lhsT=k.gwb[:, cc * 128:(cc + 1) * 128], rhs=xcb[:, cc, 0:TT], start=True, stop=True)
            k.A("activation", r=[pt, k.t_cst], w=[tr], out=r_[:, 0:TT], in_=pb[:, 0:TT], func=AF.Sigmoid, bias=cst[:, c0 + CL["gab"] + cc:c0 + CL["gab"] + cc + 1])
            pb, pt = k.pbank()
            k.T("matmul", r=[t_xcb, k.t_gwb], w=[pt], out=pb[:, 0:TT], lhsT=k.gwb[:, 512 + cc * 128:512 + (cc + 1) * 128], rhs=xcb[:, cc, 0:TT], start=True, stop=True)
            k.A("activation", r=[pt, k.t_cst], w=[ti_], out=i_[:, 0:TT], in_=pb[:, 0:TT], func=AF.Sigmoid, bias=cst[:, c0 + CL["gxb"] + cc:c0 + CL["gxb"] + cc + 1])
            yield
            k.A("activation", r=[tr, k.t_der], w=[ta], out=a_[:, 0:TT], in_=r_[:, 0:TT], func=AF.Exp, scale=k.der[:, d0 + 8 + cc:d0 + 9 + cc])
            k.A("activation", r=[tr, k.t_der], w=[tr], out=r_[:, 0:TT], in_=r_[:, 0:TT], func=AF.Exp, scale=k.der[:, d0 + 12 + cc:d0 + 13 + cc])
            k.V("tensor_scalar", r=[tr], w=[tr], out=r_[:, 0:TT], in0=r_[:, 0:TT], scalar1=-1.0, scalar2=1.0, op0=ALU.mult, op1=ALU.add)
            k.A("activation", r=[tr], w=[tr], out=r_[:, 0:TT], in_=r_[:, 0:TT], func=AF.Sqrt)
            if first:
                k.V("memset", w=[tr], ap=r_[:, 0:1], constant=1.0)
            yield
            k.V("tensor_tensor", r=[tr, ti_], w=[ti_], out=i_[:, 0:TT], in0=i_[:, 0:TT], in1=r_[:, 0:TT], op=ALU.mult)
            k.V("tensor_tensor", r=[ti_, t_xc], w=[ti_], out=i_[:, 0:TT], in0=i_[:, 0:TT], in1=xc[:, cc, 0:TT], op=ALU.mult)
            if sample:
                for c in range(nch):
                    if cc == 0:
                        k.DMA("gpsimd", k.t_hBs, w=[k.t_hBs], out=k.hBs[c][:], in_=k.s_rg[l, c])
                    k.V("tensor_tensor_scan", r=[ta, ti_, k.t_hBs], w=[t_hs], out=hs[:, cc, c * CH:(c + 1) * CH], data0=a_[:, c * CH:(c + 1) * CH],
                        data1=i_[:, c * CH:(c + 1) * CH], initial=k.hBs[c][:, cc:cc + 1], op0=ALU.mult, op1=ALU.add)
            else:
                k.V("tensor_tensor_scan", r=[ta, ti_, k.t_hB[l]], w=[t_hs], out=hs[:, cc, 0:TT], data0=a_[:, 0:TT], data1=i_[:, 0:TT],
                    initial=k.hB[l][:, cc:cc + 1], op0=ALU.mult, op1=ALU.add)
                k.V("tensor_copy", r=[t_hs], w=[k.t_hB[l]], out=k.hB[l][:, cc:cc + 1], in_=hs[:, cc, TT - 1:TT])
            yield
        if sample:
            for c in range(nch):
                k.V("tensor_copy", r=[t_hs], w=[k.t_hfin], out=k.hfin[:], in_=hs[:, :, (c + 1) * CH - 1])
                k.DMA("gpsimd", k.outslot, r=[k.t_hfin], out=k.o_rg[l, 1 + c], in_=k.hfin[:], final=True)
        k.E("tensor_tensor", r=[t_hs, t_gB], w=[k.t_o[1]], out=k.oT[1][:, :, 0:TT], in0=hs[:, :, 0:TT], in1=gB[:, :, 0:TT], op=ALU.mult)


    def conv(self, zxv, t_zx, hal, t_hal, out, t_out, ncc, wcol, bcol, t0, n, seq_halo, wstride=4, ccs=None):
        k = self
        cst = k.cst
        for cc in (range(ncc) if ccs is None else ccs):
            o = out[:, cc, t0:t0 + n]
            def wv(j):
                return cst[:, wcol + j * wstride + cc:wcol + j * wstride + cc + 1]
            rds = [t_zx, k.t_cst]
            if bcol is not None:
                k.V("tensor_scalar", r=rds, w=[t_out], out=o, in0=zxv[:, cc, 3 + t0:3 + t0 + n], scalar1=wv(3), scalar2=cst[:, bcol + cc:bcol + cc + 1],
                    op0=ALU.mult, op1=ALU.add)
            else:
                k.V("tensor_scalar", r=rds, w=[t_out], out=o, in0=zxv[:, cc, 3 + t0:3 + t0 + n], scalar1=wv(3), scalar2=None, op0=ALU.mult)
            for j in range(3):
                sh = 3 - j
                if not seq_halo:
                    k.V("scalar_tensor_tensor", r=rds + [t_out], w=[t_out], out=o, in0=zxv[:, cc, 3 + t0 - sh:3 + t0 - sh + n], scalar=wv(j), in1=o,
                        op0=ALU.mult, op1=ALU.add)
                else:
                    k.V("scalar_tensor_tensor", r=rds + [t_out], w=[t_out], out=out[:, cc, t0 + sh:t0 + n], in0=zxv[:, cc, 3 + t0:3 + t0 + n - sh], scalar=wv(j),
                        in1=out[:, cc, t0 + sh:t0 + n], op0=ALU.mult, op1=ALU.add)
                    k.V("scalar_tensor_tensor", r=rds + [t_out, t_hal], w=[t_out], out=out[:, cc, t0:t0 + sh], in0=hal[:, cc, 3 - sh:3], scalar=wv(j),
                        in1=out[:, cc, t0:t0 + sh], op0=ALU.mult, op1=ALU.add)

    def load_state_A(self, l, c):
        k = self
        k.DMA("gpsimd", k.t_SA[l], w=[k.t_SA[l]], out=k.SA[l][:], in_=k.s_hg[l, c].rearrange("h k v -> k h v"))

    def store_state_A(self, l, c):
        k = self
        k.DMA("gpsimd", k.outslot, r=[k.t_SA[l]], out=k.o_hg[l, 1 + c].rearrange("h k v -> k h v"), in_=k.SA[l][:], final=True)

    def load_state_C(self, l, c):
        k = self
        k.DMA("gpsimd", k.t_SC[l], w=[k.t_SC[l]], out=k.SC[l][:], in_=k.s_gd[l, c].rearrange("h k v -> k h v"))

    def store_state_C(self, l, c):
        k = self
        k.DMA("gpsimd", k.outslot, r=[k.t_SC[l]], out=k.o_gd[l, 1 + c].rearrange("h k v -> k h v"), in_=k.SC[l][:], final=True)

    def store_prompt_states(self):
        k = self
        for l in range(DEPTH):
            k.DMA("gpsimd", k.outslot, r=[k.t_SA[l]], out=k.o_hg[l, 0].rearrange("h k v -> k h v"), in_=k.SA[l][:], final=True)
            k.DMA("gpsimd", k.outslot, r=[k.t_SC[l]], out=k.o_gd[l, 0].rearrange("h k v -> k h v"), in_=k.SC[l][:], final=True)
            k.DMA("gpsimd", k.outslot, r=[k.t_hB[l]], out=k.o_rg[l, 0], in_=k.hB[l][:], final=True)
            k.DMA("gpsimd", k.outslot, r=[k.t_halB[l]], out=k.o_rgc[l, 0], in_=k.halB[l][:], final=True)
            k.DMA("gpsimd", k.outslot, r=[k.t_halC[l]], out=k.o_gdc[l, 0], in_=k.halC[l][:], final=True)

    def build(self):
        k = self
        k.alloc()
        k.hBs = [k.sb("hBs%d" % c, [128, 4], F32) for c in range(NSEQ)]
        k.halCs = [k.sb("halCs%d" % c, [128, 12, 3], F32) for c in range(NSEQ)]
        k.t_hBs = Buf("hBs"); k.t_halCs = Buf("halCs")
        k.hfin = k.sb("hfin", [128, 4], F32); k.t_hfin = Buf("hfin")
        k.setup()
        k.preconvert()
        xslot = Buf("xslot")
        for ti in range(k.npt):
            k.DMA("gpsimd", xslot, w=[k.t_x], out=k.xT[:], in_=k.xp[:, :, ti * 512:(ti + 1) * 512])
            for l in range(DEPTH):
                k.block(ti, l, 512, False)
            k.DMA("gpsimd", k.outslot, r=[k.t_x], out=k.yp[:, :, ti * 512:(ti + 1) * 512], in_=k.xT[:], final=True)
        k.store_prompt_states()
        if k.do_sample:
            TT = NSEQ * CH
            k.DMA("gpsimd", xslot, w=[k.t_x], out=k.xT[:, :, 0:TT], in_=k.xs[:, :, :])
            for l in range(DEPTH):
                k.block(0, l, TT, True)
            k.DMA("gpsimd", k.outslot, r=[k.t_x], out=k.ys[:, :, :], in_=k.xT[:, :, 0:TT], final=True)
        k.P.emit()
        k.P.close()
        return k.nc


def _wtile(src, kc, n0, nw, k0=0):
    blk = src[k0:k0 + kc * 128, n0:n0 + nw].reshape(kc, 128, nw)
    return np.ascontiguousarray(blk.transpose(1, 0, 2)).reshape(128, kc * nw)


def _layout_weights(w_in, w_br_a, w_br_b, w_br_c, w_out, w_up, w_down):
    wf = np.zeros((DEPTH, 128, WCOLS_PAD), np.float32)
    col_in = dict(aq=0, af=512, ai=1024, ag=1536, bx=2048, bg=2560, cq=3072, ck=3584, cv=4096, cg=4608)
    for l in range(DEPTH):
        for name, kc, nw, off in WT:
            if name in col_in:
                t = _wtile(w_in[l], 8, col_in[name], 512)
            elif name == "ba":
                t = _wtile(w_in[l], 8, 5120, 8)
            elif name.startswith("mg"):
                j = int(name[2:])
                parts = [w_in[l][:, 5128 + b * 1024 + j * 128:5128 + b * 1024 + (j + 1) * 128] for b in range(3)]
                t = _wtile(np.concatenate(parts, axis=1), 8, 0, 384)
            elif name.startswith("br"):
                j = int(name[2:])
                parts = [w[l][:, j * 128:(j + 1) * 128] for w in (w_br_a, w_br_b, w_br_c)]
                t = _wtile(np.concatenate(parts, axis=0), 12, 0, 128)
            elif name.startswith("out"):
                t = _wtile(w_out[l], 8, int(name[3:]) * 512, 512)
            elif name.startswith("up"):
                t = _wtile(w_up[l], 8, int(name[2:]) * 512, 512)
            else:
                h, g = name[2:].split("_")
                t = _wtile(w_down[l], 4, int(h) * 512, 512, k0=int(g) * 512)
            wf[l, :, off:off + kc * nw] = t
    return wf


def _fm(v, c):
    return np.ascontiguousarray(v.reshape(c, 128).T)


def _consts(inp):
    cst = np.zeros((128, DEPTH * NCL), np.float32)
    for l in range(DEPTH):
        b = l * NCL
        cst[:, b + CL["g_pre"]:b + CL["g_pre"] + 8] = _fm(inp["norm_pre_mix"][l], 8)
        cst[:, b + CL["g_post"]:b + CL["g_post"] + 8] = _fm(inp["norm_post_mix"][l], 8)
        cst[:, b + CL["g_pmlp"]:b + CL["g_pmlp"] + 8] = _fm(inp["norm_pre_mlp"][l], 8)
        cst[:, b + CL["g_postmlp"]:b + CL["g_postmlp"] + 8] = _fm(inp["norm_post_mlp"][l], 8)
        cst[:, b + CL["a_norm"]] = inp["a_norm"][l]
        cst[:, b + CL["c_norm"]] = inp["c_norm"][l]
        lr = np.stack([_fm(inp["lb_raw"][ll], 4) for ll in range(DEPTH)], axis=2)
        cst[:, b + CL["lbraw"]:b + CL["lbraw"] + 16] = lr.reshape(128, 16)
        for j in range(4):
            cst[:, b + CL["bcw"] + j * 4:b + CL["bcw"] + j * 4 + 4] = _fm(inp["b_conv_w"][l, j], 4)
            cst[:, b + CL["ccw"] + j * 12:b + CL["ccw"] + j * 12 + 12] = _fm(inp["c_conv_w"][l, j], 12)
        cst[:, b + CL["bcb"]:b + CL["bcb"] + 4] = _fm(inp["b_conv_b"][l], 4)
        cst[:, b + CL["gab"]:b + CL["gab"] + 4] = _fm(inp["b_gate_a_b"][l], 4)
        cst[:, b + CL["gxb"]:b + CL["gxb"] + 4] = _fm(inp["b_gate_x_b"][l], 4)
        cst[:, b + CL["lam"]:b + CL["lam"] + 4] = _fm(inp["b_lambda"][l], 4)
        cst[:, b + CL["alog"]:b + CL["alog"] + 4] = inp["c_a_log"][l][None, :]
        cst[:, b + CL["dtb"]:b + CL["dtb"] + 4] = inp["c_dt_bias"][l][None, :]
    gw = np.zeros((128, DEPTH * 1024), np.float32)
    for l in range(DEPTH):
        for wi, w in enumerate((inp["b_gate_a_w"][l], inp["b_gate_x_w"][l])):
            for cc in range(4):
                m = np.zeros((128, 128), np.float32)
                m[0:64, 0:64] = w[2 * cc]
                m[64:128, 64:128] = w[2 * cc + 1]
                gw[:, l * 1024 + wi * 512 + cc * 128:l * 1024 + wi * 512 + (cc + 1) * 128] = m
    k128 = np.zeros((128, 768), np.float32)
    k128[:, 0:128] = np.eye(128, dtype=np.float32)
    k128[:, 128:256] = 1.0
    sm = np.ones(512, np.float32)
    sm[::CH] = 0.0
    k128[:, 256:768] = sm[None, :]
    k64 = np.zeros((64, 896), np.float32)
    r = np.arange(64)
    k64[:, 0:64] = (r[:, None] <= r[None, :]).astype(np.float32)
    k64[:, 64:192] = 1.0
    strictL = (r[:, None] > r[None, :]).astype(np.float32)
    strictU = (r[:, None] < r[None, :]).astype(np.float32)
    inclU = (r[:, None] <= r[None, :]).astype(np.float32)
    k64[:, 192:256] = strictL
    k64[:, 256:320] = strictU
    k64[:, 320:384] = inclU
    k64[:, 384:896] = np.tile(np.eye(64, dtype=np.float32), (1, 8))
    return cst, gw, k128, k64


_NC_CACHE = {}
_LAST = {}


def _get_nc(npt, do_sample=True):
    key = (npt, do_sample)
    if key not in _NC_CACHE:
        kb = KB(npt, do_sample)
        _NC_CACHE[key] = kb.build()
        _LAST["streams"] = kb.P.streams
    return _NC_CACHE[key]


def kernel(x_prompt, x_sample, state_hgrn, state_rglru, state_rglru_conv, state_gdn, state_gdn_conv,
           lb_raw, norm_pre_mix, norm_post_mix, norm_pre_mlp, norm_post_mlp, w_in, a_norm,
           b_conv_w, b_conv_b, b_gate_a_w, b_gate_a_b, b_gate_x_w, b_gate_x_b, b_lambda,
           c_conv_w, c_a_log, c_dt_bias, c_norm, w_br_a, w_br_b, w_br_c, w_out, w_up, w_down, _npt=None):
    f = lambda a: np.asarray(a, dtype=np.float32)
    inp = dict(lb_raw=f(lb_raw), norm_pre_mix=f(norm_pre_mix), norm_post_mix=f(norm_post_mix), norm_pre_mlp=f(norm_pre_mlp),
               norm_post_mlp=f(norm_post_mlp), a_norm=f(a_norm), b_conv_w=f(b_conv_w), b_conv_b=f(b_conv_b),
               b_gate_a_w=f(b_gate_a_w), b_gate_a_b=f(b_gate_a_b), b_gate_x_w=f(b_gate_x_w), b_gate_x_b=f(b_gate_x_b),
               b_lambda=f(b_lambda), c_conv_w=f(c_conv_w), c_a_log=f(c_a_log), c_dt_bias=f(c_dt_bias), c_norm=f(c_norm))
    x_prompt = f(x_prompt); x_sample = f(x_sample)
    seq = x_prompt.shape[1]
    npt = seq // 512 if _npt is None else _npt
    ntok = npt * 512
    import time as _t, sys as _s
    _t0 = _t.time()
    nc = _get_nc(npt)
    print("[kernel] build %.1fs, ninstr=%s" % (_t.time() - _t0, {e: len(v) for e, v in _LAST.get("streams", {}).items()}), file=_s.stderr)
    wf = _layout_weights(f(w_in), f(w_br_a), f(w_br_b), f(w_br_c), f(w_out), f(w_up), f(w_down))
    cst, gw, k128, k64 = _consts(inp)
    xp = np.ascontiguousarray(x_prompt[0, :ntok].reshape(ntok, 8, 128).transpose(2, 1, 0))
    state_hgrn = f(state_hgrn); state_rglru = f(state_rglru); state_rglru_conv = f(state_rglru_conv)
    state_gdn = f(state_gdn); state_gdn_conv = f(state_gdn_conv)
    in_maps = []
    for c in range(NCORE):
        sl = slice(c * NSEQ, (c + 1) * NSEQ)
        xs = x_sample[sl].reshape(NSEQ * CH, 8, 128).transpose(2, 1, 0)
        in_maps.append(dict(
            xp=xp, xs=np.ascontiguousarray(xs), wf=wf, cst=cst, gw=gw, k128=k128, k64=k64,
            s_hg=np.ascontiguousarray(state_hgrn[:, sl]),
            s_rg=np.ascontiguousarray(state_rglru[:, sl].reshape(DEPTH, NSEQ, 4, 128).transpose(0, 1, 3, 2)),
            s_rgc=np.ascontiguousarray(state_rglru_conv[:, sl].reshape(DEPTH, NSEQ, 3, 4, 128).transpose(0, 1, 4, 3, 2)),
            s_gd=np.ascontiguousarray(state_gdn[:, sl]),
            s_gdc=np.ascontiguousarray(state_gdn_conv[:, sl].reshape(DEPTH, NSEQ, 3, 12, 128).transpose(0, 1, 4, 3, 2)),
        ))
    import time as _t, sys as _s
    _t0 = _t.time()
    res = run_bass_kernel_spmd(nc, in_maps, core_ids=list(range(NCORE)))
    print("[kernel] run %.1fs" % (_t.time() - _t0), file=_s.stderr)
    R = res.results
    yp = np.ascontiguousarray(R[0]["yp"].transpose(2, 1, 0)).reshape(1, ntok, D)
    ys = np.concatenate([np.ascontiguousarray(R[c]["ys"].transpose(2, 1, 0)).reshape(NSEQ, CH, D) for c in range(NCORE)], axis=0)
    def st(name, fn):
        p = fn(R[0][name][:, 0:1])
        s = np.concatenate([fn(R[c][name][:, 1:]) for c in range(NCORE)], axis=1)
        return np.ascontiguousarray(p), np.ascontiguousarray(s)
    ident = lambda a: a
    p_hg, s_hg = st("o_hg", ident)
    p_gd, s_gd = st("o_gd", ident)
    p_rg, s_rg = st("o_rg", lambda a: a.transpose(0, 1, 3, 2).reshape(DEPTH, a.shape[1], 512))
    p_rgc, s_rgc = st("o_rgc", lambda a: a.transpose(0, 1, 4, 3, 2).reshape(DEPTH, a.shape[1], 3, 512))
    p_gdc, s_gdc = st("o_gdc", lambda a: a.transpose(0, 1, 4, 3, 2).reshape(DEPTH, a.shape[1], 3, 1536))
    return (yp, ys, p_hg, p_rg, p_rgc, p_gd, p_gdc, s_hg, s_rg, s_rgc, s_gd, s_gdc)
```

```python
import numpy as np
import concourse.bass as bass
import concourse.mybir as mybir
from concourse.bass_utils import run_bass_kernel_spmd

F32 = mybir.dt.float32
BF16 = mybir.dt.bfloat16
ALU = mybir.AluOpType
AF = mybir.ActivationFunctionType

D = 1024
DEPTH = 4
SEQ = 16384
NCORE = 8
NSEQ = 4
CH = 64
EPS = 1e-6
D_IN = 8200
NPT = 32

ENGS = ("tensor", "vector", "scalar", "gpsimd", "sync")

WT = []
WOFF = {}


def _build_wt():
    off = 0
    def add(name, kc, nw):
        nonlocal off
        WT.append((name, kc, nw, off))
        WOFF[name] = (kc, nw, off)
        off += kc * nw
    for n in ("aq", "af", "ai", "ag", "bx", "bg", "cq", "ck", "cv", "cg"):
        add(n, 8, 512)
    add("ba", 8, 8)
    for j in range(8):
        add("mg%d" % j, 8, 384)
        add("br%d" % j, 12, 128)
    add("out0", 8, 512)
    add("out1", 8, 512)
    for j in range(8):
        add("up%d" % j, 8, 512)
    for h in range(2):
        for g in range(8):
            add("dn%d_%d" % (h, g), 4, 512)
    return off


WCOLS = _build_wt()
WCOLS_PAD = ((WCOLS + 4095) // 4096) * 4096

CL = {}


def _build_cl():
    off = 0
    def add(name, n):
        nonlocal off
        CL[name] = off
        off += n
    add("g_pre", 8); add("g_post", 8); add("g_pmlp", 8); add("g_postmlp", 8)
    add("a_norm", 1); add("c_norm", 1)
    add("lbraw", 16)
    add("bcw", 16)
    add("bcb", 4); add("gab", 4); add("gxb", 4); add("lam", 4)
    add("ccw", 48)
    add("alog", 4); add("dtb", 4)
    return off


NCL = _build_cl()


class Buf:
    __slots__ = ("name", "lw", "rd", "dsem", "dcnt")

    def __init__(self, name=""):
        self.name = name
        self.lw = None
        self.rd = []
        self.dsem = None
        self.dcnt = 0


class Prog:
    def __init__(self, nc):
        self.nc = nc
        self.streams = {e: [] for e in ENGS}
        self.cnt = {e: 0 for e in ENGS}
        self.sem = {}
        self.waited = {}
        self.ctx = []
        for e in ENGS:
            cm = nc.semaphore("cs_" + e)
            self.sem[e] = cm.__enter__()
            self.ctx.append(cm)
        self.ndsem = 0
        self.final = []

    def _dsem(self, b):
        if b.dsem is None:
            cm = self.nc.semaphore("ds%d" % self.ndsem)
            self.ndsem += 1
            b.dsem = cm.__enter__()
            self.ctx.append(cm)
        return b.dsem

    def _deps(self, reads, writes, eng=None):
        deps = {}
        def add(kv, skip=None):
            if kv is None:
                return
            k, v = kv
            if skip is not None and k == skip:
                return
            if deps.get(k, 0) < v:
                deps[k] = v
        for b in reads:
            add(b.lw)
        for b in writes:
            add(b.lw, eng)
            for r in b.rd:
                add(r, eng)
        return deps

    def _waits(self, eng, deps, skip_self):
        waits = []
        for k, v in deps.items():
            if skip_self and k == eng:
                continue
            if self.waited.get((eng, k), 0) >= v:
                continue
            self.waited[(eng, k)] = v
            semh = self.sem[k] if isinstance(k, str) else k
            waits.append((semh, v))
        return waits

    def _mark(self, done, reads, writes):
        for b in reads:
            b.rd.append(done)
            if len(b.rd) > 64:
                mx = {}
                for k, v in b.rd:
                    if mx.get(k, 0) < v:
                        mx[k] = v
                b.rd = list(mx.items())
        for b in writes:
            b.lw = done
            b.rd = []

    def op(self, eng, name, kw, reads=(), writes=()):
        deps = self._deps(reads, writes, eng)
        waits = self._waits(eng, deps, skip_self=(eng == "tensor"))
        self.cnt[eng] += 1
        done = (eng, self.cnt[eng])
        self.streams[eng].append((waits, name, kw, self.sem[eng], 1))
        self._mark(done, reads, writes)

    def dma(self, eng, kw, slot, reads=(), writes=(), final=False):
        semh = self._dsem(slot)
        deps = self._deps(reads, writes)
        if slot.dcnt > 0 and deps.get(semh, 0) < slot.dcnt:
            deps[semh] = slot.dcnt
        waits = self._waits(eng, deps, skip_self=False)
        slot.dcnt += 16
        done = (semh, slot.dcnt)
        self.streams[eng].append((waits, "dma_start", kw, semh, 16))
        self._mark(done, reads, writes)
        if final:
            self.final.append(done)

    def emit(self):
        nc = self.nc
        fin = {}
        for k, v in self.final:
            fin[k] = max(fin.get(k, 0), v)
        streams = self.streams
        with nc.Block() as block:
            def mk(ename):
                def body(engine):
                    for waits, name, kw, semh, inc in streams[ename]:
                        for (s, v) in waits:
                            engine.wait_ge(s, v)
                        getattr(engine, name)(**kw).then_inc(semh, inc)
                    if ename == "sync":
                        for s, v in fin.items():
                            engine.wait_ge(s, v)
                return body
            block.tensor(mk("tensor"))
            block.vector(mk("vector"))
            block.scalar(mk("scalar"))
            block.gpsimd(mk("gpsimd"))
            block.sync(mk("sync"))

    def close(self):
        for cm in reversed(self.ctx):
            cm.__exit__(None, None, None)


class KB:
    def __init__(self, npt=NPT, do_sample=True):
        self.npt = npt
        self.do_sample = do_sample
        nc = bass.Bass("TRN2", target_bir_lowering=False)
        self.nc = nc
        self.P = Prog(nc)
        self.cms = []
        self.rr = 0
        ntok = npt * 512
        self.ntok = ntok
        di = lambda n, s: nc.dram_tensor(n, s, F32, kind="ExternalInput").ap()
        do = lambda n, s: nc.dram_tensor(n, s, F32, kind="ExternalOutput").ap()
        self.xp = di("xp", [128, 8, ntok])
        self.xs = di("xs", [128, 8, NSEQ * CH])
        self.wf = di("wf", [DEPTH, 128, WCOLS_PAD])
        self.cst_d = di("cst", [128, DEPTH * NCL])
        self.gw_d = di("gw", [128, DEPTH * 1024])
        self.k128_d = di("k128", [128, 128 + 128 + 512])
        self.k64_d = di("k64", [64, 896])
        self.s_hg = di("s_hg", [DEPTH, NSEQ, 4, 128, 128])
        self.s_rg = di("s_rg", [DEPTH, NSEQ, 128, 4])
        self.s_rgc = di("s_rgc", [DEPTH, NSEQ, 128, 4, 3])
        self.s_gd = di("s_gd", [DEPTH, NSEQ, 4, 128, 128])
        self.s_gdc = di("s_gdc", [DEPTH, NSEQ, 128, 12, 3])
        self.yp = do("yp", [128, 8, ntok])
        self.ys = do("ys", [128, 8, NSEQ * CH])
        self.o_hg = do("o_hg", [DEPTH, NSEQ + 1, 4, 128, 128])
        self.o_rg = do("o_rg", [DEPTH, NSEQ + 1, 128, 4])
        self.o_rgc = do("o_rgc", [DEPTH, NSEQ + 1, 128, 4, 3])
        self.o_gd = do("o_gd", [DEPTH, NSEQ + 1, 4, 128, 128])
        self.o_gdc = do("o_gdc", [DEPTH, NSEQ + 1, 128, 12, 3])
        self.wb = nc.dram_tensor("wb", [DEPTH, 128, WCOLS_PAD], BF16).ap()
        self.wb_tok = [[Buf("wb%d_%d" % (l, i)) for i in range(WCOLS_PAD // 4096)] for l in range(DEPTH)]
        self.outslot = Buf("outslot")

    def sb(self, name, shape, dt):
        cm = self.nc.sbuf_tensor("sb_" + name, shape, dt)
        t = cm.__enter__()
        self.cms.append(cm)
        return t

    def ps(self, name, shape, dt):
        cm = self.nc.psum_tensor("ps_" + name, shape, dt)
        t = cm.__enter__()
        self.cms.append(cm)
        return t

    def T(self, name, r=(), w=(), **kw):
        self.P.op("tensor", name, kw, r, w)

    def V(self, name, r=(), w=(), **kw):
        self.P.op("vector", name, kw, r, w)

    def A(self, name, r=(), w=(), **kw):
        self.P.op("scalar", name, kw, r, w)

    def G(self, name, r=(), w=(), **kw):
        self.P.op("gpsimd", name, kw, r, w)

    def E(self, name, r=(), w=(), **kw):
        self.rr += 1
        self.P.op("gpsimd" if (self.rr % 3 == 0) else "vector", name, kw, r, w)

    def DMA(self, q, slot, r=(), w=(), final=False, **kw):
        self.P.dma(q, kw, slot, r, w, final)

    def pquad(self):
        i = self.pq_i
        self.pq_i = 1 - i
        return self.quads[i], self.pb_tok[i * 4:(i + 1) * 4]

    def pbank(self):
        i = self.pb_i
        self.pb_i = (i + 1) % 8
        return self.pbanks[i], self.pb_tok[i]

    def alloc(self):
        sb = self.sb
        self.quads = [self.ps("quad%d" % i, [128, 2048], F32) for i in range(2)]
        self.pbanks = [self.quads[i // 4][:, (i % 4) * 512:(i % 4 + 1) * 512] for i in range(8)]
        self.pq_i = 0
        self.pb_tok = [Buf("pb%d" % i) for i in range(8)]
        self.pb_i = 0
        self.cst = sb("cst", [128, DEPTH * NCL], F32); self.t_cst = Buf("cst")
        self.der = sb("der", [128, DEPTH * 16], F32); self.t_der = Buf("der")
        self.nega = sb("nega", [64, DEPTH * 4], F32)
        self.gwb = sb("gwb", [128, 1024], BF16); self.t_gwb = Buf("gwb")
        self.k128 = sb("k128", [128, 768], F32); self.t_k128 = Buf("k128")
        self.k64 = sb("k64", [64, 896], F32); self.t_k64 = Buf("k64")
        self.identb = sb("identb", [128, 128], BF16)
        self.onesb = sb("onesb", [128, 128], BF16)
        self.epst = sb("epst", [128, 1], F32)
        self.t_kc = Buf("kconst")
        self.xT = sb("xT", [128, 8, 512], F32); self.t_x = Buf("xT")
        self.hT = sb("hT", [128, 8, 512], BF16); self.t_h = Buf("hT")
        self.NW = 3
        self.wsl = [sb("wsl%d" % i, [128, 4096], BF16) for i in range(self.NW)]
        self.t_wsl = [Buf("wsl%d" % i) for i in range(self.NW)]
        self.ws_i = 0
        self.GA = sb("GA", [128, 8192], F32)
        self.t_G = [Buf("G%d" % i) for i in range(6)]
        self.GB = sb("GB", [128, 4096 + 64], F32)
        self.BB = sb("BB", [128, 6 * 2048], BF16)
        self.t_B = [Buf("B%d" % i) for i in range(6)]
        self.TK = [sb("TK0", [64, 8, 512], BF16)]
        self.t_TK = [Buf("TK0")]
        self.oT = [sb("oT%d" % i, [128, 4, 512], BF16) for i in range(3)]
        self.t_o = [Buf("oT%d" % i) for i in range(3)]
        self.rstd = sb("rstd", [128, 512], F32); self.t_rstd = Buf("rstd")
        self.fence = sb("fence", [128, 2], F32)
        self.t_aT = [Buf("aT%d" % i) for i in range(32)]
        self.rs4 = sb("rs4", [128, 4, 512], F32); self.t_rs4 = Buf("rs4")
        self.tmpa = [sb("tmpa%d" % i, [128, 512], F32) for i in range(3)]
        self.t_tmpa = [Buf("tmpa%d" % i) for i in range(3)]
        self.tmpb = [sb("tmpb%d" % i, [128, 512], BF16) for i in range(3)]
        self.t_tmpb = [Buf("tmpb%d" % i) for i in range(3)]
        self.SA = [sb("SA%d" % l, [128, 4, 128], F32) for l in range(DEPTH)]
        self.SC = [sb("SC%d" % l, [128, 4, 128], F32) for l in range(DEPTH)]
        self.t_SA = [Buf("SA%d" % l) for l in range(DEPTH)]
        self.t_SC = [Buf("SC%d" % l) for l in range(DEPTH)]
        self.SAb = sb("SAb", [128, 4, 128], BF16); self.t_SAb = Buf("SAb")
        self.SCb = sb("SCb", [128, 4, 128], BF16); self.t_SCb = Buf("SCb")
        self.hB = [sb("hB%d" % l, [128, 4], F32) for l in range(DEPTH)]
        self.t_hB = [Buf("hB%d" % l) for l in range(DEPTH)]
        self.halB = [sb("halB%d" % l, [128, 4, 3], F32) for l in range(DEPTH)]
        self.t_halB = [Buf("halB%d" % l) for l in range(DEPTH)]
        self.halC = [sb("halC%d" % l, [128, 12, 3], F32) for l in range(DEPTH)]
        self.t_halC = [Buf("halC%d" % l) for l in range(DEPTH)]
        def two(name, shape, dt):
            return [sb("%s%d" % (name, i), shape, dt) for i in range(2)], [Buf("%s%d" % (name, i)) for i in range(2)]
        def one(name, shape, dt):
            t = sb(name, shape, dt); b = Buf(name)
            return [t, t], [b, b]
        self.bgA = sb("bgA", [64, 8, 8], F32); self.t_bgA = Buf("bgA")
        self.gcs = sb("gcs", [64, 6, 32], F32); self.t_gcs = Buf("gcs")
        self.egl = sb("egl", [128, 32], F32); self.t_egl = Buf("egl")
        self.vn, self.t_vn = two("vn", [64, 4, 128], BF16)
        self.ident64rep = None

    def Gv(self, i, n=512, halo=0):
        if i < 4:
            base = self.GA[:, i * 2048:(i + 1) * 2048]
            return base.rearrange("p (a b) -> p a b", a=4)
        if i == 4:
            return self.GB[:, 0:2048].rearrange("p (a b) -> p a b", a=4)
        return self.GB[:, 2048:2048 + 4 * 515].rearrange("p (a b) -> p a b", a=4)

    def Bv(self, i):
        return self.BB[:, i * 2048:(i + 1) * 2048].rearrange("p (a b) -> p a b", a=4)

    def setup(self):
        k = self
        k.DMA("sync", k.t_cst, w=[k.t_cst], out=k.cst[:], in_=k.cst_d[:, :])
        k.DMA("sync", k.t_k128, w=[k.t_k128], out=k.k128[:], in_=k.k128_d[:, :])
        k.DMA("sync", k.t_k64, w=[k.t_k64], out=k.k64[:], in_=k.k64_d[:, :])
        k.V("tensor_copy", r=[k.t_k128], w=[k.t_kc], out=k.identb[:], in_=k.k128[:, 0:128])
        k.V("tensor_copy", r=[k.t_k128], w=[k.t_kc], out=k.onesb[:], in_=k.k128[:, 128:256])
        k.V("memset", w=[k.t_kc], ap=k.epst[:], constant=EPS)
        self.ident = k.k128[:, 0:128]
        self.scanmask = k.k128[:, 256:768]
        self.L64 = k.k64[:, 0:64]
        self.ones64 = k.k64[:, 64:192]
        self.strictL = k.k64[:, 192:256].unsqueeze(1).to_broadcast([64, 4, 64])
        self.strictU = k.k64[:, 256:320].unsqueeze(1).to_broadcast([64, 4, 64])
        self.inclU = k.k64[:, 320:384].unsqueeze(1).to_broadcast([64, 4, 64])
        self.identrep = k.k64[:, 384:896].rearrange("p (a b c) -> p a b c", a=4, b=2)
        self.m_strictL = k.k64[:, 192:256]
        self.m_strictU = k.k64[:, 256:320]
        self.m_inclU = k.k64[:, 320:384]
        self.m_ident = k.k64[:, 384:448]
        for l in range(DEPTH):
            c0 = l * NCL
            d0 = l * 16
            cst = k.cst
            if l == 0:
                lr = cst[:, c0 + CL["lbraw"]:c0 + CL["lbraw"] + 16].rearrange("p (h l) -> p h l", h=4)
                ex = k.tmpa[0][:, 0:16].rearrange("p (h l) -> p h l", h=4)
                sm = k.tmpa[0][:, 16:20]
                k.A("activation", r=[k.t_cst], w=[k.t_tmpa[0]], out=ex, in_=lr, func=AF.Exp)
                k.V("tensor_reduce", r=[k.t_tmpa[0]], w=[k.t_tmpa[0]], out=sm, in_=ex, axis=mybir.AxisListType.X, op=ALU.add)
                k.V("reciprocal", r=[k.t_tmpa[0]], w=[k.t_tmpa[0]], out=sm, in_=sm)
                k.V("tensor_tensor", r=[k.t_tmpa[0]], w=[k.t_tmpa[0]], out=ex, in0=ex,
                    in1=sm.unsqueeze(2).to_broadcast([128, 4, 4]), op=ALU.mult)
                k.V("memset", w=[k.t_der], ap=k.der[:, 0:4], constant=0.0)
                for ll in range(1, DEPTH):
                    k.V("tensor_tensor", r=[k.t_tmpa[0], k.t_der], w=[k.t_der], out=k.der[:, ll * 16:ll * 16 + 4],
                        in0=k.der[:, (ll - 1) * 16:(ll - 1) * 16 + 4], in1=ex[:, :, ll], op=ALU.add)
            k.V("tensor_scalar", r=[k.t_der], w=[k.t_der], out=k.der[:, d0 + 4:d0 + 8], in0=k.der[:, d0:d0 + 4],
                scalar1=-1.0, scalar2=1.0, op0=ALU.mult, op1=ALU.add)
            lam = cst[:, c0 + CL["lam"]:c0 + CL["lam"] + 4]
            t1 = k.tmpa[1][:, 0:4]
            k.A("activation", r=[k.t_cst], w=[k.t_tmpa[1]], out=t1, in_=lam, func=AF.Exp, scale=-1.0)
            k.V("tensor_scalar", r=[k.t_tmpa[1]], w=[k.t_tmpa[1]], out=t1, in0=t1, scalar1=1.0, scalar2=None, op0=ALU.add)
            k.A("activation", r=[k.t_tmpa[1]], w=[k.t_tmpa[1]], out=t1, in_=t1, func=AF.Ln)
            k.V("tensor_scalar", r=[k.t_tmpa[1]], w=[k.t_der], out=k.der[:, d0 + 8:d0 + 12], in0=t1, scalar1=-8.0, scalar2=None, op0=ALU.mult)
            k.V("tensor_scalar", r=[k.t_tmpa[1]], w=[k.t_der], out=k.der[:, d0 + 12:d0 + 16], in0=t1, scalar1=-16.0, scalar2=None, op0=ALU.mult)
            k.A("activation", r=[k.t_cst], w=[k.t_tmpa[2]], out=k.tmpa[2][0:64, 0:4], in_=cst[0:64, c0 + CL["alog"]:c0 + CL["alog"] + 4], func=AF.Exp)
            k.V("tensor_scalar", r=[k.t_tmpa[2]], w=[k.t_der], out=k.nega[:, l * 4:l * 4 + 4], in0=k.tmpa[2][0:64, 0:4], scalar1=-1.0, scalar2=None, op0=ALU.mult)
        for l in range(DEPTH):
            k.V("memset", w=[k.t_SA[l]], ap=k.SA[l][:], constant=0.0)
            k.V("memset", w=[k.t_SC[l]], ap=k.SC[l][:], constant=0.0)
            k.V("memset", w=[k.t_hB[l]], ap=k.hB[l][:], constant=0.0)
            k.V("memset", w=[k.t_halB[l]], ap=k.halB[l][:], constant=0.0)
            k.V("memset", w=[k.t_halC[l]], ap=k.halC[l][:], constant=0.0)

    def preconvert(self):
        k = self
        nblk = WCOLS_PAD // 4096
        i = 0
        stg = [(k.GA[:, 0:4096], k.t_G[0], k.t_G[1]), (k.GA[:, 4096:8192], k.t_G[2], k.t_G[3])]
        outb = [(k.BB[:, 0:4096], k.t_B[0]), (k.BB[:, 4096:8192], k.t_B[2])]
        for l in range(DEPTH):
            for b in range(nblk):
                s_ap, s_t, _ = stg[i % 2]
                o_ap, o_t = outb[i % 2]
                k.DMA("sync", s_t, w=[s_t], out=s_ap, in_=k.wf[l, :, b * 4096:(b + 1) * 4096])
                eng = ("vector", "gpsimd", "scalar")[i % 3]
                if eng == "scalar":
                    k.A("activation", r=[s_t], w=[o_t], out=o_ap, in_=s_ap, func=AF.Copy)
                else:
                    k.P.op(eng, "tensor_copy", dict(out=o_ap, in_=s_ap), [s_t], [o_t])
                k.DMA("gpsimd", o_t, r=[o_t], w=[k.wb_tok[l][b]], out=k.wb[l, :, b * 4096:(b + 1) * 4096], in_=o_ap)
                i += 1

    def wload(self, l, name):
        kc, nw, off = WOFF[name]
        i = self.ws_i
        self.ws_i = (i + 1) % self.NW
        n = kc * nw
        toks = self.wb_tok[l][off // 4096:(off + n - 1) // 4096 + 1]
        self.DMA("sync", self.t_wsl[i], r=list(toks), w=[self.t_wsl[i]], out=self.wsl[i][:, 0:n], in_=self.wb[l, :, off:off + n])
        return self.wsl[i][:, 0:n].rearrange("p (a b) -> p a b", a=kc), self.t_wsl[i]

    def sumsq_bcast(self, src_ap_list, toks, nsq, scale, TT):
        k = self
        pb, pt = k.pbank()
        n = len(src_ap_list)
        for i, a in enumerate(src_ap_list):
            k.T("matmul", r=list(toks) + [k.t_kc], w=[pt], out=pb[:, 0:TT], lhsT=k.onesb[:], rhs=a, start=(i == 0), stop=(i == n - 1))
        return pb, pt

    def rstd_from(self, pb, pt, TT, scale, out_ap, out_tok):
        k = self
        k.A("activation", r=[pt, k.t_kc], w=[out_tok], out=out_ap, in_=pb[:, 0:TT], func=AF.Ln, scale=scale, bias=k.epst[:])
        k.A("activation", r=[out_tok], w=[out_tok], out=out_ap, in_=out_ap, func=AF.Exp, scale=-0.5)

    def headsum_rstd(self, sq, tsq, TT, scale):
        k = self
        q, tq = k.pquad()
        for h in range(4):
            k.T("matmul", r=list(tsq) + [k.t_kc], w=tq, out=q[:, h * 512:h * 512 + TT], lhsT=k.onesb[:], rhs=sq[:, h, 0:TT], start=True, stop=True)
        qv = q[:].rearrange("p (h t) -> p h t", h=4)[:, :, 0:TT]
        k.A("activation", r=tq + [k.t_kc], w=[k.t_rs4], out=k.rs4[:, :, 0:TT], in_=qv, func=AF.Ln, scale=scale, bias=k.epst[:])
        k.A("activation", r=[k.t_rs4], w=[k.t_rs4], out=k.rs4[:, :, 0:TT], in_=k.rs4[:, :, 0:TT], func=AF.Exp, scale=-0.5)

    def prenorm(self, gname, l, TT):
        k = self
        sq = k.BB[:, 0:4096].rearrange("p (a b) -> p a b", a=8)
        tsq = [k.t_B[0], k.t_B[1]]
        k.A("activation", r=[k.t_x], w=tsq, out=sq[:, :, 0:TT], in_=k.xT[:, :, 0:TT], func=AF.Square)
        pb, pt = k.sumsq_bcast([sq[:, c, 0:TT] for c in range(8)], tsq, 8, 1.0 / D, TT)
        k.rstd_from(pb, pt, TT, 1.0 / D, k.rstd[:, 0:TT], k.t_rstd)
        g0 = l * NCL + CL[gname]
        for c in range(8):
            k.V("scalar_tensor_tensor", r=[k.t_x, k.t_rstd, k.t_cst], w=[k.t_h], out=k.hT[:, c, 0:TT], in0=k.xT[:, c, 0:TT],
                scalar=k.cst[:, g0 + c:g0 + c + 1], in1=k.rstd[:, 0:TT], op0=ALU.mult, op1=ALU.mult)

    def postnorm_add(self, gname, l, TT):
        k = self
        yT = k.GA[:, 0:4096].rearrange("p (a b) -> p a b", a=8)
        ty = [k.t_G[0], k.t_G[1]]
        sq = k.BB[:, 0:4096].rearrange("p (a b) -> p a b", a=8)
        tsq = [k.t_B[0], k.t_B[1]]
        k.A("activation", r=ty, w=tsq, out=sq[:, :, 0:TT], in_=yT[:, :, 0:TT], func=AF.Square)
        pb, pt = k.sumsq_bcast([sq[:, c, 0:TT] for c in range(8)], tsq, 8, 1.0 / D, TT)
        k.rstd_from(pb, pt, TT, 1.0 / D, k.rstd[:, 0:TT], k.t_rstd)
        g0 = l * NCL + CL[gname]
        for c in range(8):
            k.V("scalar_tensor_tensor", r=ty + [k.t_rstd, k.t_cst], w=ty, out=yT[:, c, 0:TT], in0=yT[:, c, 0:TT],
                scalar=k.cst[:, g0 + c:g0 + c + 1], in1=k.rstd[:, 0:TT], op0=ALU.mult, op1=ALU.mult)
        k.E("tensor_tensor", r=ty + [k.t_x], w=[k.t_x], out=k.xT[:, :, 0:TT], in0=k.xT[:, :, 0:TT], in1=yT[:, :, 0:TT], op=ALU.add)

    def headnorm_gate(self, src, t_src, gate, t_gate, ncol, out, t_out, TT, DH=128):
        k = self
        sq = k.BB[:, 0:2048].rearrange("p (a b) -> p a b", a=4)
        tsq = [k.t_B[0]]
        k.A("activation", r=[t_src], w=tsq, out=sq[:, :, 0:TT], in_=src[:, :, 0:TT], func=AF.Square)
        k.headsum_rstd(sq, tsq, TT, 1.0 / DH)
        k.V("scalar_tensor_tensor", r=[t_src, k.t_rs4, k.t_cst], w=[t_src], out=src[:, :, 0:TT], in0=src[:, :, 0:TT],
            scalar=k.cst[:, ncol:ncol + 1], in1=k.rs4[:, :, 0:TT], op0=ALU.mult, op1=ALU.mult)
        k.E("tensor_tensor", r=[t_src, t_gate], w=[t_out], out=out[:, :, 0:TT], in0=src[:, :, 0:TT], in1=gate[:, :, 0:TT], op=ALU.mult)

    def proj_fm(self, wt, t_w, ncol, TT, evac):
        k = self
        banks = [k.pbank() for _ in range(ncol)]
        for kc in range(8):
            for j in range(ncol):
                pb, pt = banks[j]
                k.T("matmul", r=[k.t_h, t_w], w=[pt], out=pb[:, 0:TT], lhsT=wt[:, kc, j * 128:(j + 1) * 128], rhs=k.hT[:, kc, 0:TT],
                    start=(kc == 0), stop=(kc == 7))
        for j in range(ncol):
            pb, pt = banks[j]
            evac(j, pb[:, 0:TT], pt)

    def proj_tm(self, wt, t_w, n, c, evac):
        k = self
        pb, pt = k.pbank()
        for kc in range(8):
            k.T("matmul", r=[k.t_h, t_w], w=[pt], out=pb[0:64, 0:n], lhsT=k.hT[:, kc, c * CH:(c + 1) * CH], rhs=wt[:, kc, 0:n],
                start=(kc == 0), stop=(kc == 7))
        evac(pb[0:64, 0:n], pt)

    def block(self, ti, l, TT, sample):
        k = self
        nch = TT // CH
        c0 = l * NCL
        d0 = l * 16
        cst = k.cst
        G = [k.Gv(i) for i in range(6)]
        tG = k.t_G
        B = [k.Bv(i) for i in range(6)]
        tB = k.t_B
        first = (ti == 0 and not sample)

        k.prenorm("g_pre", l, TT)

        qa, t_qa = G[0], tG[0]
        lf, t_lf = G[1], tG[1]
        ka, t_ka = G[2], tG[2]
        gA, t_gA = B[2], tB[2]
        vtok, t_vtok = k.TK[0], k.t_TK[0]
        wt, tw = k.wload(l, "aq")
        k.proj_fm(wt, tw, 4, TT, lambda j, p, pt: k.A("activation", r=[pt], w=[t_qa], out=qa[:, j, 0:TT], in_=p, func=AF.Silu))
        wt, tw = k.wload(l, "af")
        def ev_f(j, p, pt):
            k.A("activation", r=[pt], w=[t_ka], out=ka[:, j, 0:TT], in_=p, func=AF.Sigmoid)
            k.V("tensor_scalar", r=[t_ka, k.t_der], w=[t_lf], out=lf[:, j, 0:TT], in0=ka[:, j, 0:TT],
                scalar1=k.der[:, d0 + 4 + j:d0 + 5 + j], scalar2=k.der[:, d0 + j:d0 + j + 1], op0=ALU.mult, op1=ALU.add)
            k.V("tensor_scalar", r=[t_lf], w=[t_ka], out=ka[:, j, 0:TT], in0=lf[:, j, 0:TT], scalar1=-1.0, scalar2=1.0, op0=ALU.mult, op1=ALU.add)
            k.A("activation", r=[t_lf], w=[t_lf], out=lf[:, j, 0:TT], in_=lf[:, j, 0:TT], func=AF.Ln)
        k.proj_fm(wt, tw, 4, TT, ev_f)
        wt, tw = k.wload(l, "ai")
        for c in range(nch):
            k.proj_tm(wt, tw, 512, c, lambda p, pt, c=c: k.V("tensor_copy", r=[pt], w=[t_vtok], out=vtok[:, c, :], in_=p))
        wt, tw = k.wload(l, "ag")
        k.proj_fm(wt, tw, 4, TT, lambda j, p, pt: k.A("activation", r=[pt], w=[t_gA], out=gA[:, j, 0:TT], in_=p, func=AF.Silu))
        bcs, t_bcs = G[3], tG[3]
        for h in range(4):
            k.V("tensor_tensor_scan", r=[t_lf, k.t_k128], w=[t_bcs], out=bcs[:, h, 0:TT], data0=k.scanmask[:, 0:TT], data1=lf[:, h, 0:TT],
                initial=0.0, op0=ALU.mult, op1=ALU.add)
        eb, t_eb = G[4], tG[4]
        qt, t_qt = B[3], tB[3]
        kt, t_kt = B[4], tB[4]
        kh, t_kh = B[5], tB[5]
        k.A("activation", r=[t_bcs], w=[t_eb], out=eb[:, :, 0:TT], in_=bcs[:, :, 0:TT], func=AF.Exp)
        k.E("tensor_tensor", r=[t_qa, t_eb], w=[t_qt], out=qt[:, :, 0:TT], in0=qa[:, :, 0:TT], in1=eb[:, :, 0:TT], op=ALU.mult)
        k.A("activation", r=[t_bcs, t_qt], w=[t_eb], out=eb[:, :, 0:TT], in_=bcs[:, :, 0:TT], func=AF.Exp, scale=-1.0)
        k.E("tensor_tensor", r=[t_ka, t_eb], w=[t_kt], out=kt[:, :, 0:TT], in0=ka[:, :, 0:TT], in1=eb[:, :, 0:TT], op=ALU.mult)
        ebl = k.rs4[:, :, 0:nch]
        k.A("activation", r=[t_bcs], w=[k.t_rs4], out=ebl, in_=bcs[:, :, 0:TT].rearrange("p h (c t) -> p h c t", t=CH)[:, :, :, CH - 1], func=AF.Exp)
        k.V("tensor_tensor", r=[k.t_rs4, t_kt], w=[t_eb], out=eb[:, :, 0:TT].rearrange("p h (c t) -> p h c t", t=CH),
            in0=eb[:, :, 0:TT].rearrange("p h (c t) -> p h c t", t=CH), in1=ebl.unsqueeze(3).to_broadcast([128, 4, nch, CH]), op=ALU.mult)
        k.E("tensor_tensor", r=[t_ka, t_eb], w=[t_kh], out=kh[:, :, 0:TT], in0=ka[:, :, 0:TT], in1=eb[:, :, 0:TT], op=ALU.mult)
        oA, t_oA = G[0], tG[0]
        NB = nch * 256
        khtok = k.GB[0:64, 0:2048].bitcast(BF16).rearrange("p (c x) -> p c x", c=8)
        t_khtok = tG[4]
        q, tq = k.pquad()
        qv = q[:].bitcast(BF16)
        for c in range(nch):
            for h in range(4):
                k.T("transpose", r=[t_kh, k.t_kc], w=tq, out=qv[0:64, c * 512 + h * 128:c * 512 + (h + 1) * 128], in_=kh[:, h, c * CH:(c + 1) * CH], identity=k.identb[:])
        k.V("tensor_copy", r=tq + [t_eb], w=[t_khtok], out=khtok[:, 0:nch, :], in_=qv[0:64, 0:nch * 512].rearrange("p (c x) -> p c x", c=nch))
        atT = k.BB[0:64, 2048:4096].rearrange("p (c h s) -> p c h s", c=8, h=4)
        t_atT = tB[1]
        q, tq = k.pquad()
        for c in range(nch):
            for h in range(4):
                k.T("matmul", r=[t_kt, t_qt], w=tq, out=q[0:64, (c * 4 + h) * 64:(c * 4 + h + 1) * 64], lhsT=kt[:, h, c * CH:(c + 1) * CH], rhs=qt[:, h, c * CH:(c + 1) * CH], start=True, stop=True)
        k.V("tensor_tensor", r=tq + [k.t_k64], w=[t_atT], out=atT[:, 0:nch].rearrange("p c h s -> p (c h) s"),
            in0=q[0:64, 0:NB].rearrange("p (w s) -> p w s", s=64), in1=k.m_inclU.unsqueeze(1).to_broadcast([64, nch * 4, 64]), op=ALU.mult)
        genB = k.mixB(l, TT, sample, first)
        for c in range(nch):
            cs = slice(c * CH, (c + 1) * CH)
            for _ in range(3):
                next(genB, None)
            if sample:
                k.load_state_A(l, c)
            k.V("tensor_copy", r=[k.t_SA[l]], w=[k.t_SAb], out=k.SAb[:], in_=k.SA[l][:])
            pb, pt = k.pbank()
            for h in range(4):
                k.T("matmul", r=[t_vtok, t_atT], w=[pt], out=pb[:, h * 64:(h + 1) * 64], lhsT=vtok[:, c, h * 128:(h + 1) * 128], rhs=atT[:, c, h, :],
                    start=True, stop=False)
                k.T("matmul", r=[k.t_SAb, t_qt], w=[pt], out=pb[:, h * 64:(h + 1) * 64], lhsT=k.SAb[:, h, :], rhs=qt[:, h, cs], start=False, stop=True)
            k.A("activation", r=[pt], w=[t_oA], out=oA[:, :, cs], in_=pb[:, 0:256].rearrange("p (a b) -> p a b", a=4), func=AF.Copy)
            pb, pt = k.pbank()
            for h in range(4):
                k.T("matmul", r=[t_khtok, t_vtok], w=[pt], out=pb[:, h * 128:(h + 1) * 128], lhsT=khtok[:, c, h * 128:(h + 1) * 128],
                    rhs=vtok[:, c, h * 128:(h + 1) * 128], start=True, stop=True)
            k.V("tensor_tensor", r=[k.t_rs4, k.t_SA[l]], w=[k.t_SA[l]], out=k.SA[l][:], in0=k.SA[l][:],
                in1=k.rs4[:, :, c:c + 1].to_broadcast([128, 4, 128]), op=ALU.mult)
            k.V("tensor_tensor", r=[pt, k.t_SA[l]], w=[k.t_SA[l]], out=k.SA[l][:], in0=k.SA[l][:],
                in1=pb[:, 0:512].rearrange("p (a b) -> p a b", a=4), op=ALU.add)
            if sample:
                k.store_state_A(l, c)
        for _ in genB:
            pass
        k.headnorm_gate(oA, t_oA, gA, t_gA, c0 + CL["a_norm"], k.oT[0], k.t_o[0], TT)

        t_zx = tG[5]
        zxv = k.GB[:, 2048:2048 + 4 * 515].rearrange("p (a b) -> p a b", a=4)
        qT, t_qT = B[3], tB[3]
        kT, t_kT = B[4], tB[4]
        vT, t_vT = B[5], tB[5]
        gC, t_gC = B[2], tB[2]
        for gi, (wn, dst, t_dst) in enumerate((("cq", qT, t_qT), ("ck", kT, t_kT), ("cv", vT, t_vT))):
            wt, tw = k.wload(l, wn)
            hal = k.halC[l][:, gi * 4:(gi + 1) * 4, :]
            if not sample:
                k.V("tensor_copy", r=[k.t_halC[l]], w=[t_zx], out=zxv[:, :, 0:3], in_=hal)
            k.proj_fm(wt, tw, 4, TT, lambda j, p, pt: k.A("activation", r=[pt], w=[t_zx], out=zxv[:, j, 3:3 + TT], in_=p, func=AF.Copy))
            cv, t_cv = G[0], tG[0]
            if sample:
                for c in range(nch):
                    if gi == 0:
                        k.DMA("gpsimd", k.t_halCs, w=[k.t_halCs], out=k.halCs[c][:], in_=k.s_gdc[l, c])
                    k.conv(zxv, t_zx, k.halCs[c][:, gi * 4:(gi + 1) * 4, :], k.t_halCs, cv, t_cv, 4, c0 + CL["ccw"] + gi * 4, None, c * CH, CH, seq_halo=True, wstride=12)
                    k.DMA("gpsimd", k.outslot, r=[t_zx], out=k.o_gdc[l, 1 + c, :, gi * 4:(gi + 1) * 4, :],
                          in_=zxv[:, :, 3 + (c + 1) * CH - 3:3 + (c + 1) * CH], final=True)
            else:
                k.conv(zxv, t_zx, None, None, cv, t_cv, 4, c0 + CL["ccw"] + gi * 4, None, 0, TT, seq_halo=False, wstride=12)
                k.V("tensor_copy", r=[t_zx], w=[k.t_halC[l]], out=hal, in_=zxv[:, :, TT:TT + 3])
            k.A("activation", r=[t_cv], w=[t_cv], out=cv[:, :, 0:TT], in_=cv[:, :, 0:TT], func=AF.Silu)
            if gi < 2:
                sq = k.BB[:, 0:2048].rearrange("p (a b) -> p a b", a=4)
                tsq = [k.t_B[0]]
                k.A("activation", r=[t_cv], w=tsq, out=sq[:, :, 0:TT], in_=cv[:, :, 0:TT], func=AF.Square)
                k.headsum_rstd(sq, tsq, TT, 1.0)
                if gi == 0:
                    k.V("scalar_tensor_tensor", r=[t_cv, k.t_rs4], w=[t_dst], out=dst[:, :, 0:TT], in0=cv[:, :, 0:TT], scalar=128.0 ** -0.5,
                        in1=k.rs4[:, :, 0:TT], op0=ALU.mult, op1=ALU.mult)
                else:
                    k.V("tensor_tensor", r=[t_cv, k.t_rs4], w=[t_dst], out=dst[:, :, 0:TT], in0=cv[:, :, 0:TT], in1=k.rs4[:, :, 0:TT], op=ALU.mult)
            else:
                k.E("tensor_copy", r=[t_cv], w=[t_dst], out=dst[:, :, 0:TT], in_=cv[:, :, 0:TT])
        wt, tw = k.wload(l, "cg")
        k.proj_fm(wt, tw, 4, TT, lambda j, p, pt: k.A("activation", r=[pt], w=[t_gC], out=gC[:, j, 0:TT], in_=p, func=AF.Silu))
        wba, twba = k.wload(l, "ba")
        oC, t_oC = G[1], tG[1]
        NB = nch * 256
        W = nch * 4
        def gf(i):
            if i < 4:
                return k.GA[0:64, i * 2048:i * 2048 + NB]
            if i == 4:
                return k.GB[0:64, 0:NB]
            return k.GB[0:64, 2048:2048 + NB]
        def w3(ap):
            return ap.rearrange("p (w s) -> p w s", s=64)
        identW = k.m_ident.unsqueeze(1).to_broadcast([64, W, 64])
        pb, pt = k.pbank()
        for c in range(nch):
            for kc in range(8):
                k.T("matmul", r=[k.t_h, twba], w=[pt], out=pb[0:64, c * 8:(c + 1) * 8], lhsT=k.hT[:, kc, c * CH:(c + 1) * CH], rhs=wba[:, kc, 0:8],
                    start=(kc == 0), stop=(kc == 7))
        bgA, t_bgA = k.bgA, k.t_bgA
        p3 = pb[0:64, 0:nch * 8].rearrange("p (c x) -> p c x", x=8)
        b3 = bgA[:, 0:nch, :]
        k.A("activation", r=[pt], w=[t_bgA], out=b3[:, :, 0:4], in_=p3[:, :, 0:4], func=AF.Sigmoid)
        k.V("tensor_tensor", r=[pt, k.t_cst], w=[t_bgA], out=b3[:, :, 4:8], in0=p3[:, :, 4:8],
            in1=cst[0:64, c0 + CL["dtb"]:c0 + CL["dtb"] + 4].unsqueeze(1).to_broadcast([64, nch, 4]), op=ALU.add)
        k.A("activation", r=[t_bgA], w=[t_bgA], out=b3[:, :, 4:8], in_=b3[:, :, 4:8], func=AF.Exp)
        k.V("tensor_scalar", r=[t_bgA], w=[t_bgA], out=b3[:, :, 4:8], in0=b3[:, :, 4:8], scalar1=1.0, scalar2=None, op0=ALU.add)
        k.A("activation", r=[t_bgA], w=[t_bgA], out=b3[:, :, 4:8], in_=b3[:, :, 4:8], func=AF.Ln)
        k.V("tensor_tensor", r=[t_bgA, k.t_der], w=[t_bgA], out=b3[:, :, 4:8], in0=b3[:, :, 4:8],
            in1=k.nega[:, l * 4:l * 4 + 4].unsqueeze(1).to_broadcast([64, nch, 4]), op=ALU.mult)
        pb, pt = k.pbank()
        k.T("matmul", r=[t_bgA, k.t_k64], w=[pt], out=pb[0:64, 0:W].rearrange("p (c h) -> p c h", h=4), lhsT=k.L64, rhs=b3[:, :, 4:8], start=True, stop=True)
        gcs, t_gcs = k.gcs, k.t_gcs
        k.V("tensor_copy", r=[pt], w=[t_gcs], out=gcs[:, 0, 0:W], in_=pb[0:64, 0:W])
        k.V("tensor_copy", r=[t_bgA], w=[t_gcs], out=gcs[:, 1, 0:W].rearrange("p (c h) -> p c h", h=4), in_=b3[:, :, 0:4])
        gcv = gcs[:, 0, 0:W]
        btv = gcs[:, 1, 0:W]
        ktok, t_ktok = k.TK[0], k.t_TK[0]
        vtk = k.GB[0:64, 2048:4096].bitcast(BF16).rearrange("p (c x) -> p c x", c=8)
        t_vtk = tG[5]
        for src, t_src, dst, t_dst in ((kT, t_kT, ktok, t_ktok), (vT, t_vT, vtk, t_vtk)):
            q, tq = k.pquad()
            qv = q[:].bitcast(BF16)
            for c in range(nch):
                for h in range(4):
                    k.T("transpose", r=[t_src, k.t_kc], w=tq, out=qv[0:64, c * 512 + h * 128:c * 512 + (h + 1) * 128], in_=src[:, h, c * CH:(c + 1) * CH], identity=k.identb[:])
            k.V("tensor_copy", r=tq, w=[t_dst], out=dst[:, 0:nch, :], in_=qv[0:64, 0:nch * 512].rearrange("p (c x) -> p c x", c=nch))
        dgG = w3(gf(3))
        k.V("tensor_tensor", r=[t_gcs, k.t_k64], w=[tG[3]], out=dgG, in0=identW, in1=gcv.unsqueeze(2).to_broadcast([64, W, 64]), op=ALU.mult)
        qg, tqg = k.pquad()
        for b in range(NB // 512):
            k.T("matmul", r=[tG[3], k.t_k64], w=tqg, out=qg[:, b * 512:(b + 1) * 512], lhsT=k.ones64, rhs=gf(3)[:, b * 512:(b + 1) * 512], start=True, stop=True)
        egr = k.GA[:, 2 * 2048:2 * 2048 + NB]
        k.A("activation", r=tqg, w=[tG[2]], out=egr, in_=qg[:, 0:NB], func=AF.Exp)
        qtl = k.BB[:, 5 * 2048:5 * 2048 + NB].rearrange("p (c h s) -> p c h s", h=4, s=64)
        t_qtl = tB[5]
        for h in range(4):
            k.E("tensor_tensor", r=[t_qT, tG[2], t_vtk], w=[t_qtl], out=qtl[:, :, h, :], in0=qT[:, h, 0:TT].rearrange("p (c s) -> p c s", s=64),
                in1=egr.rearrange("p (c h s) -> p c h s", h=4, s=64)[:, :, h, :], op=ALU.mult)
        egl, t_egl = k.egl, k.t_egl
        k.V("tensor_copy", r=[tG[2]], w=[t_egl], out=egl[:, 0:W], in_=egr.rearrange("p (w s) -> p w s", s=64)[:, :, CH - 1])
        Em = w3(gf(0))
        k.V("tensor_tensor", r=tqg + [t_gcs], w=[tG[0]], out=Em, in0=w3(qg[0:64, 0:NB]), in1=gcv.unsqueeze(2).to_broadcast([64, W, 64]), op=ALU.subtract)
        k.V("tensor_copy", r=tqg, w=[t_gcs], out=gcs[:, 5, 0:W], in_=w3(qg[0:64, 0:NB])[:, :, CH - 1])
        k.V("tensor_tensor", r=[t_gcs], w=[t_gcs], out=gcs[:, 4, 0:W], in0=gcs[:, 5, 0:W], in1=gcv, op=ALU.subtract)
        k.A("activation", r=[t_gcs], w=[t_gcs], out=gcs[:, 4, 0:W], in_=gcs[:, 4, 0:W], func=AF.Exp)
        k.A("activation", r=[t_gcs], w=[t_gcs], out=gcs[:, 2, 0:W], in_=gcv, func=AF.Exp)
        k.V("tensor_tensor", r=[t_gcs], w=[t_gcs], out=gcs[:, 3, 0:W], in0=gcs[:, 2, 0:W], in1=btv, op=ALU.mult)
        GmU = w3(gf(2)); GmL = Em; GmI = w3(gf(3))
        k.V("tensor_scalar", r=[tG[0], t_qtl, t_egl], w=[tG[2]], out=GmU, in0=Em, scalar1=0.0, scalar2=None, op0=ALU.min)
        k.A("activation", r=[tG[2]], w=[tG[2]], out=GmU, in_=GmU, func=AF.Exp)
        k.V("tensor_scalar", r=[tG[0]], w=[tG[0]], out=GmL, in0=Em, scalar1=0.0, scalar2=None, op0=ALU.max)
        k.A("activation", r=[tG[0]], w=[tG[0]], out=GmL, in_=GmL, func=AF.Exp, scale=-1.0)
        k.E("tensor_tensor", r=[tG[2], k.t_k64], w=[tG[3]], out=GmI, in0=GmU, in1=k.m_inclU.unsqueeze(1).to_broadcast([64, W, 64]), op=ALU.mult)
        k.E("tensor_tensor", r=[tG[2], k.t_k64], w=[tG[2]], out=GmU, in0=GmU, in1=k.m_strictU.unsqueeze(1).to_broadcast([64, W, 64]), op=ALU.mult)
        k.E("tensor_tensor", r=[tG[0], k.t_k64], w=[tG[0]], out=GmL, in0=GmL, in1=k.m_strictL.unsqueeze(1).to_broadcast([64, W, 64]), op=ALU.mult)
        qk_, tqk = k.pquad()
        for c in range(nch):
            for h in range(4):
                w_ = c * 4 + h
                k.T("matmul", r=[t_kT], w=tqk, out=qk_[0:64, w_ * 64:(w_ + 1) * 64], lhsT=kT[:, h, c * CH:(c + 1) * CH], rhs=kT[:, h, c * CH:(c + 1) * CH], start=True, stop=True)
        Pn = w3(gf(0)); Pt = w3(gf(2))
        k.V("tensor_tensor", r=tqk + [tG[0]], w=[tG[0]], out=Pn, in0=w3(qk_[0:64, 0:NB]), in1=GmL, op=ALU.mult)
        k.V("scalar_tensor_tensor", r=[tG[0], t_gcs], w=[tG[0]], out=Pn, in0=Pn, scalar=-1.0, in1=btv.unsqueeze(2).to_broadcast([64, W, 64]), op0=ALU.mult, op1=ALU.mult)
        k.V("tensor_tensor", r=tqk + [tG[2]], w=[tG[2]], out=Pt, in0=w3(qk_[0:64, 0:NB]), in1=GmU, op=ALU.mult)
        qq, tqq = k.pquad()
        for c in range(nch):
            for h in range(4):
                w_ = c * 4 + h
                k.T("matmul", r=[t_kT, t_qT], w=tqq, out=qq[0:64, w_ * 64:(w_ + 1) * 64], lhsT=kT[:, h, c * CH:(c + 1) * CH], rhs=qT[:, h, c * CH:(c + 1) * CH], start=True, stop=True)
        QKm = k.BB[0:64, 0:NB].rearrange("p (c h s) -> p c h s", h=4, s=64)
        t_QKm = tB[0]
        k.V("tensor_tensor", r=tqq + [tG[3]], w=[t_QKm], out=QKm.rearrange("p c h s -> p (c h) s"), in0=w3(qq[0:64, 0:NB]), in1=GmI, op=ALU.mult)
        dgB = w3(gf(4))
        k.V("tensor_tensor", r=[t_gcs, k.t_k64], w=[tG[4]], out=dgB, in0=identW, in1=btv.unsqueeze(2).to_broadcast([64, W, 64]), op=ALU.mult)
        qb, tqb = k.pquad()
        for b in range(NB // 512):
            k.T("matmul", r=[tG[4], k.t_k64], w=tqb, out=qb[0:64, b * 512:(b + 1) * 512], lhsT=k.ones64[:, 0:64], rhs=gf(4)[:, b * 512:(b + 1) * 512], start=True, stop=True)
        k.V("scalar_tensor_tensor", r=tqb + [tG[2]], w=[tG[2]], out=Pt, in0=Pt, scalar=-1.0, in1=w3(qb[0:64, 0:NB]), op0=ALU.mult, op1=ALU.mult)
        Xt = w3(gf(3)); IP = w3(gf(4))
        k.V("tensor_tensor", r=[tG[2], k.t_k64, t_QKm], w=[tG[3]], out=Xt, in0=Pt, in1=identW, op=ALU.add)
        for lev in range(1, 6):
            qn, tqn = k.pquad()
            for w_ in range(W):
                k.T("matmul", r=[tG[0], tG[2]], w=tqn, out=qn[0:64, w_ * 64:(w_ + 1) * 64], lhsT=gf(2)[:, w_ * 64:(w_ + 1) * 64], rhs=gf(0)[:, w_ * 64:(w_ + 1) * 64], start=True, stop=True)
            if lev < 5:
                qt_, tqt = k.pquad()
                for w_ in range(W):
                    k.T("matmul", r=[tG[0], tG[2]], w=tqt, out=qt_[0:64, w_ * 64:(w_ + 1) * 64], lhsT=gf(0)[:, w_ * 64:(w_ + 1) * 64], rhs=gf(2)[:, w_ * 64:(w_ + 1) * 64], start=True, stop=True)
            k.V("tensor_tensor", r=tqn + [k.t_k64], w=[tG[4]], out=IP, in0=w3(qn[0:64, 0:NB]), in1=identW, op=ALU.add)
            if lev < 5:
                k.A("activation", r=tqn, w=[tG[0]], out=gf(0), in_=qn[0:64, 0:NB], func=AF.Copy)
                k.A("activation", r=tqt, w=[tG[2]], out=gf(2), in_=qt_[0:64, 0:NB], func=AF.Copy)
            qx, tqx = k.pquad()
            for w_ in range(W):
                k.T("matmul", r=[tG[4], tG[3]], w=tqx, out=qx[0:64, w_ * 64:(w_ + 1) * 64], lhsT=gf(4)[:, w_ * 64:(w_ + 1) * 64], rhs=gf(3)[:, w_ * 64:(w_ + 1) * 64], start=True, stop=True)
            k.V("tensor_copy", r=tqx, w=[tG[3]], out=gf(3), in_=qx[0:64, 0:NB])
        Xtb = k.BB[0:64, 3 * 2048:3 * 2048 + NB]
        t_Xtb = tB[3]
        k.V("tensor_copy", r=[tG[3], t_qtl], w=[t_Xtb], out=Xtb, in_=gf(3))
        k.E("tensor_tensor", r=[t_vtk, t_gcs], w=[t_vtk], out=vtk[:, 0:nch, :].rearrange("p c (h v) -> p (c h) v", h=4),
            in0=vtk[:, 0:nch, :].rearrange("p c (h v) -> p (c h) v", h=4), in1=btv.unsqueeze(2).to_broadcast([64, W, 128]), op=ALU.mult)
        kbg = k.GA[0:64, 0:2048].bitcast(BF16).rearrange("p (c x) -> p c x", c=8)
        khc = k.GA[0:64, 4096:6144].bitcast(BF16).rearrange("p (c x) -> p c x", c=8)
        k.E("tensor_tensor", r=[t_ktok, t_gcs, tG[0]], w=[tG[0]], out=kbg[:, 0:nch, :].rearrange("p c (h v) -> p (c h) v", h=4),
            in0=ktok[:, 0:nch, :].rearrange("p c (h v) -> p (c h) v", h=4), in1=gcs[:, 3, 0:W].unsqueeze(2).to_broadcast([64, W, 128]), op=ALU.mult)
        k.E("tensor_tensor", r=[t_ktok, t_gcs, tG[2]], w=[tG[2]], out=khc[:, 0:nch, :].rearrange("p c (h v) -> p (c h) v", h=4),
            in0=ktok[:, 0:nch, :].rearrange("p c (h v) -> p (c h) v", h=4), in1=gcs[:, 4, 0:W].unsqueeze(2).to_broadcast([64, W, 128]), op=ALU.mult)
        uu = k.GB[0:64, 0:2048].bitcast(BF16).rearrange("p (c x) -> p c x", c=8)
        t_uu = tG[4]
        for half in range((nch + 3) // 4):
            qu, tqu = k.pquad()
            ncl = min(4, nch - half * 4)
            for cl in range(ncl):
                c = half * 4 + cl
                for h in range(4):
                    k.T("matmul", r=[t_Xtb, t_vtk], w=tqu, out=qu[0:64, cl * 512 + h * 128:cl * 512 + (h + 1) * 128], lhsT=Xtb[:, (c * 4 + h) * 64:(c * 4 + h + 1) * 64],
                        rhs=vtk[:, c, h * 128:(h + 1) * 128], start=True, stop=True)
            k.A("activation", r=tqu, w=[t_uu], out=uu[:, half * 4:half * 4 + ncl, :], in_=qu[0:64, 0:ncl * 512].rearrange("p (c x) -> p c x", x=512), func=AF.Copy)
        wTc = k.BB[:, 2048:2048 + NB].rearrange("p (c h s) -> p c h s", h=4, s=64)
        t_wTc = tB[1]
        qw, tqw = k.pquad()
        for c in range(nch):
            for h in range(4):
                w_ = c * 4 + h
                k.T("matmul", r=[t_Xtb, tG[0]], w=tqw, out=qw[:, w_ * 64:(w_ + 1) * 64], lhsT=kbg[:, c, h * 128:(h + 1) * 128], rhs=Xtb[:, w_ * 64:(w_ + 1) * 64], start=True, stop=True)
        k.A("activation", r=tqw, w=[t_wTc], out=wTc.rearrange("p c h s -> p (c h s)"), in_=qw[:, 0:NB], func=AF.Copy)
        for c in range(nch):
            cs = slice(c * CH, (c + 1) * CH)
            pi = c % 2
            if sample:
                k.load_state_C(l, c)
            k.A("activation", r=[k.t_SC[l]], w=[k.t_SCb], out=k.SCb[:], in_=k.SC[l][:], func=AF.Copy)
            k.G("tensor_tensor", r=[t_egl, k.t_SC[l]], w=[k.t_SC[l]], out=k.SC[l][:], in0=k.SC[l][:],
                in1=egl[:, c * 4:(c + 1) * 4].unsqueeze(2).to_broadcast([128, 4, 128]), op=ALU.mult)
            pv, ptv = k.pbank()
            for h in range(4):
                k.T("matmul", r=[t_wTc, k.t_SCb], w=[ptv], out=pv[0:64, h * 128:(h + 1) * 128], lhsT=wTc[:, c, h, :], rhs=k.SCb[:, h, :], start=True, stop=True)
            vn, t_vn = k.vn[pi], k.t_vn[pi]
            k.V("tensor_tensor", r=[ptv, t_uu], w=[t_vn], out=vn[:], in0=uu[:, c, :].rearrange("p (a b) -> p a b", a=4),
                in1=pv[0:64, 0:512].rearrange("p (a b) -> p a b", a=4), op=ALU.subtract)
            po, pto = k.pbank()
            for h in range(4):
                k.T("matmul", r=[k.t_SCb, t_qtl], w=[pto], out=po[:, h * 64:(h + 1) * 64], lhsT=k.SCb[:, h, :], rhs=qtl[:, c, h, :], start=True, stop=False)
                k.T("matmul", r=[t_vn, t_QKm], w=[pto], out=po[:, h * 64:(h + 1) * 64], lhsT=vn[:, h, :], rhs=QKm[:, c, h, :], start=False, stop=True)
            k.A("activation", r=[pto], w=[t_oC], out=oC[:, :, cs], in_=po[:, 0:256].rearrange("p (a b) -> p a b", a=4), func=AF.Copy)
            psn, ptsn = k.pbank()
            for h in range(4):
                k.T("matmul", r=[tG[2], t_vn], w=[ptsn], out=psn[:, h * 128:(h + 1) * 128], lhsT=khc[:, c, h * 128:(h + 1) * 128], rhs=vn[:, h, :], start=True, stop=True)
            k.V("tensor_tensor", r=[ptsn, k.t_SC[l]], w=[k.t_SC[l]], out=k.SC[l][:], in0=k.SC[l][:],
                in1=psn[:, 0:512].rearrange("p (a b) -> p a b", a=4), op=ALU.add)
            if sample:
                k.store_state_C(l, c)
        k.headnorm_gate(oC, t_oC, gC, t_gC, c0 + CL["c_norm"], k.oT[2], k.t_o[2], TT)

        mbT = k.BB[:, 2 * 2048:4 * 2048].rearrange("p (a b) -> p a b", a=8)
        t_mb = [tB[2], tB[3]]
        for j in range(8):
            wg, twg = k.wload(l, "mg%d" % j)
            wr, twr = k.wload(l, "br%d" % j)
            acc, t_acc = k.tmpa[j % 2], k.t_tmpa[j % 2]
            gb = [k.pbank() for _ in range(3)]
            wbk = [k.pbank() for _ in range(3)]
            for kc in range(8):
                for b in range(3):
                    pgt, ptgt = gb[b]
                    k.T("matmul", r=[k.t_h, twg], w=[ptgt], out=pgt[:, 0:TT], lhsT=wg[:, kc, b * 128:(b + 1) * 128], rhs=k.hT[:, kc, 0:TT], start=(kc == 0), stop=(kc == 7))
            for kc in range(4):
                for b in range(3):
                    pwd, ptwd = wbk[b]
                    k.T("matmul", r=[k.t_o[b], twr], w=[ptwd], out=pwd[:, 0:TT], lhsT=wr[:, b * 4 + kc, :], rhs=k.oT[b][:, kc, 0:TT], start=(kc == 0), stop=(kc == 3))
            for b in range(3):
                pgt, ptgt = gb[b]
                pwd, ptwd = wbk[b]
                sg, t_sg = k.tmpb[b], k.t_tmpb[b]
                k.A("activation", r=[ptgt], w=[t_sg], out=sg[:, 0:TT], in_=pgt[:, 0:TT], func=AF.Sigmoid)
                if b == 0:
                    k.V("tensor_tensor", r=[ptwd, t_sg], w=[t_acc], out=acc[:, 0:TT], in0=pwd[:, 0:TT], in1=sg[:, 0:TT], op=ALU.mult)
                else:
                    t2, t_t2 = k.tmpa[2], k.t_tmpa[2]
                    k.V("tensor_tensor", r=[ptwd, t_sg], w=[t_t2], out=t2[:, 0:TT], in0=pwd[:, 0:TT], in1=sg[:, 0:TT], op=ALU.mult)
                    if b == 1:
                        k.V("tensor_tensor", r=[t_t2, t_acc], w=[t_acc], out=acc[:, 0:TT], in0=acc[:, 0:TT], in1=t2[:, 0:TT], op=ALU.add)
                    else:
                        k.V("tensor_tensor", r=[t_t2, t_acc], w=t_mb, out=mbT[:, j, 0:TT], in0=acc[:, 0:TT], in1=t2[:, 0:TT], op=ALU.add)
        yT = k.GA[:, 0:4096].rearrange("p (a b) -> p a b", a=8)
        ty = [tG[0], tG[1]]
        for half in range(2):
            wt, tw = k.wload(l, "out%d" % half)
            banks = [k.pbank() for _ in range(4)]
            for kc in range(8):
                for j in range(4):
                    pb, pt = banks[j]
                    k.T("matmul", r=t_mb + [tw], w=[pt], out=pb[:, 0:TT], lhsT=wt[:, kc, j * 128:(j + 1) * 128], rhs=mbT[:, kc, 0:TT], start=(kc == 0), stop=(kc == 7))
            for j in range(4):
                pb, pt = banks[j]
                k.A("activation", r=[pt], w=ty, out=yT[:, half * 4 + j, 0:TT], in_=pb[:, 0:TT], func=AF.Copy)
        k.postnorm_add("g_post", l, TT)

        k.prenorm("g_pmlp", l, TT)
        aT = k.GA[:].bitcast(BF16).rearrange("p (a b) -> p a b", a=32)
        t_a = [tG[0], tG[1], tG[2], tG[3]]
        t_aT = k.t_aT
        k.V("memset", w=t_a + t_aT, ap=k.fence[:], constant=0.0)
        for u in range(8):
            wt, tw = k.wload(l, "up%d" % u)
            banks = [k.pbank() for _ in range(4)]
            for kc in range(8):
                for j in range(4):
                    pb, pt = banks[j]
                    k.T("matmul", r=[k.t_h, tw], w=[pt], out=pb[:, 0:TT], lhsT=wt[:, kc, j * 128:(j + 1) * 128], rhs=k.hT[:, kc, 0:TT], start=(kc == 0), stop=(kc == 7))
            for j in range(4):
                pb, pt = banks[j]
                rl, t_rl = k.tmpa[j % 2], k.t_tmpa[j % 2]
                k.A("activation", r=[pt], w=[t_rl], out=rl[:, 0:TT], in_=pb[:, 0:TT], func=AF.Relu)
                k.E("tensor_tensor", r=[t_rl], w=[t_aT[u * 4 + j]], out=aT[:, u * 4 + j, 0:TT], in0=rl[:, 0:TT], in1=rl[:, 0:TT], op=ALU.mult)
        yT2 = k.GB[:, 0:4096].rearrange("p (a b) -> p a b", a=8)
        ty2 = [tG[4], tG[5]]
        for half in range(2):
            banks = [k.pbank() for _ in range(4)]
            for g in range(8):
                wt, tw = k.wload(l, "dn%d_%d" % (half, g))
                for kc in range(4):
                    for j in range(4):
                        pb, pt = banks[j]
                        k.T("matmul", r=[t_aT[g * 4 + kc], tw], w=[pt], out=pb[:, 0:TT], lhsT=wt[:, kc, j * 128:(j + 1) * 128], rhs=aT[:, g * 4 + kc, 0:TT],
                            start=(g == 0 and kc == 0), stop=(g == 7 and kc == 3))
            for j in range(4):
                pb, pt = banks[j]
                k.A("activation", r=[pt], w=ty2, out=yT2[:, half * 4 + j, 0:TT], in_=pb[:, 0:TT], func=AF.Copy)
        k.E("tensor_copy", r=ty2, w=t_a + t_aT, out=k.GA[:, 0:4096], in_=k.GB[:, 0:4096])
        k.postnorm_add("g_postmlp", l, TT)

    def mixB(self, l, TT, sample, first):
        k = self
        nch = TT // CH
        c0 = l * NCL
        d0 = l * 16
        cst = k.cst
        G = [k.Gv(i) for i in range(6)]
        tG = k.t_G
        B = [k.Bv(i) for i in range(6)]
        tB = k.t_B
        zx, t_zx = k.Gv(5), tG[5]
        zxv = k.GB[:, 2048:2048 + 4 * 515].rearrange("p (a b) -> p a b", a=4)
        gB, t_gB = B[0], tB[0]
        wt, tw = k.wload(l, "bx")
        if sample:
            pass
        else:
            k.V("tensor_copy", r=[k.t_halB[l]], w=[t_zx], out=zxv[:, :, 0:3], in_=k.halB[l][:])
        k.proj_fm(wt, tw, 4, TT, lambda j, p, pt: k.A("activation", r=[pt], w=[t_zx], out=zxv[:, j, 3:3 + TT], in_=p, func=AF.Copy))
        yield
        wt, tw = k.wload(l, "bg")
        k.proj_fm(wt, tw, 4, TT, lambda j, p, pt: k.A("activation", r=[pt], w=[t_gB], out=gB[:, j, 0:TT], in_=p, func=AF.Gelu))
        yield
        k.DMA("gpsimd", k.t_gwb, w=[k.t_gwb], out=k.gwb[:], in_=k.gw_d[:, l * 1024:(l + 1) * 1024])
        xc, t_xc = G[2], tG[2]
        if sample:
            for c in range(nch):
                k.DMA("gpsimd", k.t_halB[l], w=[k.t_halB[l]], out=k.halB[l][:], in_=k.s_rgc[l, c])
                k.conv(zxv, t_zx, k.halB[l], k.t_halB[l], xc, t_xc, 4, c0 + CL["bcw"], c0 + CL["bcb"], c * CH, CH, seq_halo=True)
                k.DMA("gpsimd", k.outslot, r=[t_zx], out=k.o_rgc[l, 1 + c], in_=zxv[:, :, 3 + (c + 1) * CH - 3:3 + (c + 1) * CH], final=True)
        else:
            for cc_ in range(4):
                k.conv(zxv, t_zx, None, None, xc, t_xc, 4, c0 + CL["bcw"], c0 + CL["bcb"], 0, TT, seq_halo=False, ccs=[cc_])
                yield
            k.V("tensor_copy", r=[t_zx], w=[k.t_halB[l]], out=k.halB[l][:], in_=zxv[:, :, TT:TT + 3])
        xcb, t_xcb = k.GA[:, 2048:4096].bitcast(BF16)[:, 0:2048].rearrange("p (a b) -> p a b", a=4), tG[1]
        k.E("tensor_copy", r=[t_xc], w=[t_xcb], out=xcb[:, :, 0:TT], in_=xc[:, :, 0:TT])
        hs, t_hs = G[3], tG[3]
        for cc in range(4):
            r_, i_, a_ = k.tmpa[0], k.tmpa[1], k.tmpa[2]
            tr, ti_, ta = k.t_tmpa
            pb, pt = k.pbank()
            k.T("matmul", r=[t_xcb, k.t_gwb], w=[pt], out=pb[:, 0:TT], lhsT=k.gwb[:, cc * 128:(cc + 1) * 128], rhs=xcb[:, cc, 0:TT], start=True, stop=True)
            k.A("activation", r=[pt, k.t_cst], w=[tr], out=r_[:, 0:TT], in_=pb[:, 0:TT], func=AF.Sigmoid, bias=cst[:, c0 + CL["gab"] + cc:c0 + CL["gab"] + cc + 1])
            pb, pt = k.pbank()
            k.T("matmul", r=[t_xcb, k.t_gwb], w=[pt], out=pb[:, 0:TT], lhsT=k.gwb[:, 512 + cc * 128:512 + (cc + 1) * 128], rhs=xcb[:, cc, 0:TT], start=True, stop=True)
            k.A("activation", r=[pt, k.t_cst], w=[ti_], out=i_[:, 0:TT], in_=pb[:, 0:TT], func=AF.Sigmoid, bias=cst[:, c0 + CL["gxb"] + cc:c0 + CL["gxb"] + cc + 1])
            yield
            k.A("activation", r=[tr, k.t_der], w=[ta], out=a_[:, 0:TT], in_=r_[:, 0:TT], func=AF.Exp, scale=k.der[:, d0 + 8 + cc:d0 + 9 + cc])
            k.A("activation", r=[tr, k.t_der], w=[tr], out=r_[:, 0:TT], in_=r_[:, 0:TT], func=AF.Exp, scale=k.der[:, d0 + 12 + cc:d0 + 13 + cc])
            k.V("tensor_scalar", r=[tr], w=[tr], out=r_[:, 0:TT], in0=r_[:, 0:TT], scalar1=-1.0, scalar2=1.0, op0=ALU.mult, op1=ALU.add)
            k.A("activation", r=[tr], w=[tr], out=r_[:, 0:TT], in_=r_[:, 0:TT], func=AF.Sqrt)
            if first:
                k.V("memset", w=[tr], ap=r_[:, 0:1], constant=1.0)
            yield
            k.V("tensor_tensor", r=[tr, ti_], w=[ti_], out=i_[:, 0:TT], in0=i_[:, 0:TT], in1=r_[:, 0:TT], op=ALU.mult)
            k.V("tensor_tensor", r=[ti_, t_xc], w=[ti_], out=i_[:, 0:TT], in0=i_[:, 0:TT], in1=xc[:, cc, 0:TT], op=ALU.mult)
            if sample:
                for c in range(nch):
                    if cc == 0:
                        k.DMA("gpsimd", k.t_hBs, w=[k.t_hBs], out=k.hBs[c][:], in_=k.s_rg[l, c])
                    k.V("tensor_tensor_scan", r=[ta, ti_, k.t_hBs], w=[t_hs], out=hs[:, cc, c * CH:(c + 1) * CH], data0=a_[:, c * CH:(c + 1) * CH],
                        data1=i_[:, c * CH:(c + 1) * CH], initial=k.hBs[c][:, cc:cc + 1], op0=ALU.mult, op1=ALU.add)
            else:
                k.V("tensor_tensor_scan", r=[ta, ti_, k.t_hB[l]], w=[t_hs], out=hs[:, cc, 0:TT], data0=a_[:, 0:TT], data1=i_[:, 0:TT],
                    initial=k.hB[l][:, cc:cc + 1], op0=ALU.mult, op1=ALU.add)
                k.V("tensor_copy", r=[t_hs], w=[k.t_hB[l]], out=k.hB[l][:, cc:cc + 1], in_=hs[:, cc, TT - 1:TT])
            yield
        if sample:
            for c in range(nch):
                k.V("tensor_copy", r=[t_hs], w=[k.t_hfin], out=k.hfin[:], in_=hs[:, :, (c + 1) * CH - 1])
                k.DMA("gpsimd", k.outslot, r=[k.t_hfin], out=k.o_rg[l, 1 + c], in_=k.hfin[:], final=True)
        k.E("tensor_tensor", r=[t_hs, t_gB], w=[k.t_o[1]], out=k.oT[1][:, :, 0:TT], in0=hs[:, :, 0:TT], in1=gB[:, :, 0:TT], op=ALU.mult)


    def conv(self, zxv, t_zx, hal, t_hal, out, t_out, ncc, wcol, bcol, t0, n, seq_halo, wstride=4, ccs=None):
        k = self
        cst = k.cst
        for cc in (range(ncc) if ccs is None else ccs):
            o = out[:, cc, t0:t0 + n]
            def wv(j):
                return cst[:, wcol + j * wstride + cc:wcol + j * wstride + cc + 1]
            rds = [t_zx, k.t_cst]
            if bcol is not None:
                k.V("tensor_scalar", r=rds, w=[t_out], out=o, in0=zxv[:, cc, 3 + t0:3 + t0 + n], scalar1=wv(3), scalar2=cst[:, bcol + cc:bcol + cc + 1],
                    op0=ALU.mult, op1=ALU.add)
            else:
                k.V("tensor_scalar", r=rds, w=[t_out], out=o, in0=zxv[:, cc, 3 + t0:3 + t0 + n], scalar1=wv(3), scalar2=None, op0=ALU.mult)
            for j in range(3):
                sh = 3 - j
                if not seq_halo:
                    k.V("scalar_tensor_tensor", r=rds + [t_out], w=[t_out], out=o, in0=zxv[:, cc, 3 + t0 - sh:3 + t0 - sh + n], scalar=wv(j), in1=o,
                        op0=ALU.mult, op1=ALU.add)
                else:
                    k.V("scalar_tensor_tensor", r=rds + [t_out], w=[t_out], out=out[:, cc, t0 + sh:t0 + n], in0=zxv[:, cc, 3 + t0:3 + t0 + n - sh], scalar=wv(j),
                        in1=out[:, cc, t0 + sh:t0 + n], op0=ALU.mult, op1=ALU.add)
                    k.V("scalar_tensor_tensor", r=rds + [t_out, t_hal], w=[t_out], out=out[:, cc, t0:t0 + sh], in0=hal[:, cc, 3 - sh:3], scalar=wv(j),
                        in1=out[:, cc, t0:t0 + sh], op0=ALU.mult, op1=ALU.add)

    def load_state_A(self, l, c):
        k = self
        k.DMA("gpsimd", k.t_SA[l], w=[k.t_SA[l]], out=k.SA[l][:], in_=k.s_hg[l, c].rearrange("h k v -> k h v"))

    def store_state_A(self, l, c):
        k = self
        k.DMA("gpsimd", k.outslot, r=[k.t_SA[l]], out=k.o_hg[l, 1 + c].rearrange("h k v -> k h v"), in_=k.SA[l][:], final=True)

    def load_state_C(self, l, c):
        k = self
        k.DMA("gpsimd", k.t_SC[l], w=[k.t_SC[l]], out=k.SC[l][:], in_=k.s_gd[l, c].rearrange("h k v -> k h v"))

    def store_state_C(self, l, c):
        k = self
        k.DMA("gpsimd", k.outslot, r=[k.t_SC[l]], out=k.o_gd[l, 1 + c].rearrange("h k v -> k h v"), in_=k.SC[l][:], final=True)

    def store_prompt_states(self):
        k = self
        for l in range(DEPTH):
            k.DMA("gpsimd", k.outslot, r=[k.t_SA[l]], out=k.o_hg[l, 0].rearrange("h k v -> k h v"), in_=k.SA[l][:], final=True)
            k.DMA("gpsimd", k.outslot, r=[k.t_SC[l]], out=k.o_gd[l, 0].rearrange("h k v -> k h v"), in_=k.SC[l][:], final=True)
            k.DMA("gpsimd", k.outslot, r=[k.t_hB[l]], out=k.o_rg[l, 0], in_=k.hB[l][:], final=True)
            k.DMA("gpsimd", k.outslot, r=[k.t_halB[l]], out=k.o_rgc[l, 0], in_=k.halB[l][:], final=True)
            k.DMA("gpsimd", k.outslot, r=[k.t_halC[l]], out=k.o_gdc[l, 0], in_=k.halC[l][:], final=True)

    def build(self):
        k = self
        k.alloc()
        k.hBs = [k.sb("hBs%d" % c, [128, 4], F32) for c in range(NSEQ)]
        k.halCs = [k.sb("halCs%d" % c, [128, 12, 3], F32) for c in range(NSEQ)]
        k.t_hBs = Buf("hBs"); k.t_halCs = Buf("halCs")
        k.hfin = k.sb("hfin", [128, 4], F32); k.t_hfin = Buf("hfin")
        k.setup()
        k.preconvert()
        xslot = Buf("xslot")
        for ti in range(k.npt):
            k.DMA("gpsimd", xslot, w=[k.t_x], out=k.xT[:], in_=k.xp[:, :, ti * 512:(ti + 1) * 512])
            for l in range(DEPTH):
                k.block(ti, l, 512, False)
            k.DMA("gpsimd", k.outslot, r=[k.t_x], out=k.yp[:, :, ti * 512:(ti + 1) * 512], in_=k.xT[:], final=True)
        k.store_prompt_states()
        if k.do_sample:
            TT = NSEQ * CH
            k.DMA("gpsimd", xslot, w=[k.t_x], out=k.xT[:, :, 0:TT], in_=k.xs[:, :, :])
            for l in range(DEPTH):
                k.block(0, l, TT, True)
            k.DMA("gpsimd", k.outslot, r=[k.t_x], out=k.ys[:, :, :], in_=k.xT[:, :, 0:TT], final=True)
        k.P.emit()
        k.P.close()
        return k.nc


def _wtile(src, kc, n0, nw, k0=0):
    blk = src[k0:k0 + kc * 128, n0:n0 + nw].reshape(kc, 128, nw)
    return np.ascontiguousarray(blk.transpose(1, 0, 2)).reshape(128, kc * nw)


def _layout_weights(w_in, w_br_a, w_br_b, w_br_c, w_out, w_up, w_down):
    wf = np.zeros((DEPTH, 128, WCOLS_PAD), np.float32)
    col_in = dict(aq=0, af=512, ai=1024, ag=1536, bx=2048, bg=2560, cq=3072, ck=3584, cv=4096, cg=4608)
    for l in range(DEPTH):
        for name, kc, nw, off in WT:
            if name in col_in:
                t = _wtile(w_in[l], 8, col_in[name], 512)
            elif name == "ba":
                t = _wtile(w_in[l], 8, 5120, 8)
            elif name.startswith("mg"):
                j = int(name[2:])
                parts = [w_in[l][:, 5128 + b * 1024 + j * 128:5128 + b * 1024 + (j + 1) * 128] for b in range(3)]
                t = _wtile(np.concatenate(parts, axis=1), 8, 0, 384)
            elif name.startswith("br"):
                j = int(name[2:])
                parts = [w[l][:, j * 128:(j + 1) * 128] for w in (w_br_a, w_br_b, w_br_c)]
                t = _wtile(np.concatenate(parts, axis=0), 12, 0, 128)
            elif name.startswith("out"):
                t = _wtile(w_out[l], 8, int(name[3:]) * 512, 512)
            elif name.startswith("up"):
                t = _wtile(w_up[l], 8, int(name[2:]) * 512, 512)
            else:
                h, g = name[2:].split("_")
                t = _wtile(w_down[l], 4, int(h) * 512, 512, k0=int(g) * 512)
            wf[l, :, off:off + kc * nw] = t
    return wf


def _fm(v, c):
    return np.ascontiguousarray(v.reshape(c, 128).T)


def _consts(inp):
    cst = np.zeros((128, DEPTH * NCL), np.float32)
    for l in range(DEPTH):
        b = l * NCL
        cst[:, b + CL["g_pre"]:b + CL["g_pre"] + 8] = _fm(inp["norm_pre_mix"][l], 8)
        cst[:, b + CL["g_post"]:b + CL["g_post"] + 8] = _fm(inp["norm_post_mix"][l], 8)
        cst[:, b + CL["g_pmlp"]:b + CL["g_pmlp"] + 8] = _fm(inp["norm_pre_mlp"][l], 8)
        cst[:, b + CL["g_postmlp"]:b + CL["g_postmlp"] + 8] = _fm(inp["norm_post_mlp"][l], 8)
        cst[:, b + CL["a_norm"]] = inp["a_norm"][l]
        cst[:, b + CL["c_norm"]] = inp["c_norm"][l]
        lr = np.stack([_fm(inp["lb_raw"][ll], 4) for ll in range(DEPTH)], axis=2)
        cst[:, b + CL["lbraw"]:b + CL["lbraw"] + 16] = lr.reshape(128, 16)
        for j in range(4):
            cst[:, b + CL["bcw"] + j * 4:b + CL["bcw"] + j * 4 + 4] = _fm(inp["b_conv_w"][l, j], 4)
            cst[:, b + CL["ccw"] + j * 12:b + CL["ccw"] + j * 12 + 12] = _fm(inp["c_conv_w"][l, j], 12)
        cst[:, b + CL["bcb"]:b + CL["bcb"] + 4] = _fm(inp["b_conv_b"][l], 4)
        cst[:, b + CL["gab"]:b + CL["gab"] + 4] = _fm(inp["b_gate_a_b"][l], 4)
        cst[:, b + CL["gxb"]:b + CL["gxb"] + 4] = _fm(inp["b_gate_x_b"][l], 4)
        cst[:, b + CL["lam"]:b + CL["lam"] + 4] = _fm(inp["b_lambda"][l], 4)
        cst[:, b + CL["alog"]:b + CL["alog"] + 4] = inp["c_a_log"][l][None, :]
        cst[:, b + CL["dtb"]:b + CL["dtb"] + 4] = inp["c_dt_bias"][l][None, :]
    gw = np.zeros((128, DEPTH * 1024), np.float32)
    for l in range(DEPTH):
        for wi, w in enumerate((inp["b_gate_a_w"][l], inp["b_gate_x_w"][l])):
            for cc in range(4):
                m = np.zeros((128, 128), np.float32)
                m[0:64, 0:64] = w[2 * cc]
                m[64:128, 64:128] = w[2 * cc + 1]
                gw[:, l * 1024 + wi * 512 + cc * 128:l * 1024 + wi * 512 + (cc + 1) * 128] = m
    k128 = np.zeros((128, 768), np.float32)
    k128[:, 0:128] = np.eye(128, dtype=np.float32)
    k128[:, 128:256] = 1.0
    sm = np.ones(512, np.float32)
    sm[::CH] = 0.0
    k128[:, 256:768] = sm[None, :]
    k64 = np.zeros((64, 896), np.float32)
    r = np.arange(64)
    k64[:, 0:64] = (r[:, None] <= r[None, :]).astype(np.float32)
    k64[:, 64:192] = 1.0
    strictL = (r[:, None] > r[None, :]).astype(np.float32)
    strictU = (r[:, None] < r[None, :]).astype(np.float32)
    inclU = (r[:, None] <= r[None, :]).astype(np.float32)
    k64[:, 192:256] = strictL
    k64[:, 256:320] = strictU
    k64[:, 320:384] = inclU
    k64[:, 384:896] = np.tile(np.eye(64, dtype=np.float32), (1, 8))
    return cst, gw, k128, k64


_NC_CACHE = {}
_LAST = {}


def _get_nc(npt, do_sample=True):
    key = (npt, do_sample)
    if key not in _NC_CACHE:
        kb = KB(npt, do_sample)
        _NC_CACHE[key] = kb.build()
        _LAST["streams"] = kb.P.streams
    return _NC_CACHE[key]


def kernel(x_prompt, x_sample, state_hgrn, state_rglru, state_rglru_conv, state_gdn, state_gdn_conv,
           lb_raw, norm_pre_mix, norm_post_mix, norm_pre_mlp, norm_post_mlp, w_in, a_norm,
           b_conv_w, b_conv_b, b_gate_a_w, b_gate_a_b, b_gate_x_w, b_gate_x_b, b_lambda,
           c_conv_w, c_a_log, c_dt_bias, c_norm, w_br_a, w_br_b, w_br_c, w_out, w_up, w_down, _npt=None):
    f = lambda a: np.asarray(a, dtype=np.float32)
    inp = dict(lb_raw=f(lb_raw), norm_pre_mix=f(norm_pre_mix), norm_post_mix=f(norm_post_mix), norm_pre_mlp=f(norm_pre_mlp),
               norm_post_mlp=f(norm_post_mlp), a_norm=f(a_norm), b_conv_w=f(b_conv_w), b_conv_b=f(b_conv_b),
               b_gate_a_w=f(b_gate_a_w), b_gate_a_b=f(b_gate_a_b), b_gate_x_w=f(b_gate_x_w), b_gate_x_b=f(b_gate_x_b),
               b_lambda=f(b_lambda), c_conv_w=f(c_conv_w), c_a_log=f(c_a_log), c_dt_bias=f(c_dt_bias), c_norm=f(c_norm))
    x_prompt = f(x_prompt); x_sample = f(x_sample)
    seq = x_prompt.shape[1]
    npt = seq // 512 if _npt is None else _npt
    ntok = npt * 512
    import time as _t, sys as _s
    _t0 = _t.time()
    nc = _get_nc(npt)
    print("[kernel] build %.1fs, ninstr=%s" % (_t.time() - _t0, {e: len(v) for e, v in _LAST.get("streams", {}).items()}), file=_s.stderr)
    wf = _layout_weights(f(w_in), f(w_br_a), f(w_br_b), f(w_br_c), f(w_out), f(w_up), f(w_down))
    cst, gw, k128, k64 = _consts(inp)
    xp = np.ascontiguousarray(x_prompt[0, :ntok].reshape(ntok, 8, 128).transpose(2, 1, 0))
    state_hgrn = f(state_hgrn); state_rglru = f(state_rglru); state_rglru_conv = f(state_rglru_conv)
    state_gdn = f(state_gdn); state_gdn_conv = f(state_gdn_conv)
    in_maps = []
    for c in range(NCORE):
        sl = slice(c * NSEQ, (c + 1) * NSEQ)
        xs = x_sample[sl].reshape(NSEQ * CH, 8, 128).transpose(2, 1, 0)
        in_maps.append(dict(
            xp=xp, xs=np.ascontiguousarray(xs), wf=wf, cst=cst, gw=gw, k128=k128, k64=k64,
            s_hg=np.ascontiguousarray(state_hgrn[:, sl]),
            s_rg=np.ascontiguousarray(state_rglru[:, sl].reshape(DEPTH, NSEQ, 4, 128).transpose(0, 1, 3, 2)),
            s_rgc=np.ascontiguousarray(state_rglru_conv[:, sl].reshape(DEPTH, NSEQ, 3, 4, 128).transpose(0, 1, 4, 3, 2)),
            s_gd=np.ascontiguousarray(state_gdn[:, sl]),
            s_gdc=np.ascontiguousarray(state_gdn_conv[:, sl].reshape(DEPTH, NSEQ, 3, 12, 128).transpose(0, 1, 4, 3, 2)),
        ))
    import time as _t, sys as _s
    _t0 = _t.time()
    res = run_bass_kernel_spmd(nc, in_maps, core_ids=list(range(NCORE)))
    print("[kernel] run %.1fs" % (_t.time() - _t0), file=_s.stderr)
    R = res.results
    yp = np.ascontiguousarray(R[0]["yp"].transpose(2, 1, 0)).reshape(1, ntok, D)
    ys = np.concatenate([np.ascontiguousarray(R[c]["ys"].transpose(2, 1, 0)).reshape(NSEQ, CH, D) for c in range(NCORE)], axis=0)
    def st(name, fn):
        p = fn(R[0][name][:, 0:1])
        s = np.concatenate([fn(R[c][name][:, 1:]) for c in range(NCORE)], axis=1)
        return np.ascontiguousarray(p), np.ascontiguousarray(s)
    ident = lambda a: a
    p_hg, s_hg = st("o_hg", ident)
    p_gd, s_gd = st("o_gd", ident)
    p_rg, s_rg = st("o_rg", lambda a: a.transpose(0, 1, 3, 2).reshape(DEPTH, a.shape[1], 512))
    p_rgc, s_rgc = st("o_rgc", lambda a: a.transpose(0, 1, 4, 3, 2).reshape(DEPTH, a.shape[1], 3, 512))
    p_gdc, s_gdc = st("o_gdc", lambda a: a.transpose(0, 1, 4, 3, 2).reshape(DEPTH, a.shape[1], 3, 1536))
    return (yp, ys, p_hg, p_rg, p_rgc, p_gd, p_gdc, s_hg, s_rg, s_rgc, s_gd, s_gdc)
```
